# Optimizing a Trainium2 kernel written in Bass

```python
import math
import jax
import jax.numpy as jnp
from jax import lax
import numpy as np

D_MODEL = 1024
BATCH = 8
SEQ = 4096
DEPTH = 1

CTX_LEN = 256
GRID_W = 64
N_HEADS = 8
HEAD_DIM = 64
D_ATTN = N_HEADS * 2 * HEAD_DIM
D_HYENA = D_MODEL
SHORT_CONV = 3
FILTER_EMB = 33
FILTER_ORDER = 64
DECAY_TARGET = 1e-2
FAST_DECAY_PCT = 0.3
SLOW_DECAY_PCT = 1.5
D_FF = 2816
ROPE_THETA = 10000.0
Q_BLOCK = 128
N_MOD = 9
N_PROJ = 3 * D_HYENA + 3 * D_ATTN + 2 * D_MODEL
SPLITS = (3 * D_HYENA, 3 * D_HYENA + D_ATTN, 3 * D_HYENA + 2 * D_ATTN, 3 * D_HYENA + 3 * D_ATTN)
EPS = 1e-6

kernel_name = 'hybrid_hyena_diffattn_macaron_dit_layer'


def rmsnorm(x, g):
    xf = x.astype(jnp.float32)
    y = xf * lax.rsqrt(jnp.mean(xf * xf, axis=-1, keepdims=True) + EPS)
    return (y * g.astype(jnp.float32)).astype(x.dtype)


def modulate(x, shift, scale):
    return x * (1.0 + scale) + shift


def swiglu(x, w_in, w_out):
    gate, up = jnp.split(x @ w_in, 2, axis=-1)
    return (jax.nn.silu(gate) * up) @ w_out


def ffn_half(s, shift, scale, gate, g_pre, g_post, w_in, w_out):
    h = swiglu(modulate(rmsnorm(s, g_pre), shift, scale), w_in, w_out)
    return s + 0.5 * gate * rmsnorm(h, g_post)


def short_conv(x, w, b):
    L = x.shape[1]
    pad = SHORT_CONV // 2
    xp = jnp.pad(x, ((0, 0), (pad, pad), (0, 0)))
    return sum(xp[:, j:j + L] * w[j] for j in range(SHORT_CONV)) + b


def hyena_kernel(L, w1, b1, w2, b2, w3, b3, w4, freq):
    f32 = jnp.float32
    bands = (FILTER_EMB - 1) // 2
    t = jnp.linspace(0.0, 1.0, L, dtype=f32)[:, None]
    w = 2.0 * math.pi * jnp.arange(L, dtype=f32)[:, None] / L
    f = jnp.linspace(1e-4, bands - 1, bands, dtype=f32)[None, :]
    z = jnp.concatenate([t, jnp.cos(f * w), -jnp.sin(f * w)], axis=-1)
    h = jnp.sin(freq * (z @ w1 + b1))
    h = jnp.sin(freq * (h @ w2 + b2))
    h = jnp.sin(freq * (h @ w3 + b3))
    h = (h @ w4).astype(f32)
    deltas = jnp.linspace(math.log(DECAY_TARGET) / SLOW_DECAY_PCT,
                          math.log(DECAY_TARGET) / FAST_DECAY_PCT, D_HYENA, dtype=f32)
    decay = jnp.exp(-t * jnp.abs(deltas))
    h_fwd, h_bwd = jnp.split(h, 2, axis=-1)
    h_fwd = h_fwd * decay
    h_bwd = h_bwd * decay
    k = jnp.concatenate([h_fwd, jnp.zeros_like(h_fwd[:1]), h_bwd[:0:-1]], axis=0)
    return k / (jnp.sum(jnp.abs(k), axis=0, keepdims=True) + EPS)


def bidir_long_conv(u, k):
    L = u.shape[1]
    uf = jnp.fft.rfft(u.astype(jnp.float32), n=2 * L, axis=1)
    kf = jnp.fft.rfft(k, n=2 * L, axis=0)
    y = jnp.fft.irfft(uf * kf[None], n=2 * L, axis=1)[:, :L]
    return y.astype(u.dtype)


def hyena_branch(z_hy, conv_w, conv_b, filt, hy_bias):
    z = short_conv(z_hy, conv_w, conv_b)
    x0, x1, v = jnp.split(z, 3, axis=-1)
    k = hyena_kernel(z.shape[1], *filt)
    u = v * x1
    y = bidir_long_conv(u, k) + u * hy_bias
    return y * x0


def axial_rope_tables(L):
    rows = L // GRID_W
    r, col = jnp.meshgrid(jnp.arange(rows), jnp.arange(GRID_W), indexing='ij')
    r = r.reshape(-1).astype(jnp.float32)
    col = col.reshape(-1).astype(jnp.float32)
    half = HEAD_DIM // 2
    inv = ROPE_THETA ** (-jnp.arange(0, half, 2, dtype=jnp.float32) / half)
    ang = jnp.stack([r[:, None] * inv, col[:, None] * inv], axis=1)
    return jnp.cos(ang), jnp.sin(ang)


def apply_axial_rope(x, cos, sin):
    B, L = x.shape[:2]
    xr = x.reshape(B, L, N_HEADS, 2, 2, 2, HEAD_DIM // 4)
    x0, x1 = xr[..., 0, :], xr[..., 1, :]
    c = cos[None, :, None, None]
    s = sin[None, :, None, None]
    out = jnp.stack([x0 * c - x1 * s, x0 * s + x1 * c], axis=-2)
    return out.reshape(x.shape).astype(x.dtype)


def diff_attention(q, k, v, lam):
    B, L = q.shape[:2]
    nb = L // Q_BLOCK
    qb = jnp.moveaxis(q.reshape(B, nb, Q_BLOCK, N_HEADS, 2, HEAD_DIM), 1, 0)

    def block(qblk):
        s = jnp.einsum('bqhmd,bkhmd->bhmqk', qblk, k).astype(jnp.float32) * (HEAD_DIM ** -0.5)
        p = jax.nn.softmax(s, axis=-1)
        p = p[:, :, 0] - lam * p[:, :, 1]
        return jnp.einsum('bhqk,bkhe->bqhe', p.astype(v.dtype), v)

    o = lax.map(block, qb)
    return jnp.moveaxis(o, 0, 1).reshape(B, L, N_HEADS, 2 * HEAD_DIM)


def context_kv(h, w_in):
    B, L = h.shape[:2]
    k, v = jnp.split(h @ w_in[:, SPLITS[1]:SPLITS[3]], 2, axis=-1)
    return k.reshape(B, L, N_HEADS, 2, HEAD_DIM), v.reshape(B, L, N_HEADS, 2 * HEAD_DIM)


def token_mixer(h, w_in, conv_w, conv_b, filt, hy_bias, lam, lam_init, subln_g,
                w_hy_out, w_da_out, w_o, rope, ctx_kv):
    B, L = h.shape[:2]
    z_hy, q, k, v, gates = jnp.split(h @ w_in, SPLITS, axis=-1)
    y_hy = hyena_branch(z_hy, conv_w, conv_b, filt, hy_bias) @ w_hy_out
    q = q.reshape(B, L, N_HEADS, 2, HEAD_DIM)
    k = k.reshape(B, L, N_HEADS, 2, HEAD_DIM)
    v = v.reshape(B, L, N_HEADS, 2 * HEAD_DIM)
    if rope is not None:
        q = apply_axial_rope(q, *rope)
        k = apply_axial_rope(k, *rope)
    if ctx_kv is not None:
        k = jnp.concatenate([k, ctx_kv[0]], axis=1)
        v = jnp.concatenate([v, ctx_kv[1]], axis=1)
    o = diff_attention(q, k, v, lam)
    o = rmsnorm(o, subln_g) * (1.0 - lam_init)
    y_da = o.reshape(B, L, D_ATTN) @ w_da_out
    g_hy, g_da = jnp.split(jax.nn.sigmoid(gates), 2, axis=-1)
    return (g_hy * y_hy + g_da * y_da) @ w_o


def setup_inputs(seed: int = 0) -> dict:
    key = jax.random.key(seed)
    ks = jax.random.split(key, 32)
    f32 = jnp.float32

    def nrm(k, shape, scale):
        return jax.random.normal(k, shape, f32) * scale

    return {
        'x': nrm(ks[0], (BATCH, SEQ, D_MODEL), 1.0),
        'c': nrm(ks[1], (BATCH, D_MODEL), 1.0),
        'ctx': nrm(ks[2], (BATCH, CTX_LEN, D_MODEL), 1.0),
        'c_ctx': nrm(ks[3], (D_MODEL,), 1.0),
        'w_ada': nrm(ks[4], (DEPTH, D_MODEL, N_MOD * D_MODEL), 0.5 * D_MODEL ** -0.5),
        'b_ada': nrm(ks[5], (DEPTH, N_MOD * D_MODEL), 0.01),
        'norm_g': 1.0 + nrm(ks[6], (DEPTH, 6, D_MODEL), 0.02),
        'w_ff_in': nrm(ks[7], (DEPTH, 2, D_MODEL, 2 * D_FF), D_MODEL ** -0.5),
        'w_ff_out': nrm(ks[8], (DEPTH, 2, D_FF, D_MODEL), D_FF ** -0.5),
        'w_in': nrm(ks[9], (DEPTH, D_MODEL, N_PROJ), D_MODEL ** -0.5),
        'hy_conv_w': nrm(ks[10], (DEPTH, SHORT_CONV, 3 * D_HYENA), SHORT_CONV ** -0.5),
        'hy_conv_b': nrm(ks[11], (DEPTH, 3 * D_HYENA), 0.01),
        'filt_w1': nrm(ks[12], (DEPTH, FILTER_EMB, FILTER_ORDER), FILTER_EMB ** -0.5),
        'filt_b1': nrm(ks[13], (DEPTH, FILTER_ORDER), 0.1),
        'filt_w2': nrm(ks[14], (DEPTH, FILTER_ORDER, FILTER_ORDER), FILTER_ORDER ** -0.5),
        'filt_b2': nrm(ks[15], (DEPTH, FILTER_ORDER), 0.1),
        'filt_w3': nrm(ks[16], (DEPTH, FILTER_ORDER, FILTER_ORDER), FILTER_ORDER ** -0.5),
        'filt_b3': nrm(ks[17], (DEPTH, FILTER_ORDER), 0.1),
        'filt_w4': nrm(ks[18], (DEPTH, FILTER_ORDER, 2 * D_HYENA), FILTER_ORDER ** -0.5),
        'filt_freq': 1.0 + nrm(ks[19], (DEPTH, FILTER_ORDER), 0.01),
        'hy_bias': nrm(ks[20], (DEPTH, D_HYENA), 0.1),
        'lambda_q1': nrm(ks[21], (DEPTH, HEAD_DIM), 0.1),
        'lambda_k1': nrm(ks[22], (DEPTH, HEAD_DIM), 0.1),
        'lambda_q2': nrm(ks[23], (DEPTH, HEAD_DIM), 0.1),
        'lambda_k2': nrm(ks[24], (DEPTH, HEAD_DIM), 0.1),
        'subln_g': 1.0 + nrm(ks[25], (DEPTH, 2 * HEAD_DIM), 0.02),
        'w_hy_out': nrm(ks[26], (DEPTH, D_HYENA, D_MODEL), D_HYENA ** -0.5),
        'w_da_out': nrm(ks[27], (DEPTH, D_ATTN, D_MODEL), D_ATTN ** -0.5),
        'w_o': nrm(ks[28], (DEPTH, D_MODEL, D_MODEL), D_MODEL ** -0.5),
    }


def reference(x, c, ctx, c_ctx, w_ada, b_ada, norm_g, w_ff_in, w_ff_out, w_in,
              hy_conv_w, hy_conv_b, filt_w1, filt_b1, filt_w2, filt_b2, filt_w3, filt_b3,
              filt_w4, filt_freq, hy_bias, lambda_q1, lambda_k1, lambda_q2, lambda_k2,
              subln_g, w_hy_out, w_da_out, w_o):
    rope = axial_rope_tables(x.shape[1])
    xs, cs = x, ctx
    for l in range(DEPTH):
        lam_init = 0.8 - 0.6 * math.exp(-0.3 * l)
        lam = (jnp.exp(jnp.sum(lambda_q1[l] * lambda_k1[l]).astype(jnp.float32))
               - jnp.exp(jnp.sum(lambda_q2[l] * lambda_k2[l]).astype(jnp.float32)) + lam_init)
        mod_x = jnp.split((jax.nn.silu(c) @ w_ada[l] + b_ada[l])[:, None, :], N_MOD, axis=-1)
        mod_c = jnp.split((jax.nn.silu(c_ctx) @ w_ada[l] + b_ada[l])[None, None, :], N_MOD, axis=-1)
        g = norm_g[l]
        filt = (filt_w1[l], filt_b1[l], filt_w2[l], filt_b2[l], filt_w3[l], filt_b3[l],
                filt_w4[l], filt_freq[l])
        mixer_params = (w_in[l], hy_conv_w[l], hy_conv_b[l], filt, hy_bias[l], lam, lam_init,
                        subln_g[l], w_hy_out[l], w_da_out[l], w_o[l])
        xs = ffn_half(xs, mod_x[0], mod_x[1], mod_x[2], g[0], g[1], w_ff_in[l, 0], w_ff_out[l, 0])
        cs = ffn_half(cs, mod_c[0], mod_c[1], mod_c[2], g[0], g[1], w_ff_in[l, 0], w_ff_out[l, 0])
        hx = modulate(rmsnorm(xs, g[2]), mod_x[3], mod_x[4])
        hc = modulate(rmsnorm(cs, g[2]), mod_c[3], mod_c[4])
        y = token_mixer(hx, *mixer_params, rope, context_kv(hc, w_in[l]))
        xs = xs + mod_x[5] * rmsnorm(y, g[3])
        if l < DEPTH - 1:
            yc = token_mixer(hc, *mixer_params, None, None)
            cs = cs + mod_c[5] * rmsnorm(yc, g[3])
            cs = ffn_half(cs, mod_c[6], mod_c[7], mod_c[8], g[4], g[5], w_ff_in[l, 1], w_ff_out[l, 1])
        xs = ffn_half(xs, mod_x[6], mod_x[7], mod_x[8], g[4], g[5], w_ff_in[l, 1], w_ff_out[l, 1])
    return xs
```

```python
import contextlib
import math
import os
STAGE = int(os.environ.get('KSTAGE', '99'))
FSUB = int(os.environ.get('FSUB', '99'))
FV = os.environ.get('FV', '')
ATH = int(os.environ.get('ATH', '8'))
HYENA = int(os.environ.get('HYENA', '1'))
ASUB = int(os.environ.get('ASUB', '99'))
import numpy as np
import concourse.bass as bass
import concourse.mybir as mybir
from concourse.bass_utils import run_bass_kernel_spmd

F32 = mybir.dt.float32
BF16 = mybir.dt.bfloat16
AF = mybir.ActivationFunctionType
ALU = mybir.AluOpType
AX = mybir.AxisListType

D = 1024
L = 4096
LC = 256
LK = L + LC
DFF = 2816
NJ = DFF // 128
NH = 8
EPS = 1e-6
NPROJ = 8192
N_DMA_LANES = 40
N_BG_LANES = 24
LAM_INIT = 0.8 - 0.6 * math.exp(-0.3 * 0)


class Sched:
    def __init__(self, nc):
        self.nc = nc
        self.eng = {"pe": nc.tensor, "act": nc.scalar, "dve": nc.vector,
                    "pool": nc.gpsimd, "sp": nc.sync}
        self.sem = {}
        self.cnt = {}
        for e in self.eng:
            self.sem[e] = nc.alloc_semaphore("s_" + e)
            self.cnt[e] = 0
        for i in range(N_DMA_LANES):
            self.sem["d%d" % i] = nc.alloc_semaphore("d%d" % i)
            self.cnt["d%d" % i] = 0
        for i in range(N_BG_LANES):
            self.sem["b%d" % i] = nc.alloc_semaphore("b%d" % i)
            self.cnt["b%d" % i] = 0
        self.next_bg = 0
        self.next_lane = 0
        for k in self.sem:
            nc.gpsimd.sem_clear(self.sem[k])
        nc.all_engine_barrier()
        self.waited = {e: {} for e in self.eng}
        self.W = {}
        self.R = {}
        self.q = {e: [] for e in self.eng}
        self.n_ins = 0
        self.n_wait = 0
        self.psum = set()

    def reg_psum(self, t):
        self.psum.add(t.name)
        return t

    def _wait(self, eng, deps):
        need = {}
        for (k, v) in deps:
            if v > need.get(k, 0):
                need[k] = v
        w = self.waited[eng]
        for k, v in need.items():
            if k == eng and eng in ("pe", "sp"):
                continue
            if w.get(k, 0) >= v:
                continue
            self.q[eng].append(("w", self.sem[k], v))
            self.n_wait += 1
            w[k] = v

    @staticmethod
    def _key(k):
        if isinstance(k, tuple):
            return tuple(Sched._key(x) for x in k)
        if isinstance(k, (str, int)):
            return k
        return k.name

    def _deps(self, reads, writes):
        deps = set()
        for k in reads:
            if k in self.W:
                deps.add(self.W[k])
        for k in writes:
            if k in self.W:
                deps.add(self.W[k])
            for ev in self.R.get(k, {}).items():
                deps.add(ev)
        return deps

    def _commit(self, ev, reads, writes):
        for k in reads:
            r = self.R.setdefault(k, {})
            if ev[1] > r.get(ev[0], 0):
                r[ev[0]] = ev[1]
        for k in writes:
            self.W[k] = ev
            self.R[k] = {}

    def op(self, eng, meth, *args, reads=(), writes=(), **kw):
        reads = [self._key(k) for k in reads]
        writes = [self._key(k) for k in writes]
        writes = writes + [k for k in reads if k in self.psum]
        reads = [k for k in reads if k not in self.psum]
        self._wait(eng, self._deps(reads, writes))
        fn = (lambda e, meth=meth, args=args, kw=kw: getattr(e, meth)(*args, **kw))
        self.cnt[eng] += 1
        self.q[eng].append(("i", fn, self.sem[eng], 1))
        self.n_ins += 1
        self._commit((eng, self.cnt[eng]), reads, writes)

    def dma(self, eng, out, in_, reads=(), writes=(), bg=False, **kw):
        reads = [self._key(k) for k in reads]
        writes = [self._key(k) for k in writes]
        if bg:
            lane = "b%d" % self.next_bg
            self.next_bg = (self.next_bg + 1) % N_BG_LANES
        else:
            lane = "d%d" % self.next_lane
            self.next_lane = (self.next_lane + 1) % N_DMA_LANES
        deps = self._deps(reads, writes)
        if self.cnt[lane] > 0:
            deps.add((lane, self.cnt[lane]))
        self._wait(eng, deps)
        self.cnt[lane] += 16
        self.q[eng].append(("i", (lambda e, out=out, in_=in_, kw=kw: e.dma_start(out=out, in_=in_, **kw)),
                            self.sem[lane], 16))
        self.n_ins += 1
        self._commit((lane, self.cnt[lane]), reads, writes)

    def emit(self):
        nc = self.nc
        q = self.q
        self.q = {e: [] for e in self.eng}

        def run(e, items):
            for it in items:
                if it[0] == "w":
                    e.wait_ge(it[1], it[2])
                else:
                    it[1](e).then_inc(it[2], it[3])

        with nc.Block() as block:
            @block.sync
            def _(e):
                run(e, q["sp"])

            @block.scalar
            def _(e):
                run(e, q["act"])

            @block.vector
            def _(e):
                run(e, q["dve"])

            @block.gpsimd
            def _(e):
                run(e, q["pool"])

            @block.tensor
            def _(e):
                run(e, q["pe"])

    def barrier(self, skip_bg=False):
        allev = set((k, v) for k, v in self.cnt.items() if v > 0 and not (skip_bg and k[0] == "b"))
        for e in self.eng:
            self._wait(e, allev)

    def phase_end(self, skip_bg=False):
        self.barrier(skip_bg)
        self.emit()


class Ctx:
    pass


def _pn_stats(S, g, xin, xin_key, bs):
    ss, rs, xs = bs["ss"], bs["rs"], bs["xs"]
    S.op("pool", "memset", ss[:], 0.0, writes=[ss])
    S.op("act", "activation", out=g.junk[:], in_=xin, func=AF.Square, accum_out=ss[:],
         reads=[xin_key, ss], writes=[ss])
    S.op("act", "activation", out=rs[:], in_=ss[:], func=AF.Ln, scale=1.0 / D, bias=g.eps_t[:, 0:1],
         reads=[ss], writes=[rs])
    S.op("act", "activation", out=rs[:], in_=rs[:], func=AF.Exp, scale=-0.5, reads=[rs], writes=[rs])
    S.op("dve", "tensor_scalar", out=xs[:], in0=xin, scalar1=rs[:, 0:1], scalar2=None, op0=ALU.mult,
         reads=[xin_key, rs], writes=[xs])


def _pn_transpose(S, g, bs, psT, A, B, dst, dst_key, col0):
    xs = bs["xs"]
    for kc in range(8):
        S.op("pe", "transpose", out=psT[:, kc, :], in_=xs[:, kc * 128:(kc + 1) * 128], identity=g.identb[:],
             reads=[xs], writes=[psT])
    for kc in range(8):
        S.op("act", "activation", out=dst[:, kc, col0:col0 + 128], in_=psT[:, kc, :], func=AF.Identity,
             scale=A[:, kc:kc + 1], bias=B[:, kc:kc + 1], reads=[psT], writes=[dst_key])


def ffn_alloc_wi(S, nc, st, tag, w_in_d, from_bf16=False, bg=False):
    wi = st.enter_context(nc.sbuf_tensor(tag + "wi", [128, 8, 2 * DFF], BF16))
    q = "sp" if from_bf16 else "pool"
    for kc in range(8):
        S.dma(q, wi[:, kc, :], w_in_d[kc * 128:(kc + 1) * 128, :], reads=[("wsrc", tag, "i", kc)], writes=[wi], bg=bg)
    return wi


def ffn_alloc_wo(S, nc, st, tag, w_out_d, from_bf16=False, bg=False):
    wo = st.enter_context(nc.sbuf_tensor(tag + "wo", [128, NJ, D], BF16))
    q = "sp" if from_bf16 else "pool"
    wov = w_out_d.rearrange("(j p) n -> p j n", p=128)
    for j0 in range(0, NJ, 2):
        S.dma(q, wo[:, j0:j0 + 2, :], wov[:, j0:j0 + 2, :], reads=[("wsrc", tag, "o", j0)], writes=[wo], bg=bg)
    return wo


def ffn_phase(S, nc, g, tag, w_in_d, w_out_d, streams, preloaded=None, from_bf16=False):
    with contextlib.ExitStack() as st:
        def T(name, shape, dt):
            return st.enter_context(nc.sbuf_tensor(tag + name, shape, dt))

        def P(name, shape, dt):
            return S.reg_psum(st.enter_context(nc.psum_tensor(tag + name, shape, dt)))

        g.psT = [P("psT%d" % i, [128, 8, 128], BF16) for i in range(2)]
        wi = preloaded if preloaded is not None else ffn_alloc_wi(S, nc, st, tag, w_in_d, from_bf16=from_bf16)
        wo = ffn_alloc_wo(S, nc, st, tag, w_out_d, from_bf16=from_bf16)
        TB = 256
        xin = [T("xin%d" % i, [128, D], F32) for i in range(2)]
        xres = [T("xres%d" % i, [128, D], F32) for i in range(1)]
        xT = [T("xT%d" % i, [128, 8, TB], BF16) for i in range(2)]
        hT = T("hT", [128, NJ, TB], BF16)
        sg = [T("sg%d" % i, [128, TB], F32) for i in range(2)]
        ytmp = [T("yt%d" % i, [128, D], F32) for i in range(2)]
        h2 = [T("h2_%d" % i, [128, 8, 128], BF16) for i in range(2)]
        ss2 = [T("ss2_%d" % i, [128, 2], F32) for i in range(2)]
        rs2 = [T("rs2_%d" % i, [128, 1], F32) for i in range(2)]
        psgu = [P("psgu%d" % i, [128, 2, TB], F32) for i in range(2)]
        pso = [P("pso%d" % i, [128, D], F32) for i in range(2)]

        cnt = dict(gu=0)
        blocks = []
        for sdef in streams:
            nblk = (sdef["T"] + TB - 1) // TB
            for b_ in range(nblk):
                blocks.append((sdef, b_))
        bs_pre = [dict(ss=T("ssp%d" % i, [128, 1], F32), rs=T("rsp%d" % i, [128, 1], F32), xs=T("xsp%d" % i, [128, D], BF16))
                  for i in range(2)]
        bs_nxt = [dict(ss=T("ssn%d" % i, [128, 1], F32), rs=T("rsn%d" % i, [128, 1], F32), xs=T("xsn%d" % i, [128, D], BF16))
                  for i in range(2)]

        def geom(bi):
            sdef, b_ = blocks[bi]
            t0 = b_ * TB
            return sdef, t0, min(TB, sdef["T"] - t0) // 128

        def stats(bi):
            sdef, t0, ntile = geom(bi)
            for ti in range(ntile):
                r0 = t0 + ti * 128
                xi = xin[ti % 2]
                S.dma("sp", xi[:], sdef["src"][r0:r0 + 128, :], writes=[xi])
                _pn_stats(S, g, xi[:], xi.name, bs_pre[ti % 2])

        def transposes(bi):
            sdef, t0, ntile = geom(bi)
            xTb = xT[bi % 2]
            for ti in range(ntile):
                _pn_transpose(S, g, bs_pre[ti % 2], g.psT[ti % 2], sdef["A"], sdef["B"], xTb, xTb.name, ti * 128)

        def gateup(bi):
            sdef, t0, ntile = geom(bi)
            nt = ntile * 128
            xTb = xT[bi % 2]
            for j in range(NJ):
                pg = psgu[cnt["gu"] % 2]
                sgi = sg[cnt["gu"] % 2]
                cnt["gu"] += 1
                for kc in range(8):
                    S.op("pe", "matmul", pg[:, 0, :nt], lhsT=wi[:, kc, j * 128:(j + 1) * 128], rhs=xTb[:, kc, :nt],
                         start=(kc == 0), stop=(kc == 7), reads=[wi, xTb], writes=[pg])
                for kc in range(8):
                    S.op("pe", "matmul", pg[:, 1, :nt], lhsT=wi[:, kc, DFF + j * 128:DFF + (j + 1) * 128],
                         rhs=xTb[:, kc, :nt], start=(kc == 0), stop=(kc == 7), reads=[wi, xTb], writes=[pg])
                S.op("act", "activation", out=sgi[:, :nt], in_=pg[:, 0, :nt], func=AF.Silu, reads=[pg], writes=[sgi])
                S.op("dve", "tensor_tensor", out=hT[:, j, :nt], in0=sgi[:, :nt], in1=pg[:, 1, :nt], op=ALU.mult,
                     reads=[sgi, pg], writes=[hT])

        def outmm(bi):
            sdef, t0, ntile = geom(bi)
            for ti in range(ntile):
                po = pso[ti % 2]
                for half in range(2):
                    for j in range(NJ):
                        S.op("pe", "matmul", po[:, half * 512:(half + 1) * 512],
                             lhsT=hT[:, j, ti * 128:(ti + 1) * 128], rhs=wo[:, j, half * 512:(half + 1) * 512],
                             start=(j == 0), stop=(j == NJ - 1), reads=[hT, wo], writes=[po])

        def post(bi):
            sdef, t0, ntile = geom(bi)
            for ti in range(ntile):
                r0 = t0 + ti * 128
                xr = xres[0]
                S.dma("sp", xr[:], sdef["src"][r0:r0 + 128, :], writes=[xr])
                yt = ytmp[ti % 2]
                post_norm_residual(S, g, tag, pso[ti % 2], sdef["G"], xr, yt, ss2[ti % 2], rs2[ti % 2],
                                   sdef["dst"][r0:r0 + 128, :], (tag, "dst", sdef["T"], r0))
                if sdef.get("nxt") is not None:
                    _pn_stats(S, g, yt[:], yt.name, bs_nxt[ti % 2])

        def nxt_transposes(bi):
            sdef, t0, ntile = geom(bi)
            if sdef.get("nxt") is None:
                return
            nx = sdef["nxt"]
            for ti in range(ntile):
                r0 = t0 + ti * 128
                hh = h2[ti % 2]
                _pn_transpose(S, g, bs_nxt[ti % 2], g.psT[ti % 2], nx["A"], nx["B"], hh, hh.name, 0)
                S.dma("sp", nx["dst"][:, :, r0:r0 + 128], hh[:], reads=[hh], writes=[(tag, "hdst", sdef["T"], r0)])

        nb = len(blocks)
        stats(0)
        transposes(0)
        for bi in range(nb):
            if bi + 1 < nb:
                stats(bi + 1)
            gateup(bi)
            if bi + 1 < nb:
                transposes(bi + 1)
            if bi >= 1:
                nxt_transposes(bi - 1)
            outmm(bi)
            post(bi)
        nxt_transposes(nb - 1)
        S.phase_end()


def post_norm_residual(S, g, tag, po, G, xr, yt, s2, r2, dst_ap, dst_key):
    S.op("pool", "memset", s2[:], 0.0, writes=[s2])
    for half in range(2):
        S.op("act", "activation", out=g.junk[:, 0:512], in_=po[:, half * 512:(half + 1) * 512],
             func=AF.Square, accum_out=s2[:, half:half + 1], reads=[po, s2], writes=[s2])
    S.op("dve", "tensor_tensor", out=r2[:], in0=s2[:, 0:1], in1=s2[:, 1:2], op=ALU.add, reads=[s2], writes=[r2])
    S.op("act", "activation", out=r2[:], in_=r2[:], func=AF.Ln, scale=1.0 / D, bias=g.eps_t[:, 0:1],
         reads=[r2], writes=[r2])
    S.op("act", "activation", out=r2[:], in_=r2[:], func=AF.Exp, scale=-0.5, reads=[r2], writes=[r2])
    S.op("dve", "scalar_tensor_tensor", out=yt[:], in0=po[:], scalar=r2[:, 0:1], in1=G[:],
         op0=ALU.mult, op1=ALU.mult, reads=[po, r2, G], writes=[yt])
    S.op("pool", "tensor_tensor", out=yt[:], in0=yt[:], in1=xr[:], op=ALU.add, reads=[yt, xr], writes=[yt])
    S.dma("sp", dst_ap, yt[:], reads=[yt], writes=[dst_key])


def hyena_proj(S, nc, g, d, hT, st_outer):
    with contextlib.ExitStack() as st:
        def T(name, shape, dt):
            return st.enter_context(nc.sbuf_tensor("hp" + name, shape, dt))

        psA = S.reg_psum(st.enter_context(nc.psum_tensor("hppsA", [128, 512], F32)))
        psA2 = S.reg_psum(st.enter_context(nc.psum_tensor("hppsA2", [128, 512], F32)))
        pss = [psA, psA2]
        cw = T("cw", [128, 3, 3, 8], F32)
        cb = T("cb", [128, 3, 8], F32)
        for j in range(3):
            for p in range(3):
                S.dma("sp", cw[:, j, p, :], d.hy_conv_w_d[j:j + 1, p * D:(p + 1) * D].rearrange("o (ch q) -> q (o ch)", q=128),
                      writes=[cw], allow_slow_non_contiguous=True)
        for p in range(3):
            S.dma("sp", cb[:, p, :], d.hy_conv_b_d[0:1, p * D:(p + 1) * D].rearrange("o (ch q) -> q (o ch)", q=128),
                  writes=[cb], allow_slow_non_contiguous=True)
        w3 = [T("w3_%d" % i, [128, 8, 3, 128], BF16) for i in range(2)]
        zhs = [T("zh%d" % i, [128, L + 2], F32) for i in range(2)]
        zc = [T("zc%d" % i, [128, L], F32) for i in range(2)]
        ob = [T("ob%d" % i, [128, L], BF16) for i in range(2)]
        for zh in zhs:
            S.op("pool", "memset", zh[:, 0:1], 0.0, writes=[zh])
            S.op("pool", "memset", zh[:, L + 1:L + 2], 0.0, writes=[zh])
        zi = 0
        wv_ = d.w_in_d.rearrange("(kc p) n -> p kc n", p=128)

        def load_w(ch):
            for p in range(3):
                S.dma("pool", w3[ch % 2][:, :, p, :], wv_[:, :, p * D + ch * 128:p * D + (ch + 1) * 128], writes=[w3[ch % 2]])
        load_w(0)
        pi = 0
        for ch in range(8):
            if ch + 1 < 8:
                load_w(ch + 1)
            w = w3[ch % 2]
            for part in (0, 1, 2):
                zh = zhs[zi % 2]
                zi += 1
                for blk in range(L // 512):
                    ps = pss[pi % 2]
                    pi += 1
                    for kc in range(8):
                        S.op("pe", "matmul", ps[:], lhsT=w[:, kc, part, :], rhs=hT[:, kc, blk * 512:(blk + 1) * 512],
                             start=(kc == 0), stop=(kc == 7), reads=[w, hT], writes=[ps])
                    S.op("act", "copy", out=zh[:, 1 + blk * 512:1 + (blk + 1) * 512], in_=ps[:], reads=[ps], writes=[zh])
                z = zc[1] if part == 1 else zc[0]
                S.op("act", "activation", out=z[:], in_=zh[:, 1:L + 1], func=AF.Identity, scale=cw[:, 1, part, ch:ch + 1],
                     bias=cb[:, part, ch:ch + 1], reads=[zh, cw, cb], writes=[z])
                S.op("dve", "scalar_tensor_tensor", out=z[:], in0=zh[:, 0:L], scalar=cw[:, 0, part, ch:ch + 1], in1=z[:],
                     op0=ALU.mult, op1=ALU.add, reads=[zh, cw, z], writes=[z])
                S.op("dve", "scalar_tensor_tensor", out=z[:], in0=zh[:, 2:L + 2], scalar=cw[:, 2, part, ch:ch + 1], in1=z[:],
                     op0=ALU.mult, op1=ALU.add, reads=[zh, cw, z], writes=[z])
                if part == 0:
                    S.op("act", "copy", out=ob[0][:], in_=z[:], reads=[z], writes=[ob[0]])
                    S.dma("sp", d.x0_d[:, ch, :], ob[0][:], reads=[ob[0]], writes=[("x0_d", ch)])
                elif part == 2:
                    S.op("pool", "tensor_tensor", out=ob[1][:], in0=z[:], in1=zc[1][:], op=ALU.mult, reads=[z, zc[1]],
                         writes=[ob[1]])
                    S.dma("sp", d.u_d[:, ch, :], ob[1][:], reads=[ob[1]], writes=[("u_d", ch)])
        S.barrier()


def hyena_fft_phase(S, nc, g, d):
    TWO_PI = 2.0 * math.pi
    MAGIC = 12582912.0
    with contextlib.ExitStack() as st:
        def T(name, shape, dt):
            return st.enter_context(nc.sbuf_tensor("hf" + name, shape, dt))

        bank = [S.reg_psum(st.enter_context(nc.psum_tensor("hfbank%d" % i, [128, 512], F32))) for i in range(8)]
        bankb = [b[:].bitcast(BF16) for b in bank]
        tabA = T("tabA", [128, 130], BF16)
        tabB = T("tabB", [64, 65, 256], BF16)
        tabBp = T("tabBp", [128, 65, 128], BF16)
        tabAp = T("tabAp", [65, 128], BF16)
        S.dma("pool", tabA[:], d.tabA_d, writes=[tabA])
        for f0 in range(0, 65, 13):
            S.dma("pool", tabB[:, f0:f0 + 13, :], d.tabB_d[:, f0:f0 + 13, :], writes=[tabB])
            S.dma("pool", tabBp[:, f0:f0 + 13, :], d.tabBp_d[:, f0:f0 + 13, :], writes=[tabBp])
        S.dma("pool", tabAp[:], d.tabAp_d, writes=[tabAp])
        w4 = T("w4", [64, 2 * D], BF16)
        S.dma("pool", w4[:], d.filt_w4_d, writes=[w4])
        hA = T("hA", [64, 2 * L], BF16)
        with contextlib.ExitStack() as st2:
            def T2(name, shape, dt):
                return st2.enter_context(nc.sbuf_tensor("hf" + name, shape, dt))
            zT = T2("zT", [33, 2 * L], BF16)
            S.dma("pool", zT[:], d.zemb_d, writes=[zT])
            w1 = T2("w1", [33, 64], BF16)
            w2 = T2("w2", [64, 64], BF16)
            w3_ = T2("w3", [64, 64], BF16)
            S.dma("pool", w1[:], d.filt_w1_d, writes=[w1])
            S.dma("pool", w2[:], d.filt_w2_d, writes=[w2])
            S.dma("pool", w3_[:], d.filt_w3_d, writes=[w3_])
            fb = T2("fb", [64, 4], F32)
            for i, ap in enumerate([d.filt_b1_d, d.filt_b2_d, d.filt_b3_d, d.filt_freq_d]):
                S.dma("sp", fb[:, i:i + 1], ap.rearrange("o p -> p o"), writes=[fb], allow_slow_non_contiguous=True)
            fbb = T2("fbb", [64, 3], F32)
            S.op("dve", "tensor_scalar", out=fbb[:], in0=fb[:, 0:3], scalar1=fb[:, 3:4], scalar2=None, op0=ALU.mult,
                 reads=[fb], writes=[fbb])
            hB = T2("hB", [64, 2 * L], BF16)
            ra = [T2("ra%d" % i, [64, 512], F32) for i in range(2)]
            rb = [T2("rb%d" % i, [64, 512], F32) for i in range(2)]
            li = 0
            for layer, (wl, src, dst) in enumerate(((w1, zT, hA), (w2, hA, hB), (w3_, hB, hA))):
                kdim = 33 if layer == 0 else 64
                for blk in range(2 * L // 512):
                    cs = slice(blk * 512, (blk + 1) * 512)
                    ps = bank[6 + li % 2]
                    a_, b_ = ra[li % 2], rb[li % 2]
                    li += 1
                    S.op("pe", "matmul", ps[0:64, :], lhsT=wl[0:kdim, :], rhs=src[0:kdim, cs], start=True, stop=True,
                         reads=[wl, src], writes=[ps])
                    S.op("act", "activation", out=a_[:], in_=ps[0:64, :], func=AF.Identity, scale=fb[:, 3:4],
                         bias=fbb[:, layer:layer + 1], reads=[ps, fb, fbb], writes=[a_])
                    S.op("dve", "tensor_scalar", out=b_[:], in0=a_[:], scalar1=1.0 / TWO_PI, scalar2=MAGIC, op0=ALU.mult,
                         op1=ALU.add, reads=[a_], writes=[b_])
                    S.op("dve", "tensor_scalar", out=b_[:], in0=b_[:], scalar1=MAGIC, scalar2=-TWO_PI, op0=ALU.subtract,
                         op1=ALU.mult, reads=[b_], writes=[b_])
                    S.op("dve", "tensor_tensor", out=a_[:], in0=a_[:], in1=b_[:], op=ALU.add, reads=[a_, b_], writes=[a_])
                    S.op("dve", "tensor_scalar", out=a_[:], in0=a_[:], scalar1=3.1415925, scalar2=-3.1415925, op0=ALU.min,
                         op1=ALU.max, reads=[a_], writes=[a_])
                    S.op("act", "activation", out=dst[:, cs], in_=a_[:], func=AF.Sin, reads=[a_], writes=[dst])
            S.barrier()
        h3 = hA
        dl = T("dl", [128, 8], F32)
        S.dma("sp", dl[:], d.deltas_d, writes=[dl])
        tl1 = T("tl1", [128, 2, 8], F32)
        tl2 = T("tl2", [128, 512], F32)
        for j in range(2):
            S.dma("sp", tl1[:, j, :], d.tlin1_d[j:j + 1, :].broadcast_to([128, 8]), writes=[tl1])
        S.dma("sp", tl2[:], d.tlin2_d[0:1, :].broadcast_to([128, 512]), writes=[tl2])
        ndl = T("ndl", [128, 8], F32)
        S.op("dve", "tensor_scalar", out=ndl[:], in0=dl[:], scalar1=-1.0, scalar2=None, op0=ALU.mult, reads=[dl], writes=[ndl])
        hbias = T("hbias", [128, 8], F32)
        S.dma("sp", hbias[:], d.hy_bias_d.rearrange("o (ch q) -> q (o ch)", q=128), writes=[hbias],
              allow_slow_non_contiguous=True)
        E1 = T("E1", [128, 2, 8], F32)
        E2 = T("E2", [128, 2, 512], F32)
        sq2 = T("sq2", [128, 2, L], BF16)
        x0 = T("x0", [128, L], BF16)
        hyo = sq2[:, 1, :]
        buf1 = T("buf1", [128, 65 * 128], BF16)
        buf2 = T("buf2", [128, 65 * 128], BF16)
        buf3 = T("buf3", [128, 128 * 130], BF16)
        Kf = T("Kf", [128, 65, 128], BF16)
        P1 = [T("P1_%d" % i, [128, 4, 128], F32) for i in range(2)]
        P2 = [T("P2_%d" % i, [128, 4, 128], F32) for i in range(1)]
        asum = T("asum", [128, 17], F32)
        nrm = T("nrm", [128, 1], F32)
        UB = buf1[0:64, 0:64 * 128].rearrange("p (a c) -> p a c", c=128)
        UBf = buf1[:, 0:64 * 128].rearrange("p (a c) -> p a c", c=128)
        YT = buf1[:, :].rearrange("p (f c) -> p f c", c=128)
        Y = buf2[:, :].rearrange("p (f k) -> p f k", k=128)
        Q = Y
        VA = buf3[0:64, :].rearrange("p (c k) -> p c k", k=130)
        QT = buf3[0:65, 0:128 * 128].rearrange("p (c a) -> p c a", a=128)
        ev = [0]
        rot = [0]

        def nb():
            rot[0] += 1
            return (rot[0] - 1) % 8

        def evac(out, in_, reads, writes):
            if ev[0] % 3 != 2:
                S.op("act", "copy", out=out, in_=in_, reads=reads, writes=writes)
            else:
                S.op("dve", "tensor_copy", out=out, in_=in_, reads=reads, writes=writes)
            ev[0] += 1

        def forward(full, consume):
            K = 128 if full else 64
            sv = sq2[:, :, :].rearrange("p h (b a) -> p a h b", a=64)
            ub = UBf if full else UB
            for a0 in range(0, 64, 8):
                bi = nb()
                for u in range(8):
                    in_ = sv[:, a0 + u, :, :] if full else sv[:, a0 + u, 0, :]
                    S.op("pe", "transpose", out=bankb[bi][0:K, u * 128:(u + 1) * 128], in_=in_,
                         identity=g.identb[:], reads=[sq2], writes=[bank[bi]])
                evac(ub[:, a0:a0 + 8, :], bankb[bi][0:K, 0:1024].rearrange("p (a c) -> p a c", c=128), [bank[bi]], [buf1])
            for c0 in range(0, 128, 3):
                n = min(3, 128 - c0)
                bi = nb()
                for u in range(n):
                    S.op("pe", "matmul", bank[bi][0:64, u * 130:(u + 1) * 130], lhsT=ub[:, :, c0 + u], rhs=tabA[0:K, :],
                         start=True, stop=True, reads=[buf1, tabA], writes=[bank[bi]])
                evac(VA[:, c0:c0 + n, :], bank[bi][0:64, 0:n * 130].rearrange("p (c k) -> p c k", k=130), [bank[bi]], [buf3])
            for f0 in range(0, 65, 4):
                n = min(4, 65 - f0)
                bi = nb()
                for u in range(n):
                    f = f0 + u
                    S.op("pe", "matmul", bank[bi][:, u * 128:(u + 1) * 128], lhsT=VA[:, :, f], rhs=tabB[:, f, 0:128],
                         start=True, stop=False, reads=[buf3, tabB], writes=[bank[bi]])
                    S.op("pe", "matmul", bank[bi][:, u * 128:(u + 1) * 128], lhsT=VA[:, :, 65 + f], rhs=tabB[:, f, 128:256],
                         start=False, stop=True, reads=[buf3, tabB], writes=[bank[bi]])
                consume(bank[bi], f0, n)

        for ch in range(8):
            for j in range(2):
                S.op("act", "activation", out=E1[:, j, :], in_=tl1[:, j, :], func=AF.Exp, scale=dl[:, ch:ch + 1],
                     reads=[tl1, dl], writes=[E1])
                S.op("act", "activation", out=E2[:, j, :], in_=tl2[:], func=AF.Exp,
                     scale=(dl if j == 0 else ndl)[:, ch:ch + 1], reads=[tl2, dl, ndl], writes=[E2])
            S.op("pool", "memset", asum[:], 0.0, writes=[asum, sq2] + [(sq2.name, fi_, b_) for fi_ in range(2) for b_ in range(8)])
            for fi in range(2):
                for blk in range(L // 512):
                    ps = bank[nb()]
                    S.op("pe", "matmul", ps[:], lhsT=w4[:, fi * D + ch * 128:fi * D + (ch + 1) * 128], rhs=h3[:, fi * L + blk * 512:fi * L + (blk + 1) * 512],
                         start=True, stop=True, reads=[w4, h3], writes=[ps])
                    sqv = sq2[:, fi, blk * 512:(blk + 1) * 512]
                    S.op("dve", "scalar_tensor_tensor", out=sqv, in0=ps[:], scalar=E1[:, fi, blk:blk + 1], in1=E2[:, fi, :],
                         op0=ALU.mult, op1=ALU.mult, reads=[ps, E1, E2], writes=[(sq2.name, fi, blk)])
                    if fi == 1 and blk == 0:
                        S.op("dve", "memset", sq2[:, 1, 0:1], 0.0, writes=[(sq2.name, fi, blk)])
                    S.op("act", "activation", out=g.junk[:, 0:512], in_=sqv, func=AF.Abs,
                         accum_out=asum[:, fi * 8 + blk:fi * 8 + blk + 1], reads=[(sq2.name, fi, blk), asum], writes=[asum])
            S.op("pool", "memset", asum[:, 16:17], 0.0, reads=[(sq2.name, fi_, b_) for fi_ in range(2) for b_ in range(8)], writes=[sq2])
            S.op("dve", "reduce_sum", out=nrm[:], in_=asum[:, 0:16], axis=AX.X, reads=[asum], writes=[nrm])
            S.op("dve", "tensor_scalar", out=nrm[:], in0=nrm[:], scalar1=EPS, scalar2=None, op0=ALU.add, reads=[nrm], writes=[nrm])
            S.op("dve", "reciprocal", out=nrm[:], in_=nrm[:], reads=[nrm], writes=[nrm])

            def cons_f(bk, f0, n):
                bv = bk[:, 0:n * 128].rearrange("p (f k) -> p f k", k=128)
                S.op("act", "activation", out=Kf[:, f0:f0 + n, 0:64], in_=bv[:, :, 0:64], func=AF.Identity, scale=nrm[:, 0:1],
                     bias=hbias[:, ch:ch + 1], reads=[bk, nrm, hbias], writes=[Kf])
                S.op("act", "activation", out=Kf[:, f0:f0 + n, 64:128], in_=bv[:, :, 64:128], func=AF.Identity, scale=nrm[:, 0:1],
                     reads=[bk, nrm], writes=[Kf])

            def cons_b(bk, f0, n):
                bv = bk[:, 0:n * 128].rearrange("p (f k) -> p f k", k=128)
                S.op("dve", "scalar_tensor_tensor", out=Kf[:, f0:f0 + n, 0:64], in0=bv[:, :, 0:64], scalar=nrm[:, 0:1],
                     in1=Kf[:, f0:f0 + n, 0:64], op0=ALU.mult, op1=ALU.add, reads=[bk, nrm, Kf], writes=[Kf])
                S.op("dve", "scalar_tensor_tensor", out=Kf[:, f0:f0 + n, 64:128], in0=bv[:, :, 64:128], scalar=nrm[:, 0:1],
                     in1=Kf[:, f0:f0 + n, 64:128], op0=ALU.mult, op1=ALU.subtract, reads=[bk, nrm, Kf], writes=[Kf])
                S.op("pool", "tensor_scalar", out=Kf[:, f0:f0 + n, 64:128], in0=Kf[:, f0:f0 + n, 64:128], scalar1=-1.0,
                     scalar2=None, op0=ALU.mult, reads=[Kf], writes=[Kf])

            forward(True, cons_f)
            S.dma("sp", sq2[:, 0, :], d.u_d[:, ch, :], writes=[sq2])
            S.dma("sp", x0[:], d.x0_d[:, ch, :], writes=[x0])
            pc = [0]

            def cons_u(bk, f0, n):
                p1, p2 = P1[pc[0] % 2], P2[0]
                pc[0] += 1
                bv = bk[:, 0:n * 128].rearrange("p (f r k) -> p f r k", r=2, k=64)
                kr = Kf[:, f0:f0 + n, 0:64].unsqueeze(2).broadcast_to([128, n, 2, 64])
                ki = Kf[:, f0:f0 + n, 64:128].unsqueeze(2).broadcast_to([128, n, 2, 64])
                p1v = p1[:, 0:n, :].rearrange("p f (r k) -> p f r k", r=2)
                p2v = p2[:, 0:n, :].rearrange("p f (r k) -> p f r k", r=2)
                S.op("dve", "tensor_tensor", out=p1v, in0=bv, in1=kr, op=ALU.mult, reads=[bk, Kf], writes=[p1])
                S.op("dve", "tensor_tensor", out=p2v, in0=bv, in1=ki, op=ALU.mult, reads=[bk, Kf], writes=[p2])
                S.op("pool", "tensor_tensor", out=Y[:, f0:f0 + n, 0:64], in0=p1[:, 0:n, 0:64], in1=p2[:, 0:n, 64:128],
                     op=ALU.subtract, reads=[p1, p2], writes=[buf2])
                S.op("pool", "tensor_tensor", out=Y[:, f0:f0 + n, 64:128], in0=p2[:, 0:n, 0:64], in1=p1[:, 0:n, 64:128],
                     op=ALU.add, reads=[p1, p2], writes=[buf2])

            forward(False, cons_u)
            for f0 in range(0, 65, 8):
                n = min(8, 65 - f0)
                bi = nb()
                for u in range(n):
                    S.op("pe", "transpose", out=bankb[bi][:, u * 128:(u + 1) * 128], in_=Y[:, f0 + u, :], identity=g.identb[:],
                         reads=[buf2], writes=[bank[bi]])
                evac(YT[:, f0:f0 + n, :], bankb[bi][:, 0:n * 128].rearrange("p (f c) -> p f c", c=128), [bank[bi]], [buf1])
            for f0 in range(0, 65, 4):
                n = min(4, 65 - f0)
                bi = nb()
                for u in range(n):
                    S.op("pe", "matmul", bank[bi][:, u * 128:(u + 1) * 128], lhsT=tabBp[:, f0 + u, :], rhs=YT[:, f0 + u, :],
                         start=True, stop=True, reads=[tabBp, buf1], writes=[bank[bi]])
                evac(Q[:, f0:f0 + n, :], bank[bi][:, 0:n * 128].rearrange("p (f c) -> p f c", c=128), [bank[bi]], [buf2])
            for c0 in range(0, 128, 8):
                bi = nb()
                for u in range(8):
                    S.op("pe", "transpose", out=bankb[bi][0:65, u * 128:(u + 1) * 128], in_=Q[:, :, c0 + u], identity=g.identb[:],
                         reads=[buf2], writes=[bank[bi]])
                evac(QT[:, c0:c0 + 8, :], bankb[bi][0:65, 0:1024].rearrange("p (c a) -> p c a", a=128), [bank[bi]], [buf3])
            x0v = x0[:, :].rearrange("p (b a) -> p a b", a=64)
            hyv = hyo.rearrange("p (b a) -> p a b", a=64)
            for a0 in range(0, 64, 8):
                bi = nb()
                for u in range(8):
                    a = a0 + u
                    S.op("pe", "matmul", bank[bi][:, u * 64:(u + 1) * 64], lhsT=QT[:, :, a], rhs=tabAp[:, 0:64],
                         start=True, stop=False, reads=[buf3, tabAp], writes=[bank[bi]])
                    S.op("pe", "matmul", bank[bi][:, u * 64:(u + 1) * 64], lhsT=QT[:, :, 64 + a], rhs=tabAp[:, 64:128],
                         start=False, stop=True, reads=[buf3, tabAp], writes=[bank[bi]])
                S.op("dve", "tensor_tensor", out=hyv[:, a0:a0 + 8, :], in0=bank[bi][:].rearrange("p (a b) -> p a b", b=64),
                     in1=x0v[:, a0:a0 + 8, :], op=ALU.mult, reads=[bank[bi], x0], writes=[sq2])
            S.dma("sp", d.hyT_d[:, ch, :], hyo, reads=[sq2], writes=[("hyT_d", ch)])
        S.phase_end()


def attention_phase(S, nc, g, d):
    with contextlib.ExitStack() as st:
        def T(name, shape, dt):
            return st.enter_context(nc.sbuf_tensor("at" + name, shape, dt))

        def P(name):
            return S.reg_psum(st.enter_context(nc.psum_tensor("at" + name, [128, 512], F32)))

        hT = T("hT", [128, 8, L], BF16)
        hcT = T("hcT", [128, 8, LC], BF16)
        for kc in range(8):
            S.dma("sp", hT[:, kc, :], d.hT_d[:, kc, :], writes=[hT])
        S.dma("sp", hcT[:], d.hcT_d, writes=[hcT])
        if HYENA:
            hyena_proj(S, nc, g, d, hT, st)
        ropeC = T("ropeC", [128, L], BF16)
        ropeS = T("ropeS", [128, L], BF16)
        S.dma("pool", ropeC[:], d.ropec_d, writes=[ropeC])
        S.dma("pool", ropeS[:], d.ropes_d, writes=[ropeS])
        Rm = T("Rm", [128, 128], BF16)
        S.dma("pool", Rm[:], d.rotm_d, writes=[Rm])
        lq = T("lq", [128, 4, 64], F32)
        for i, ap in enumerate([d.lq1_d, d.lk1_d, d.lq2_d, d.lk2_d]):
            S.dma("sp", lq[:, i, :], ap.broadcast_to([128, 64]), writes=[lq])
        lprod = T("lprod", [128, 2, 64], F32)
        S.op("dve", "tensor_tensor", out=lprod[:, 0, :], in0=lq[:, 0, :], in1=lq[:, 1, :], op=ALU.mult,
             reads=[lq], writes=[lprod])
        S.op("dve", "tensor_tensor", out=lprod[:, 1, :], in0=lq[:, 2, :], in1=lq[:, 3, :], op=ALU.mult,
             reads=[lq], writes=[lprod])
        lsum = T("lsum", [128, 2], F32)
        S.op("dve", "reduce_sum", out=lsum[:], in_=lprod[:], axis=AX.X, reads=[lprod], writes=[lsum])
        S.op("act", "activation", out=lsum[:], in_=lsum[:], func=AF.Exp, reads=[lsum], writes=[lsum])
        nlam = T("nlam", [128, 1], F32)
        S.op("dve", "scalar_tensor_tensor", out=nlam[:], in0=lsum[:, 1:2], scalar=-LAM_INIT, in1=lsum[:, 0:1],
             op0=ALU.add, op1=ALU.subtract, reads=[lsum], writes=[nlam])

        wq = [T("wq%d" % i, [128, 8, 128], BF16) for i in range(2)]
        wk = [T("wk%d" % i, [128, 8, 128], BF16) for i in range(2)]
        wv = [T("wv%d" % i, [128, 8, 256], BF16) for i in range(2)]
        qT = T("qT", [128, L], BF16)
        kT = T("kT", [128, LK], BF16)
        NKT = LK // 128
        vaug = T("vaug", [128, NKT, 2, 132], BF16)
        S.op("pool", "memset", vaug[:], 1.0, writes=[vaug])
        attnT = [T("attnT%d" % i, [128, L], BF16) for i in range(2)]
        qsb = [T("qsb%d" % i, [128, 512], BF16) for i in range(2)]
        t1 = [T("t1_%d" % i, [128, 512], F32) for i in range(2)]
        t2 = [T("t2_%d" % i, [128, 512], F32) for i in range(2)]
        pT2 = [[T("pT%d_%d" % (m, i), [128, 512], BF16) for i in range(3)] for m in range(2)]
        ppair = [T("ppair%d" % m, [128, 512], BF16) for m in range(2)]
        pprev = [None, None]
        acc = [[T("acc%d_%d" % (m, i), [128, 512], F32) for i in range(1)] for m in range(2)]
        accb = T("accb", [128, 512], BF16)
        rr = T("rr", [128, 512], F32)
        om = [T("om%d" % m, [128, 512], F32) for m in range(2)]
        of_ = T("of", [128, 512], F32)
        psS2 = [[P("psS%d_%d" % (m, i)) for i in range(2)] for m in range(2)]
        OT = [P("OT%d" % m) for m in range(2)]
        psA = P("psA")
        psB = P("psB")
        w_in = d.w_in_d

        def load_w(h):
            i = h % 2
            for (w, c0) in ((wq[i], 3072), (wk[i], 4096)):
                S.dma("pool", w[:], w_in[:, c0 + h * 128:c0 + (h + 1) * 128].rearrange("(kc p) n -> p kc n", p=128),
                      writes=[w])
            if h % 2 == 0:
                w = wv[(h // 2) % 2]
                S.dma("pool", w[:], w_in[:, 5120 + h * 128:5120 + (h + 2) * 128].rearrange("(kc p) n -> p kc n", p=128),
                      writes=[w])

        load_w(0)
        ei = 0
        for h in range(ATH if STAGE >= 3 else 0):
            if h + 1 < ATH:
                load_w(h + 1)
            i2 = h % 2
            for (w, dstT) in ((wq[i2], qT), (wk[i2], kT)):
                for blk in range(L // 512 if ASUB >= 2 else 0):
                    cs = slice(blk * 512, (blk + 1) * 512)
                    for kc in range(8):
                        S.op("pe", "matmul", psA[:], lhsT=w[:, kc, :], rhs=hT[:, kc, cs], start=(kc == 0), stop=(kc == 7),
                             reads=[w, hT], writes=[psA])
                    qs_, ta, tb = qsb[ei % 2], t1[ei % 2], t2[ei % 2]
                    ei += 1
                    S.op("act", "copy", out=qs_[:], in_=psA[:], reads=[psA], writes=[qs_])
                    S.op("pe", "matmul", psB[:], lhsT=Rm[:], rhs=qs_[:], start=True, stop=True, reads=[Rm, qs_],
                         writes=[psB])
                    S.op("dve", "tensor_tensor", out=ta[:], in0=psA[:], in1=ropeC[:, cs], op=ALU.mult,
                         reads=[psA, ropeC], writes=[ta])
                    S.op("dve", "tensor_tensor", out=tb[:], in0=psB[:], in1=ropeS[:, cs], op=ALU.mult,
                         reads=[psB, ropeS], writes=[tb])
                    S.op("pool", "tensor_tensor", out=dstT[:, cs], in0=ta[:], in1=tb[:], op=ALU.add,
                         reads=[ta, tb], writes=[dstT])
            if ASUB < 3:
                continue
            for kc in range(8):
                S.op("pe", "matmul", psA[:, 0:LC], lhsT=wk[i2][:, kc, :], rhs=hcT[:, kc, :], start=(kc == 0), stop=(kc == 7),
                     reads=[wk[i2], hcT], writes=[psA])
            S.op("act", "copy", out=kT[:, L:LK], in_=psA[:, 0:LC], reads=[psA], writes=[kT])
            if h % 2 == 0:
                wvp = wv[(h // 2) % 2]
                for tp in range(0, NKT, 2):
                    for u in range(2):
                        kt = tp + u
                        src = hT[:, :, kt * 128:(kt + 1) * 128] if kt < L // 128 else hcT[:, :, (kt - L // 128) * 128:(kt - L // 128 + 1) * 128]
                        for kc in range(8):
                            S.op("pe", "matmul", psA[:, u * 256:(u + 1) * 256], lhsT=src[:, kc, :], rhs=wvp[:, kc, :],
                                 start=(kc == 0), stop=(kc == 7), reads=[hT, hcT, wvp], writes=[psA])
                    S.op("act", "copy", out=vaug[:, tp:tp + 2, :, 0:128],
                         in_=psA[:, :].rearrange("p (u hh e) -> p u hh e", hh=2, e=128), reads=[psA], writes=[vaug])
            vh = h % 2
            if ASUB < 4:
                continue
            aT = attnT[i2]
            for qb in range(L // 512):
                qcs = slice(qb * 512, (qb + 1) * 512)

                def qk(kt):
                    for m in range(2):
                        ms = slice(m * 64, (m + 1) * 64)
                        ps = psS2[m][kt % 2]
                        S.op("pe", "matmul", ps[:], lhsT=kT[ms, kt * 128:(kt + 1) * 128], rhs=qT[ms, qcs],
                             start=True, stop=True, reads=[kT, qT], writes=[ps])
                qk(0)
                for kt in range(NKT):
                    if kt + 1 < NKT:
                        qk(kt + 1)
                    for m in range(2):
                        p_ = pT2[m][kt % 3]
                        ps = psS2[m][kt % 2]
                        S.op("act", "activation", out=p_[:], in_=ps[:], func=AF.Exp, scale=0.125, reads=[ps], writes=[p_])
                        S.op("pe", "matmul", OT[m][:], lhsT=vaug[:, kt, vh, 0:128], rhs=p_[:], start=(kt == 0), stop=(kt == NKT - 1),
                             reads=[vaug, p_], writes=[OT[m]])
                        if kt % 2 == 1:
                            pr = ppair[m]
                            S.op("dve", "tensor_tensor", out=pr[:], in0=pprev[m][:], in1=p_[:], op=ALU.add,
                                 reads=[pprev[m], p_], writes=[pr])
                        pprev[m] = p_
                    if kt % 2 == 1:
                        for m in range(2):
                            pr = ppair[m]
                            if kt == 1:
                                S.op("dve", "tensor_copy", out=acc[m][0][:], in_=pr[:], reads=[pr], writes=[acc[m][0]])
                            else:
                                S.op("dve", "tensor_tensor", out=acc[m][0][:], in0=acc[m][0][:], in1=pr[:], op=ALU.add,
                                     reads=[acc[m][0], pr], writes=[acc[m][0]])
                for m in range(2):
                    S.op("dve", "tensor_copy", out=accb[:], in_=acc[m][0][:], reads=[acc[m][0]], writes=[accb])
                    S.op("pe", "matmul", psA[:], lhsT=g.onesbb[:], rhs=accb[:], start=True, stop=True, reads=[accb], writes=[psA])
                    S.op("act", "activation", out=rr[:], in_=psA[:], func=AF.Ln, reads=[psA], writes=[rr])
                    S.op("act", "activation", out=rr[:], in_=rr[:], func=AF.Exp, scale=-1.0, reads=[rr], writes=[rr])
                    S.op("dve", "tensor_tensor", out=om[m][:], in0=OT[m][:], in1=rr[:], op=ALU.mult, reads=[OT[m], rr],
                         writes=[om[m]])
                S.op("dve", "scalar_tensor_tensor", out=of_[:], in0=om[1][:], scalar=nlam[:, 0:1], in1=om[0][:], op0=ALU.mult,
                     op1=ALU.add, reads=[om[0], om[1], nlam], writes=[of_])
                S.op("act", "activation", out=accb[:], in_=of_[:], func=AF.Square, reads=[of_], writes=[accb])
                S.op("pe", "matmul", psA[:], lhsT=g.onesbb[:], rhs=accb[:], start=True, stop=True, reads=[accb], writes=[psA])
                S.op("act", "activation", out=rr[:], in_=psA[:], func=AF.Ln, scale=1.0 / 128, bias=g.eps_t[:, 0:1],
                     reads=[psA], writes=[rr])
                S.op("act", "activation", out=rr[:], in_=rr[:], func=AF.Exp, scale=-0.5, reads=[rr], writes=[rr])
                S.op("dve", "tensor_tensor", out=aT[:, qcs], in0=of_[:], in1=rr[:], op=ALU.mult, reads=[of_, rr], writes=[aT])
            S.dma("sp", d.attnT_d[:, h, :], aT[:], reads=[aT], writes=[("attnT_d", h)])
        S.phase_end()


def mixer_out_phase(S, nc, g, d):
    with contextlib.ExitStack() as st:
        def T(name, shape, dt):
            return st.enter_context(nc.sbuf_tensor("mo" + name, shape, dt))

        def P(name):
            return S.reg_psum(st.enter_context(nc.psum_tensor("mo" + name, [128, 512], F32)))

        wg = T("wg", [128, 8, 2048], BF16)
        why = T("why", [128, 8, D], BF16)
        wda = T("wda", [128, 8, D], BF16)
        wo_ = T("wo", [128, 8, D], BF16)
        wv_ = d.w_in_d.rearrange("(kc p) n -> p kc n", p=128)
        for kc in range(0, 8, 2):
            S.dma("pool", wg[:, kc:kc + 2, :], wv_[:, kc:kc + 2, 6144:8192], writes=[wg])
        for (w, ap) in ((why, d.w_hy_out_d), (wda, d.w_da_out_d), (wo_, d.w_o_d)):
            S.dma("pool", w[:], ap.rearrange("(kc p) n -> p kc n", p=128), writes=[w])
        sg_ = T("sg", [128, 1], F32)
        S.dma("sp", sg_[:], d.subln_g_d.rearrange("o p -> p o"), writes=[sg_], allow_slow_non_contiguous=True)
        S.op("dve", "tensor_scalar", out=why[:], in0=why[:], scalar1=0.5, scalar2=None, op0=ALU.mult,
             reads=[why], writes=[why])
        S.op("dve", "tensor_scalar", out=wda[:], in0=wda[:], scalar1=sg_[:, 0:1], scalar2=0.5 * (1.0 - LAM_INIT),
             op0=ALU.mult, op1=ALU.mult, reads=[wda, sg_], writes=[wda])
        TB = 256
        hTb = [T("hTb%d" % i, [128, 8, TB], BF16) for i in range(2)]
        hyb = [T("hyb%d" % i, [128, 8, TB], BF16) for i in range(2)]
        atb = [T("atb%d" % i, [128, 8, TB], BF16) for i in range(2)]
        mT = T("mT", [128, 8, TB], BF16)
        th = [T("th%d" % i, [128, 2 * TB], F32) for i in range(2)]
        mm = [T("mm%d" % i, [128, 2 * TB], F32) for i in range(2)]
        xres = [T("xres%d" % i, [128, D], F32) for i in range(2)]
        ytmp = [T("yt%d" % i, [128, D], F32) for i in range(2)]
        ss2 = [T("ss2_%d" % i, [128, 2], F32) for i in range(2)]
        rs2 = [T("rs2_%d" % i, [128, 1], F32) for i in range(2)]
        psY = [P("psY%d" % i) for i in range(2)]
        psG = [P("psG%d" % i) for i in range(2)]
        pso = [S.reg_psum(st.enter_context(nc.psum_tensor("mopso%d" % i, [128, D], F32))) for i in range(2)]
        ci = 0
        for b in range(L // TB):
            cs = slice(b * TB, (b + 1) * TB)
            hb, ab, hT = hyb[b % 2], atb[b % 2], hTb[b % 2]
            S.dma("sp", hT[:], d.hT_d[:, :, cs], writes=[hT])
            S.dma("sp", hb[:], d.hyT_d[:, :, cs], writes=[hb])
            S.dma("sp", ab[:], d.attnT_d[:, :, cs], writes=[ab])
            for dc in range(8):
                py, pg = psY[ci % 2], psG[ci % 2]
                th_, mm_ = th[ci % 2], mm[ci % 2]
                ci += 1
                dcs = slice(dc * 128, (dc + 1) * 128)
                for c in range(8):
                    S.op("pe", "matmul", py[:, 0:TB], lhsT=why[:, c, dcs], rhs=hb[:, c, :], start=(c == 0), stop=(c == 7),
                         reads=[why, hb], writes=[py])
                for c in range(8):
                    S.op("pe", "matmul", py[:, TB:2 * TB], lhsT=wda[:, c, dcs], rhs=ab[:, c, :], start=(c == 0), stop=(c == 7),
                         reads=[wda, ab], writes=[py])
                for half in range(2):
                    for kc in range(8):
                        S.op("pe", "matmul", pg[:, half * TB:(half + 1) * TB],
                             lhsT=wg[:, kc, half * 1024 + dc * 128:half * 1024 + (dc + 1) * 128], rhs=hT[:, kc, :],
                             start=(kc == 0), stop=(kc == 7), reads=[wg, hT], writes=[pg])
                S.op("act", "activation", out=th_[:], in_=pg[:], func=AF.Tanh, scale=0.5, reads=[pg], writes=[th_])
                S.op("dve", "scalar_tensor_tensor", out=mm_[:], in0=th_[:], scalar=1.0, in1=py[:], op0=ALU.add, op1=ALU.mult,
                     reads=[th_, py], writes=[mm_])
                S.op("pool", "tensor_tensor", out=mT[:, dc, :], in0=mm_[:, 0:TB], in1=mm_[:, TB:2 * TB], op=ALU.add,
                     reads=[mm_], writes=[mT])
            for ti in range(TB // 128):
                r0 = b * TB + ti * 128
                po = pso[ti % 2]
                for half in range(2):
                    for dc in range(8):
                        S.op("pe", "matmul", po[:, half * 512:(half + 1) * 512], lhsT=mT[:, dc, ti * 128:(ti + 1) * 128],
                             rhs=wo_[:, dc, half * 512:(half + 1) * 512], start=(dc == 0), stop=(dc == 7),
                             reads=[mT, wo_], writes=[po])
                xr = xres[ti % 2]
                S.dma("sp", xr[:], d.x1_d[r0:r0 + 128, :], writes=[xr])
                post_norm_residual(S, g, "mo", po, g.Gx[1], xr, ytmp[ti % 2], ss2[ti % 2], rs2[ti % 2],
                                   d.x2_d[r0:r0 + 128, :], ("x2_d", r0))
        S.phase_end()


def build(debug=False):
    nc = bass.Bass("TRN2", target_bir_lowering=False)

    def din(name, shape):
        return nc.dram_tensor(name, shape, F32, kind="ExternalInput").ap()

    x_d = din("x", [L, D])
    c_d = din("c", [1, D])
    ctx_d = din("ctx", [LC, D])
    cctx_d = din("c_ctx", [1, D])
    w_ada_d = din("w_ada", [D, 9 * D])
    b_ada_d = din("b_ada", [1, 9 * D])
    norm_g_d = din("norm_g", [6, D])
    w_ff_in_d = din("w_ff_in", [2, D, 2 * DFF])
    w_ff_out_d = din("w_ff_out", [2, DFF, D])
    w_in_d = din("w_in", [D, NPROJ])
    ident_d = din("ident", [128, 128])
    d = Ctx()
    d.w_in_d = w_in_d
    d.lq1_d = din("lambda_q1", [1, 64]); d.lk1_d = din("lambda_k1", [1, 64])
    d.lq2_d = din("lambda_q2", [1, 64]); d.lk2_d = din("lambda_k2", [1, 64])
    d.subln_g_d = din("subln_g", [1, 128])
    d.w_hy_out_d = din("w_hy_out", [D, D]); d.w_da_out_d = din("w_da_out", [D, D]); d.w_o_d = din("w_o", [D, D])
    d.hy_conv_w_d = din("hy_conv_w", [3, 3 * D]); d.hy_conv_b_d = din("hy_conv_b", [1, 3 * D])
    d.filt_w1_d = din("filt_w1", [33, 64]); d.filt_b1_d = din("filt_b1", [1, 64])
    d.filt_w2_d = din("filt_w2", [64, 64]); d.filt_b2_d = din("filt_b2", [1, 64])
    d.filt_w3_d = din("filt_w3", [64, 64]); d.filt_b3_d = din("filt_b3", [1, 64])
    d.filt_w4_d = din("filt_w4", [64, 2 * D]); d.filt_freq_d = din("filt_freq", [1, 64])
    d.hy_bias_d = din("hy_bias", [1, D])
    d.tabA_d = din("tabA", [128, 130]); d.tabB_d = din("tabB", [64, 65, 256])
    d.tabBp_d = din("tabBp", [128, 65, 128]); d.tabAp_d = din("tabAp", [65, 128])
    d.zemb_d = din("zemb", [33, 2 * L]); d.deltas_d = din("deltas", [128, 8]); d.tlin1_d = din("tlin1", [2, 8]); d.tlin2_d = din("tlin2", [2, 512])
    d.ropec_d = din("ropec", [128, L]); d.ropes_d = din("ropes", [128, L]); d.rotm_d = din("rotm", [128, 128])
    out_d = nc.dram_tensor("out", [L, D], F32, kind="ExternalOutput").ap()
    sk = "ExternalOutput" if debug else "Internal"
    x1_d = nc.dram_tensor("x1_s", [L, D], F32, kind=sk).ap()
    c1_d = nc.dram_tensor("c1_s", [LC, D], F32, kind=sk).ap()
    hT_d = nc.dram_tensor("hT_s", [128, 8, L], BF16, kind=sk).ap()
    hcT_d = nc.dram_tensor("hcT_s", [128, 8, LC], BF16, kind=sk).ap()
    d.hT_d, d.hcT_d, d.x1_d = hT_d, hcT_d, x1_d
    d.attnT_d = nc.dram_tensor("attnT_s", [128, 8, L], BF16, kind=sk).ap()
    d.hyT_d = nc.dram_tensor("hyT_s", [128, 8, L], BF16, kind=sk).ap()
    d.x2_d = nc.dram_tensor("x2_s", [L, D], F32, kind=sk).ap()
    d.u_d = nc.dram_tensor("u_s", [128, 8, L], BF16, kind=sk).ap()
    d.x0_d = nc.dram_tensor("x0_s", [128, 8, L], BF16, kind=sk).ap()

    with nc.cleanup_on_exit(), contextlib.ExitStack() as gst:
        S = Sched(nc)
        g = Ctx()

        def GT(name, shape, dt):
            return gst.enter_context(nc.sbuf_tensor(name, shape, dt))

        g.identb = GT("identb", [128, 128], BF16)
        g.eps_t = GT("eps_t", [128, 1], F32)
        g.onesb = GT("onesb", [1, 128], BF16)
        g.onesbb = GT("onesbb", [128, 128], BF16)
        g.junk = GT("junk", [128, D], BF16)
        g.Ax = [GT("Ax%d" % i, [128, 8], F32) for i in range(3)]
        g.Bx = [GT("Bx%d" % i, [128, 8], F32) for i in range(3)]
        g.Ac = [GT("Ac%d" % i, [128, 8], F32) for i in range(2)]
        g.Bc = [GT("Bc%d" % i, [128, 8], F32) for i in range(2)]
        g.Gx = [GT("Gx%d" % i, [128, D], F32) for i in range(3)]
        g.Gc = GT("Gc0", [128, D], F32)

        st_f1w = contextlib.ExitStack()
        f1w = ffn_alloc_wi(S, nc, st_f1w, "f1", w_ff_in_d[0], bg=True)

        with contextlib.ExitStack() as st:
            def T(name, shape, dt):
                return st.enter_context(nc.sbuf_tensor("p0" + name, shape, dt))

            identf = T("identf", [128, 128], F32)
            S.dma("sp", identf[:], ident_d, writes=[identf])
            S.op("dve", "tensor_copy", out=g.identb[:], in_=identf[:], reads=[identf], writes=[g.identb])
            S.op("dve", "memset", g.eps_t[:], EPS, writes=[g.eps_t])
            S.op("dve", "memset", g.onesb[:], 1.0, writes=[g.onesb])
            S.op("dve", "memset", g.onesbb[:], 1.0, writes=[g.onesbb])
            craw = T("craw", [128, 2, 8], F32)
            S.dma("sp", craw[:, 0, :], c_d.rearrange("o (kc p) -> p (o kc)", p=128), writes=[craw],
                  allow_slow_non_contiguous=True)
            S.dma("sp", craw[:, 1, :], cctx_d.rearrange("o (kc p) -> p (o kc)", p=128), writes=[craw],
                  allow_slow_non_contiguous=True)
            csil = T("csil", [128, 2, 8], F32)
            S.op("act", "activation", out=csil[:], in_=craw[:], func=AF.Silu, reads=[craw], writes=[csil])
            sT = T("sT", [128, 8, 2], BF16)
            for j in range(2):
                S.op("dve", "tensor_copy", out=sT[:, :, j], in_=csil[:, j, :], reads=[csil], writes=[sT])
            sbc = T("sbc", [128, 2, 8, 128], BF16)
            for j in range(2):
                for kc in range(8):
                    S.op("dve", "tensor_copy", out=sbc[:, j, kc, :], in_=csil[:, j, kc:kc + 1].broadcast_to([128, 128]),
                         reads=[csil], writes=[sbc])
            badaf = [T("badaf%d" % i, [1, D], F32) for i in range(2)]
            badab = [T("badab%d" % i, [1, D], BF16) for i in range(2)]
            gpre = T("gpre", [128, 3, 8], F32)
            for i in range(3):
                S.dma("sp", gpre[:, i, :], norm_g_d[2 * i:2 * i + 1, :].rearrange("o (kc p) -> p (o kc)", p=128),
                      writes=[gpre], allow_slow_non_contiguous=True)
            gpost = [T("gpost%d" % i, [128, D], F32) for i in range(3)]
            for i in range(3):
                S.dma("sp", gpost[i][:], norm_g_d[2 * i + 1:2 * i + 2, :].broadcast_to([128, D]), writes=[gpost[i]])
            wa = [T("wa%d" % i, [128, 8, D], BF16) for i in range(2)]
            waf = [T("waf%d" % i, [128, 2, D], F32) for i in range(3)]
            wfi = [0]
            modp = S.reg_psum(st.enter_context(nc.psum_tensor("p0modp", [128, 8, 2], F32)))
            psG = [S.reg_psum(st.enter_context(nc.psum_tensor("p0psG%d" % i, [128, 512], F32))) for i in range(2)]
            wav = w_ada_d.rearrange("(kc p) n -> p kc n", p=128)
            gi = 0
            for m in range(9):
                w = wa[m % 2]
                bada = badab[m % 2]
                S.dma("sp", badaf[m % 2][:], b_ada_d[0:1, m * D:(m + 1) * D], writes=[badaf[m % 2]])
                S.op("dve", "tensor_copy", out=bada[:], in_=badaf[m % 2][:], reads=[badaf[m % 2]], writes=[bada])
                for kc in range(0, 8, 2):
                    wf = waf[wfi[0] % 3]
                    S.dma("sp", wf[:], wav[:, kc:kc + 2, m * D:(m + 1) * D], writes=[wf])
                    if wfi[0] % 2 == 0:
                        S.op("dve", "tensor_copy", out=w[:, kc:kc + 2, :], in_=wf[:], reads=[wf], writes=[(w.name, kc)])
                    else:
                        S.op("act", "copy", out=w[:, kc:kc + 2, :], in_=wf[:], reads=[wf], writes=[(w.name, kc)])
                    wfi[0] += 1
                grp, which = m // 3, m % 3
                if which < 2:
                    for j in range(8):
                        for kc in range(8):
                            S.op("pe", "matmul", modp[:, j, :], lhsT=w[:, kc, j * 128:(j + 1) * 128], rhs=sT[:, kc, :],
                                 start=(kc == 0), stop=False, reads=[(w.name, (kc // 2) * 2), sT], writes=[modp])
                        S.op("pe", "matmul", modp[:, j, :], lhsT=bada[0:1, j * 128:(j + 1) * 128],
                             rhs=g.onesb[0:1, 0:2], start=False, stop=True, reads=[bada, g.onesb], writes=[modp])
                    if which == 0:
                        S.op("dve", "tensor_copy", out=g.Bx[grp][:], in_=modp[:, :, 0], reads=[modp], writes=[g.Bx[grp]])
                        if grp < 2:
                            S.op("dve", "tensor_copy", out=g.Bc[grp][:], in_=modp[:, :, 1], reads=[modp],
                                 writes=[g.Bc[grp]])
                    else:
                        S.op("dve", "scalar_tensor_tensor", out=g.Ax[grp][:], in0=modp[:, :, 0], scalar=1.0,
                             in1=gpre[:, grp, :], op0=ALU.add, op1=ALU.mult, reads=[modp, gpre], writes=[g.Ax[grp]])
                        if grp < 2:
                            S.op("dve", "scalar_tensor_tensor", out=g.Ac[grp][:], in0=modp[:, :, 1], scalar=1.0,
                                 in1=gpre[:, grp, :], op0=ALU.add, op1=ALU.mult, reads=[modp, gpre],
                                 writes=[g.Ac[grp]])
                else:
                    fac = 1.0 if grp == 1 else 0.5
                    targets = [(0, g.Gx[grp])] + ([(1, g.Gc)] if grp == 0 else [])
                    for (j, Gdst) in targets:
                        for half in range(2):
                            pg = psG[gi % 2]
                            gi += 1
                            for kc in range(8):
                                S.op("pe", "matmul", pg[:], lhsT=sbc[:, j, kc, :], rhs=w[:, kc, half * 512:(half + 1) * 512],
                                     start=(kc == 0), stop=False, reads=[(w.name, (kc // 2) * 2), sbc], writes=[pg])
                            S.op("pe", "matmul", pg[:], lhsT=g.onesb[0:1, :],
                                 rhs=bada[0:1, half * 512:(half + 1) * 512],
                                 start=False, stop=True, reads=[bada, g.onesb], writes=[pg])
                            S.op("dve", "scalar_tensor_tensor", out=Gdst[:, half * 512:(half + 1) * 512], in0=pg[:],
                                 scalar=fac, in1=gpost[grp][:, half * 512:(half + 1) * 512], op0=ALU.mult, op1=ALU.mult,
                                 reads=[pg, gpost[grp]], writes=[Gdst])
            S.phase_end(skip_bg=True)

        if debug:
            dbg_d = nc.dram_tensor("dbg", [128, 10, 8], F32, kind="ExternalOutput").ap()
            dbgG_d = nc.dram_tensor("dbgG", [4, 128, D], F32, kind="ExternalOutput").ap()
            for i, t in enumerate(g.Ax + g.Bx + g.Ac + g.Bc):
                S.dma("sp", dbg_d[:, i, :], t[:], reads=[t], writes=[("dbg", i)])
            for i, t in enumerate(g.Gx + [g.Gc]):
                S.dma("sp", dbgG_d[i], t[:], reads=[t], writes=[("dbgG", i)])
        if STAGE >= 1:
          ffn_phase(S, nc, g, "f1", w_ff_in_d[0], w_ff_out_d[0], preloaded=f1w, streams=[
            dict(T=LC, src=ctx_d, dst=c1_d, A=g.Ac[0], B=g.Bc[0], G=g.Gc,
                 nxt=dict(A=g.Ac[1], B=g.Bc[1], dst=hcT_d)),
        ] + ([dict(T=L, src=x_d, dst=x1_d, A=g.Ax[0], B=g.Bx[0], G=g.Gx[0],
                 nxt=dict(A=g.Ax[1], B=g.Bx[1], dst=hT_d))] if STAGE >= 2 else []))

        st_f1w.close()

        if STAGE >= 3:
            if not HYENA:
                with contextlib.ExitStack() as st:
                    z = st.enter_context(nc.sbuf_tensor("zt", [128, 8, 512], BF16))
                    S.op("dve", "memset", z[:], 0.0, writes=[z])
                    for i in range(L // 512):
                        S.dma("sp", d.hyT_d[:, :, i * 512:(i + 1) * 512], z[:], reads=[z], writes=[("hyz", i)])
                    S.phase_end()
            attention_phase(S, nc, g, d)
            if HYENA:
                hyena_fft_phase(S, nc, g, d)
        if STAGE >= 4:
            mixer_out_phase(S, nc, g, d)
        if STAGE >= 5:
            ffn_phase(S, nc, g, "f2", w_ff_in_d[1], w_ff_out_d[1], [
                dict(T=L, src=d.x2_d, dst=out_d, A=g.Ax[2], B=g.Bx[2], G=g.Gx[2], nxt=None)])
        nc.all_engine_barrier()
        print("instructions", S.n_ins, "waits", S.n_wait)
    return nc


_NC_CACHE = {}


def _host_consts():
    p = np.arange(128)
    dd = p % 64
    axis, half, fr = dd // 32, (dd % 32) // 16, dd % 16
    rotm = np.zeros((128, 128), np.float32)
    for po in range(128):
        if half[po] == 0:
            rotm[po + 16, po] = -1.0
        else:
            rotm[po - 16, po] = 1.0
    t = np.arange(L)
    pos = np.stack([(t // 64).astype(np.float32), (t % 64).astype(np.float32)], 0)
    inv = (np.float32(10000.0) ** (-np.arange(0, 32, 2, dtype=np.float32) / np.float32(32))).astype(np.float32)
    ang = pos[axis, :] * inv[fr][:, None]
    out = {"ident": np.eye(128, dtype=np.float32), "rotm": rotm,
           "ropec": np.cos(ang).astype(np.float32), "ropes": np.sin(ang).astype(np.float32)}
    N = 2 * L
    bp = np.arange(128)[:, None]; f2 = np.arange(65)[None, :]
    phi = 2 * np.pi * ((bp * f2) % 128) / 128
    out["tabA"] = np.concatenate([np.cos(phi), -np.sin(phi)], 1).astype(np.float32)
    ap = np.arange(64)[:, None, None]; f2_ = np.arange(65)[None, :, None]; f1 = np.arange(64)[None, None, :]
    th = 2 * np.pi * ((ap * (128 * f1 + f2_)) % N) / N
    out["tabB"] = np.concatenate([np.cos(th), -np.sin(th), np.sin(th), np.cos(th)], 2).astype(np.float32)
    thT = np.transpose(th, (2, 1, 0))
    top = np.concatenate([np.cos(thT), np.sin(thT)], 2)
    bot = np.concatenate([-np.sin(thT), np.cos(thT)], 2)
    out["tabBp"] = np.concatenate([top, bot], 0).astype(np.float32)
    w = np.full(65, 2.0); w[0] = 1; w[64] = 1
    phiT = 2 * np.pi * np.arange(65)[:, None] * np.arange(64)[None, :] / 128
    out["tabAp"] = (np.concatenate([w[:, None] * np.cos(phiT), -w[:, None] * np.sin(phiT)], 1) / N).astype(np.float32)
    tt = np.linspace(0.0, 1.0, L, dtype=np.float32)[None, :]
    ww = (2.0 * np.pi * np.arange(L, dtype=np.float32) / L)[None, :]
    ff = np.linspace(1e-4, 15, 16, dtype=np.float32)[:, None]
    zemb = np.concatenate([tt, np.cos(ff * ww), -np.sin(ff * ww)], 0).astype(np.float32)
    ridx = (L - np.arange(L)) % L
    out["zemb"] = np.ascontiguousarray(np.concatenate([zemb, zemb[:, ridx]], 1))
    deltas = np.abs(np.linspace(math.log(1e-2) / 1.5, math.log(1e-2) / 0.3, D, dtype=np.float32))
    out["deltas"] = np.ascontiguousarray(-deltas.reshape(8, 128).T).astype(np.float32)
    a64 = np.arange(64, dtype=np.float64)
    b8 = np.arange(8, dtype=np.float64); r512 = np.arange(512, dtype=np.float64)
    out["tlin1"] = np.stack([512 * b8 / (L - 1), (L - 512 * b8) / (L - 1)], 0).astype(np.float32)
    out["tlin2"] = np.stack([r512 / (L - 1), -r512 / (L - 1)], 0).astype(np.float32)
    return out


def make_in_maps(inputs, cores):
    consts = _host_consts()
    maps = []
    f = lambda a: np.ascontiguousarray(np.asarray(a, dtype=np.float32))
    for b in cores:
        m = {
            "x": f(inputs["x"][b]), "c": f(inputs["c"][b:b + 1]), "ctx": f(inputs["ctx"][b]),
            "c_ctx": f(np.asarray(inputs["c_ctx"]).reshape(1, D)),
            "w_ada": f(inputs["w_ada"][0]), "b_ada": f(inputs["b_ada"][0:1]), "norm_g": f(inputs["norm_g"][0]),
            "w_ff_in": f(inputs["w_ff_in"][0]), "w_ff_out": f(inputs["w_ff_out"][0]), "w_in": f(inputs["w_in"][0]),
            "lambda_q1": f(inputs["lambda_q1"][0:1]), "lambda_k1": f(inputs["lambda_k1"][0:1]),
            "lambda_q2": f(inputs["lambda_q2"][0:1]), "lambda_k2": f(inputs["lambda_k2"][0:1]),
            "subln_g": f(inputs["subln_g"][0:1]), "w_hy_out": f(inputs["w_hy_out"][0]),
            "hy_conv_w": f(inputs["hy_conv_w"][0]), "hy_conv_b": f(inputs["hy_conv_b"][0:1]),
            "filt_w1": f(inputs["filt_w1"][0]), "filt_b1": f(inputs["filt_b1"][0:1]),
            "filt_w2": f(inputs["filt_w2"][0]), "filt_b2": f(inputs["filt_b2"][0:1]),
            "filt_w3": f(inputs["filt_w3"][0]), "filt_b3": f(inputs["filt_b3"][0:1]),
            "filt_w4": f(inputs["filt_w4"][0]), "filt_freq": f(inputs["filt_freq"][0:1]),
            "hy_bias": f(inputs["hy_bias"][0:1]),
            "w_da_out": f(inputs["w_da_out"][0]), "w_o": f(inputs["w_o"][0]),
        }
        m.update(consts)
        maps.append(m)
    return maps


def kernel(**inputs):
    if "nc" not in _NC_CACHE:
        _NC_CACHE["nc"] = build()
    nc = _NC_CACHE["nc"]
    in_maps = make_in_maps(inputs, list(range(8)))
    res = run_bass_kernel_spmd(nc, in_maps, core_ids=list(range(8)))
    return np.stack([np.asarray(r["out"], dtype=np.float32) for r in res.results], axis=0)
```

```python
import contextlib
import math
import os
STAGE = int(os.environ.get('KSTAGE', '99'))
FSUB = int(os.environ.get('FSUB', '99'))
FV = os.environ.get('FV', '')
ATH = int(os.environ.get('ATH', '8'))
HYENA = int(os.environ.get('HYENA', '1'))
ASUB = int(os.environ.get('ASUB', '99'))
import numpy as np
import concourse.bass as bass
import concourse.mybir as mybir
from concourse.bass_utils import run_bass_kernel_spmd

F32 = mybir.dt.float32
BF16 = mybir.dt.bfloat16
AF = mybir.ActivationFunctionType
ALU = mybir.AluOpType
AX = mybir.AxisListType

D = 1024
L = 4096
LC = 256
LK = L + LC
DFF = 2816
NJ = DFF // 128
NH = 8
EPS = 1e-6
NPROJ = 8192
N_DMA_LANES = 40
N_BG_LANES = 24
LAM_INIT = 0.8 - 0.6 * math.exp(-0.3 * 0)


class Sched:
    def __init__(self, nc):
        self.nc = nc
        self.eng = {"pe": nc.tensor, "act": nc.scalar, "dve": nc.vector,
                    "pool": nc.gpsimd, "sp": nc.sync}
        self.sem = {}
        self.cnt = {}
        for e in self.eng:
            self.sem[e] = nc.alloc_semaphore("s_" + e)
            self.cnt[e] = 0
        for i in range(N_DMA_LANES):
            self.sem["d%d" % i] = nc.alloc_semaphore("d%d" % i)
            self.cnt["d%d" % i] = 0
        for i in range(N_BG_LANES):
            self.sem["b%d" % i] = nc.alloc_semaphore("b%d" % i)
            self.cnt["b%d" % i] = 0
        self.next_bg = 0
        self.next_lane = 0
        for k in self.sem:
            nc.gpsimd.sem_clear(self.sem[k])
        nc.all_engine_barrier()
        self.waited = {e: {} for e in self.eng}
        self.W = {}
        self.R = {}
        self.q = {e: [] for e in self.eng}
        self.n_ins = 0
        self.n_wait = 0
        self.psum = set()

    def reg_psum(self, t):
        self.psum.add(t.name)
        return t

    def _wait(self, eng, deps):
        need = {}
        for (k, v) in deps:
            if v > need.get(k, 0):
                need[k] = v
        w = self.waited[eng]
        for k, v in need.items():
            if k == eng and eng in ("pe", "sp"):
                continue
            if w.get(k, 0) >= v:
                continue
            self.q[eng].append(("w", self.sem[k], v))
            self.n_wait += 1
            w[k] = v

    @staticmethod
    def _key(k):
        if isinstance(k, tuple):
            return tuple(Sched._key(x) for x in k)
        if isinstance(k, (str, int)):
            return k
        return k.name

    def _deps(self, reads, writes):
        deps = set()
        for k in reads:
            if k in self.W:
                deps.add(self.W[k])
        for k in writes:
            if k in self.W:
                deps.add(self.W[k])
            for ev in self.R.get(k, {}).items():
                deps.add(ev)
        return deps

    def _commit(self, ev, reads, writes):
        for k in reads:
            r = self.R.setdefault(k, {})
            if ev[1] > r.get(ev[0], 0):
                r[ev[0]] = ev[1]
        for k in writes:
            self.W[k] = ev
            self.R[k] = {}

    def op(self, eng, meth, *args, reads=(), writes=(), **kw):
        reads = [self._key(k) for k in reads]
        writes = [self._key(k) for k in writes]
        writes = writes + [k for k in reads if k in self.psum]
        reads = [k for k in reads if k not in self.psum]
        self._wait(eng, self._deps(reads, writes))
        fn = (lambda e, meth=meth, args=args, kw=kw: getattr(e, meth)(*args, **kw))
        self.cnt[eng] += 1
        self.q[eng].append(("i", fn, self.sem[eng], 1))
        self.n_ins += 1
        self._commit((eng, self.cnt[eng]), reads, writes)

    def dma(self, eng, out, in_, reads=(), writes=(), bg=False, **kw):
        reads = [self._key(k) for k in reads]
        writes = [self._key(k) for k in writes]
        if bg:
            lane = "b%d" % self.next_bg
            self.next_bg = (self.next_bg + 1) % N_BG_LANES
        else:
            lane = "d%d" % self.next_lane
            self.next_lane = (self.next_lane + 1) % N_DMA_LANES
        deps = self._deps(reads, writes)
        if self.cnt[lane] > 0:
            deps.add((lane, self.cnt[lane]))
        self._wait(eng, deps)
        self.cnt[lane] += 16
        self.q[eng].append(("i", (lambda e, out=out, in_=in_, kw=kw: e.dma_start(out=out, in_=in_, **kw)),
                            self.sem[lane], 16))
        self.n_ins += 1
        self._commit((lane, self.cnt[lane]), reads, writes)

    def emit(self):
        nc = self.nc
        q = self.q
        self.q = {e: [] for e in self.eng}

        def run(e, items):
            for it in items:
                if it[0] == "w":
                    e.wait_ge(it[1], it[2])
                else:
                    it[1](e).then_inc(it[2], it[3])

        with nc.Block() as block:
            @block.sync
            def _(e):
                run(e, q["sp"])

            @block.scalar
            def _(e):
                run(e, q["act"])

            @block.vector
            def _(e):
                run(e, q["dve"])

            @block.gpsimd
            def _(e):
                run(e, q["pool"])

            @block.tensor
            def _(e):
                run(e, q["pe"])

    def barrier(self, skip_bg=False):
        allev = set((k, v) for k, v in self.cnt.items() if v > 0 and not (skip_bg and k[0] == "b"))
        for e in self.eng:
            self._wait(e, allev)

    def phase_end(self, skip_bg=False):
        self.barrier(skip_bg)
        self.emit()


class Ctx:
    pass


def _pn_stats(S, g, xin, xin_key, bs):
    ss, rs, xs = bs["ss"], bs["rs"], bs["xs"]
    S.op("pool", "memset", ss[:], 0.0, writes=[ss])
    S.op("act", "activation", out=g.junk[:], in_=xin, func=AF.Square, accum_out=ss[:],
         reads=[xin_key, ss], writes=[ss])
    S.op("act", "activation", out=rs[:], in_=ss[:], func=AF.Ln, scale=1.0 / D, bias=g.eps_t[:, 0:1],
         reads=[ss], writes=[rs])
    S.op("act", "activation", out=rs[:], in_=rs[:], func=AF.Exp, scale=-0.5, reads=[rs], writes=[rs])
    S.op("dve", "tensor_scalar", out=xs[:], in0=xin, scalar1=rs[:, 0:1], scalar2=None, op0=ALU.mult,
         reads=[xin_key, rs], writes=[xs])


def _pn_transpose(S, g, bs, psT, A, B, dst, dst_key, col0):
    xs = bs["xs"]
    for kc in range(8):
        S.op("pe", "transpose", out=psT[:, kc, :], in_=xs[:, kc * 128:(kc + 1) * 128], identity=g.identb[:],
             reads=[xs], writes=[psT])
    for kc in range(8):
        S.op("act", "activation", out=dst[:, kc, col0:col0 + 128], in_=psT[:, kc, :], func=AF.Identity,
             scale=A[:, kc:kc + 1], bias=B[:, kc:kc + 1], reads=[psT], writes=[dst_key])


def ffn_alloc_wi(S, nc, st, tag, w_in_d, from_bf16=False, bg=False):
    wi = st.enter_context(nc.sbuf_tensor(tag + "wi", [128, 8, 2 * DFF], BF16))
    q = "sp" if from_bf16 else "pool"
    for kc in range(8):
        S.dma(q, wi[:, kc, :], w_in_d[kc * 128:(kc + 1) * 128, :], reads=[("wsrc", tag, "i", kc)], writes=[wi], bg=bg)
    return wi


def ffn_alloc_wo(S, nc, st, tag, w_out_d, from_bf16=False, bg=False):
    wo = st.enter_context(nc.sbuf_tensor(tag + "wo", [128, NJ, D], BF16))
    q = "sp" if from_bf16 else "pool"
    wov = w_out_d.rearrange("(j p) n -> p j n", p=128)
    for j0 in range(0, NJ, 2):
        S.dma(q, wo[:, j0:j0 + 2, :], wov[:, j0:j0 + 2, :], reads=[("wsrc", tag, "o", j0)], writes=[wo], bg=bg)
    return wo


def ffn_phase(S, nc, g, tag, w_in_d, w_out_d, streams, preloaded=None, from_bf16=False):
    with contextlib.ExitStack() as st:
        def T(name, shape, dt):
            return st.enter_context(nc.sbuf_tensor(tag + name, shape, dt))

        def P(name, shape, dt):
            return S.reg_psum(st.enter_context(nc.psum_tensor(tag + name, shape, dt)))

        g.psT = [P("psT%d" % i, [128, 8, 128], BF16) for i in range(2)]
        wi = preloaded if preloaded is not None else ffn_alloc_wi(S, nc, st, tag, w_in_d, from_bf16=from_bf16)
        wo = ffn_alloc_wo(S, nc, st, tag, w_out_d, from_bf16=from_bf16)
        TB = 256
        xin = [T("xin%d" % i, [128, D], F32) for i in range(2)]
        xres = [T("xres%d" % i, [128, D], F32) for i in range(1)]
        xT = [T("xT%d" % i, [128, 8, TB], BF16) for i in range(2)]
        hT = T("hT", [128, NJ, TB], BF16)
        sg = [T("sg%d" % i, [128, TB], F32) for i in range(2)]
        ytmp = [T("yt%d" % i, [128, D], F32) for i in range(2)]
        h2 = [T("h2_%d" % i, [128, 8, 128], BF16) for i in range(2)]
        ss2 = [T("ss2_%d" % i, [128, 2], F32) for i in range(2)]
        rs2 = [T("rs2_%d" % i, [128, 1], F32) for i in range(2)]
        psgu = [P("psgu%d" % i, [128, 2, TB], F32) for i in range(2)]
        pso = [P("pso%d" % i, [128, D], F32) for i in range(2)]

        cnt = dict(gu=0)
        blocks = []
        for sdef in streams:
            nblk = (sdef["T"] + TB - 1) // TB
            for b_ in range(nblk):
                blocks.append((sdef, b_))
        bs_pre = [dict(ss=T("ssp%d" % i, [128, 1], F32), rs=T("rsp%d" % i, [128, 1], F32), xs=T("xsp%d" % i, [128, D], BF16))
                  for i in range(2)]
        bs_nxt = [dict(ss=T("ssn%d" % i, [128, 1], F32), rs=T("rsn%d" % i, [128, 1], F32), xs=T("xsn%d" % i, [128, D], BF16))
                  for i in range(2)]

        def geom(bi):
            sdef, b_ = blocks[bi]
            t0 = b_ * TB
            return sdef, t0, min(TB, sdef["T"] - t0) // 128

        def stats(bi):
            sdef, t0, ntile = geom(bi)
            for ti in range(ntile):
                r0 = t0 + ti * 128
                xi = xin[ti % 2]
                S.dma("sp", xi[:], sdef["src"][r0:r0 + 128, :], writes=[xi])
                _pn_stats(S, g, xi[:], xi.name, bs_pre[ti % 2])

        def transposes(bi):
            sdef, t0, ntile = geom(bi)
            xTb = xT[bi % 2]
            for ti in range(ntile):
                _pn_transpose(S, g, bs_pre[ti % 2], g.psT[ti % 2], sdef["A"], sdef["B"], xTb, xTb.name, ti * 128)

        def gateup(bi):
            sdef, t0, ntile = geom(bi)
            nt = ntile * 128
            xTb = xT[bi % 2]
            for j in range(NJ):
                pg = psgu[cnt["gu"] % 2]
                sgi = sg[cnt["gu"] % 2]
                cnt["gu"] += 1
                for kc in range(8):
                    S.op("pe", "matmul", pg[:, 0, :nt], lhsT=wi[:, kc, j * 128:(j + 1) * 128], rhs=xTb[:, kc, :nt],
                         start=(kc == 0), stop=(kc == 7), reads=[wi, xTb], writes=[pg])
                for kc in range(8):
                    S.op("pe", "matmul", pg[:, 1, :nt], lhsT=wi[:, kc, DFF + j * 128:DFF + (j + 1) * 128],
                         rhs=xTb[:, kc, :nt], start=(kc == 0), stop=(kc == 7), reads=[wi, xTb], writes=[pg])
                S.op("act", "activation", out=sgi[:, :nt], in_=pg[:, 0, :nt], func=AF.Silu, reads=[pg], writes=[sgi])
                S.op("dve", "tensor_tensor", out=hT[:, j, :nt], in0=sgi[:, :nt], in1=pg[:, 1, :nt], op=ALU.mult,
                     reads=[sgi, pg], writes=[hT])

        def outmm(bi):
            sdef, t0, ntile = geom(bi)
            for ti in range(ntile):
                po = pso[ti % 2]
                for half in range(2):
                    for j in range(NJ):
                        S.op("pe", "matmul", po[:, half * 512:(half + 1) * 512],
                             lhsT=hT[:, j, ti * 128:(ti + 1) * 128], rhs=wo[:, j, half * 512:(half + 1) * 512],
                             start=(j == 0), stop=(j == NJ - 1), reads=[hT, wo], writes=[po])

        def post(bi):
            sdef, t0, ntile = geom(bi)
            for ti in range(ntile):
                r0 = t0 + ti * 128
                xr = xres[0]
                S.dma("sp", xr[:], sdef["src"][r0:r0 + 128, :], writes=[xr])
                yt = ytmp[ti % 2]
                post_norm_residual(S, g, tag, pso[ti % 2], sdef["G"], xr, yt, ss2[ti % 2], rs2[ti % 2],
                                   sdef["dst"][r0:r0 + 128, :], (tag, "dst", sdef["T"], r0))
                if sdef.get("nxt") is not None:
                    _pn_stats(S, g, yt[:], yt.name, bs_nxt[ti % 2])

        def nxt_transposes(bi):
            sdef, t0, ntile = geom(bi)
            if sdef.get("nxt") is None:
                return
            nx = sdef["nxt"]
            for ti in range(ntile):
                r0 = t0 + ti * 128
                hh = h2[ti % 2]
                _pn_transpose(S, g, bs_nxt[ti % 2], g.psT[ti % 2], nx["A"], nx["B"], hh, hh.name, 0)
                S.dma("sp", nx["dst"][:, :, r0:r0 + 128], hh[:], reads=[hh], writes=[(tag, "hdst", sdef["T"], r0)])

        nb = len(blocks)
        stats(0)
        transposes(0)
        for bi in range(nb):
            if bi + 1 < nb:
                stats(bi + 1)
            gateup(bi)
            if bi + 1 < nb:
                transposes(bi + 1)
            if bi >= 1:
                nxt_transposes(bi - 1)
            outmm(bi)
            post(bi)
        nxt_transposes(nb - 1)
        S.phase_end()


def post_norm_residual(S, g, tag, po, G, xr, yt, s2, r2, dst_ap, dst_key):
    S.op("pool", "memset", s2[:], 0.0, writes=[s2])
    for half in range(2):
        S.op("act", "activation", out=g.junk[:, 0:512], in_=po[:, half * 512:(half + 1) * 512],
             func=AF.Square, accum_out=s2[:, half:half + 1], reads=[po, s2], writes=[s2])
    S.op("dve", "tensor_tensor", out=r2[:], in0=s2[:, 0:1], in1=s2[:, 1:2], op=ALU.add, reads=[s2], writes=[r2])
    S.op("act", "activation", out=r2[:], in_=r2[:], func=AF.Ln, scale=1.0 / D, bias=g.eps_t[:, 0:1],
         reads=[r2], writes=[r2])
    S.op("act", "activation", out=r2[:], in_=r2[:], func=AF.Exp, scale=-0.5, reads=[r2], writes=[r2])
    S.op("dve", "scalar_tensor_tensor", out=yt[:], in0=po[:], scalar=r2[:, 0:1], in1=G[:],
         op0=ALU.mult, op1=ALU.mult, reads=[po, r2, G], writes=[yt])
    S.op("pool", "tensor_tensor", out=yt[:], in0=yt[:], in1=xr[:], op=ALU.add, reads=[yt, xr], writes=[yt])
    S.dma("sp", dst_ap, yt[:], reads=[yt], writes=[dst_key])


def hyena_proj(S, nc, g, d, hT, st_outer):
    with contextlib.ExitStack() as st:
        def T(name, shape, dt):
            return st.enter_context(nc.sbuf_tensor("hp" + name, shape, dt))

        psA = S.reg_psum(st.enter_context(nc.psum_tensor("hppsA", [128, 512], F32)))
        psA2 = S.reg_psum(st.enter_context(nc.psum_tensor("hppsA2", [128, 512], F32)))
        pss = [psA, psA2]
        cw = T("cw", [128, 3, 3, 8], F32)
        cb = T("cb", [128, 3, 8], F32)
        for j in range(3):
            for p in range(3):
                S.dma("sp", cw[:, j, p, :], d.hy_conv_w_d[j:j + 1, p * D:(p + 1) * D].rearrange("o (ch q) -> q (o ch)", q=128),
                      writes=[cw], allow_slow_non_contiguous=True)
        for p in range(3):
            S.dma("sp", cb[:, p, :], d.hy_conv_b_d[0:1, p * D:(p + 1) * D].rearrange("o (ch q) -> q (o ch)", q=128),
                  writes=[cb], allow_slow_non_contiguous=True)
        w3 = [T("w3_%d" % i, [128, 8, 3, 128], BF16) for i in range(2)]
        zhs = [T("zh%d" % i, [128, L + 2], F32) for i in range(2)]
        zc = [T("zc%d" % i, [128, L], F32) for i in range(2)]
        ob = [T("ob%d" % i, [128, L], BF16) for i in range(2)]
        for zh in zhs:
            S.op("pool", "memset", zh[:, 0:1], 0.0, writes=[zh])
            S.op("pool", "memset", zh[:, L + 1:L + 2], 0.0, writes=[zh])
        zi = 0
        wv_ = d.w_in_d.rearrange("(kc p) n -> p kc n", p=128)

        def load_w(ch):
            for p in range(3):
                S.dma("pool", w3[ch % 2][:, :, p, :], wv_[:, :, p * D + ch * 128:p * D + (ch + 1) * 128], writes=[w3[ch % 2]])
        load_w(0)
        pi = 0
        for ch in range(8):
            if ch + 1 < 8:
                load_w(ch + 1)
            w = w3[ch % 2]
            for part in (0, 1, 2):
                zh = zhs[zi % 2]
                zi += 1
                for blk in range(L // 512):
                    ps = pss[pi % 2]
                    pi += 1
                    for kc in range(8):
                        S.op("pe", "matmul", ps[:], lhsT=w[:, kc, part, :], rhs=hT[:, kc, blk * 512:(blk + 1) * 512],
                             start=(kc == 0), stop=(kc == 7), reads=[w, hT], writes=[ps])
                    S.op("act", "copy", out=zh[:, 1 + blk * 512:1 + (blk + 1) * 512], in_=ps[:], reads=[ps], writes=[zh])
                z = zc[1] if part == 1 else zc[0]
                S.op("act", "activation", out=z[:], in_=zh[:, 1:L + 1], func=AF.Identity, scale=cw[:, 1, part, ch:ch + 1],
                     bias=cb[:, part, ch:ch + 1], reads=[zh, cw, cb], writes=[z])
                S.op("dve", "scalar_tensor_tensor", out=z[:], in0=zh[:, 0:L], scalar=cw[:, 0, part, ch:ch + 1], in1=z[:],
                     op0=ALU.mult, op1=ALU.add, reads=[zh, cw, z], writes=[z])
                S.op("dve", "scalar_tensor_tensor", out=z[:], in0=zh[:, 2:L + 2], scalar=cw[:, 2, part, ch:ch + 1], in1=z[:],
                     op0=ALU.mult, op1=ALU.add, reads=[zh, cw, z], writes=[z])
                if part == 0:
                    S.op("act", "copy", out=ob[0][:], in_=z[:], reads=[z], writes=[ob[0]])
                    S.dma("sp", d.x0_d[:, ch, :], ob[0][:], reads=[ob[0]], writes=[("x0_d", ch)])
                elif part == 2:
                    S.op("pool", "tensor_tensor", out=ob[1][:], in0=z[:], in1=zc[1][:], op=ALU.mult, reads=[z, zc[1]],
                         writes=[ob[1]])
                    S.dma("sp", d.u_d[:, ch, :], ob[1][:], reads=[ob[1]], writes=[("u_d", ch)])
        S.barrier()


def hyena_fft_phase(S, nc, g, d):
    TWO_PI = 2.0 * math.pi
    MAGIC = 12582912.0
    with contextlib.ExitStack() as st:
        def T(name, shape, dt):
            return st.enter_context(nc.sbuf_tensor("hf" + name, shape, dt))

        bank = [S.reg_psum(st.enter_context(nc.psum_tensor("hfbank%d" % i, [128, 512], F32))) for i in range(8)]
        bankb = [b[:].bitcast(BF16) for b in bank]
        tabA = T("tabA", [128, 130], BF16)
        tabB = T("tabB", [64, 65, 256], BF16)
        tabBp = T("tabBp", [128, 65, 128], BF16)
        tabAp = T("tabAp", [65, 128], BF16)
        S.dma("pool", tabA[:], d.tabA_d, writes=[tabA])
        for f0 in range(0, 65, 13):
            S.dma("pool", tabB[:, f0:f0 + 13, :], d.tabB_d[:, f0:f0 + 13, :], writes=[tabB])
            S.dma("pool", tabBp[:, f0:f0 + 13, :], d.tabBp_d[:, f0:f0 + 13, :], writes=[tabBp])
        S.dma("pool", tabAp[:], d.tabAp_d, writes=[tabAp])
        w4 = T("w4", [64, 2 * D], BF16)
        S.dma("pool", w4[:], d.filt_w4_d, writes=[w4])
        hA = T("hA", [64, 2 * L], BF16)
        with contextlib.ExitStack() as st2:
            def T2(name, shape, dt):
                return st2.enter_context(nc.sbuf_tensor("hf" + name, shape, dt))
            zT = T2("zT", [33, 2 * L], BF16)
            S.dma("pool", zT[:], d.zemb_d, writes=[zT])
            w1 = T2("w1", [33, 64], BF16)
            w2 = T2("w2", [64, 64], BF16)
            w3_ = T2("w3", [64, 64], BF16)
            S.dma("pool", w1[:], d.filt_w1_d, writes=[w1])
            S.dma("pool", w2[:], d.filt_w2_d, writes=[w2])
            S.dma("pool", w3_[:], d.filt_w3_d, writes=[w3_])
            fb = T2("fb", [64, 4], F32)
            for i, ap in enumerate([d.filt_b1_d, d.filt_b2_d, d.filt_b3_d, d.filt_freq_d]):
                S.dma("sp", fb[:, i:i + 1], ap.rearrange("o p -> p o"), writes=[fb], allow_slow_non_contiguous=True)
            fbb = T2("fbb", [64, 3], F32)
            S.op("dve", "tensor_scalar", out=fbb[:], in0=fb[:, 0:3], scalar1=fb[:, 3:4], scalar2=None, op0=ALU.mult,
                 reads=[fb], writes=[fbb])
            hB = T2("hB", [64, 2 * L], BF16)
            ra = [T2("ra%d" % i, [64, 512], F32) for i in range(2)]
            rb = [T2("rb%d" % i, [64, 512], F32) for i in range(2)]
            li = 0
            for layer, (wl, src, dst) in enumerate(((w1, zT, hA), (w2, hA, hB), (w3_, hB, hA))):
                kdim = 33 if layer == 0 else 64
                for blk in range(2 * L // 512):
                    cs = slice(blk * 512, (blk + 1) * 512)
                    ps = bank[6 + li % 2]
                    a_, b_ = ra[li % 2], rb[li % 2]
                    li += 1
                    S.op("pe", "matmul", ps[0:64, :], lhsT=wl[0:kdim, :], rhs=src[0:kdim, cs], start=True, stop=True,
                         reads=[wl, src], writes=[ps])
                    S.op("act", "activation", out=a_[:], in_=ps[0:64, :], func=AF.Identity, scale=fb[:, 3:4],
                         bias=fbb[:, layer:layer + 1], reads=[ps, fb, fbb], writes=[a_])
                    S.op("dve", "tensor_scalar", out=b_[:], in0=a_[:], scalar1=1.0 / TWO_PI, scalar2=MAGIC, op0=ALU.mult,
                         op1=ALU.add, reads=[a_], writes=[b_])
                    S.op("dve", "tensor_scalar", out=b_[:], in0=b_[:], scalar1=MAGIC, scalar2=-TWO_PI, op0=ALU.subtract,
                         op1=ALU.mult, reads=[b_], writes=[b_])
                    S.op("dve", "tensor_tensor", out=a_[:], in0=a_[:], in1=b_[:], op=ALU.add, reads=[a_, b_], writes=[a_])
                    S.op("dve", "tensor_scalar", out=a_[:], in0=a_[:], scalar1=3.1415925, scalar2=-3.1415925, op0=ALU.min,
                         op1=ALU.max, reads=[a_], writes=[a_])
                    S.op("act", "activation", out=dst[:, cs], in_=a_[:], func=AF.Sin, reads=[a_], writes=[dst])
            S.barrier()
        h3 = hA
        dl = T("dl", [128, 8], F32)
        S.dma("sp", dl[:], d.deltas_d, writes=[dl])
        tl1 = T("tl1", [128, 2, 8], F32)
        tl2 = T("tl2", [128, 512], F32)
        for j in range(2):
            S.dma("sp", tl1[:, j, :], d.tlin1_d[j:j + 1, :].broadcast_to([128, 8]), writes=[tl1])
        S.dma("sp", tl2[:], d.tlin2_d[0:1, :].broadcast_to([128, 512]), writes=[tl2])
        ndl = T("ndl", [128, 8], F32)
        S.op("dve", "tensor_scalar", out=ndl[:], in0=dl[:], scalar1=-1.0, scalar2=None, op0=ALU.mult, reads=[dl], writes=[ndl])
        hbias = T("hbias", [128, 8], F32)
        S.dma("sp", hbias[:], d.hy_bias_d.rearrange("o (ch q) -> q (o ch)", q=128), writes=[hbias],
              allow_slow_non_contiguous=True)
        E1 = T("E1", [128, 2, 8], F32)
        E2 = T("E2", [128, 2, 512], F32)
        sq2 = T("sq2", [128, 2, L], BF16)
        x0 = T("x0", [128, L], BF16)
        hyo = sq2[:, 1, :]
        buf1 = T("buf1", [128, 65 * 128], BF16)
        buf2 = T("buf2", [128, 65 * 128], BF16)
        buf3 = T("buf3", [128, 128 * 130], BF16)
        Kf = T("Kf", [128, 65, 128], BF16)
        P1 = [T("P1_%d" % i, [128, 4, 128], F32) for i in range(2)]
        P2 = [T("P2_%d" % i, [128, 4, 128], F32) for i in range(1)]
        asum = T("asum", [128, 17], F32)
        nrm = T("nrm", [128, 1], F32)
        UB = buf1[0:64, 0:64 * 128].rearrange("p (a c) -> p a c", c=128)
        UBf = buf1[:, 0:64 * 128].rearrange("p (a c) -> p a c", c=128)
        YT = buf1[:, :].rearrange("p (f c) -> p f c", c=128)
        Y = buf2[:, :].rearrange("p (f k) -> p f k", k=128)
        Q = Y
        VA = buf3[0:64, :].rearrange("p (c k) -> p c k", k=130)
        QT = buf3[0:65, 0:128 * 128].rearrange("p (c a) -> p c a", a=128)
        ev = [0]
        rot = [0]

        def nb():
            rot[0] += 1
            return (rot[0] - 1) % 8

        def evac(out, in_, reads, writes):
            if ev[0] % 4 != 3:
                S.op("act", "copy", out=out, in_=in_, reads=reads, writes=writes)
            else:
                S.op("dve", "tensor_copy", out=out, in_=in_, reads=reads, writes=writes)
            ev[0] += 1

        def forward(full, consume):
            K = 128 if full else 64
            sv = sq2[:, :, :].rearrange("p h (b a) -> p a h b", a=64)
            ub = UBf if full else UB
            for a0 in range(0, 64, 8):
                bi = nb()
                for u in range(8):
                    in_ = sv[:, a0 + u, :, :] if full else sv[:, a0 + u, 0, :]
                    S.op("pe", "transpose", out=bankb[bi][0:K, u * 128:(u + 1) * 128], in_=in_,
                         identity=g.identb[:], reads=[sq2], writes=[bank[bi]])
                evac(ub[:, a0:a0 + 8, :], bankb[bi][0:K, 0:1024].rearrange("p (a c) -> p a c", c=128), [bank[bi]], [buf1])
            for c0 in range(0, 128, 3):
                n = min(3, 128 - c0)
                bi = nb()
                for u in range(n):
                    S.op("pe", "matmul", bank[bi][0:64, u * 130:(u + 1) * 130], lhsT=ub[:, :, c0 + u], rhs=tabA[0:K, :],
                         start=True, stop=True, reads=[buf1, tabA], writes=[bank[bi]])
                evac(VA[:, c0:c0 + n, :], bank[bi][0:64, 0:n * 130].rearrange("p (c k) -> p c k", k=130), [bank[bi]], [buf3])
            for f0 in range(0, 65, 4):
                n = min(4, 65 - f0)
                bi = nb()
                for u in range(n):
                    f = f0 + u
                    S.op("pe", "matmul", bank[bi][:, u * 128:(u + 1) * 128], lhsT=VA[:, :, f], rhs=tabB[:, f, 0:128],
                         start=True, stop=False, reads=[buf3, tabB], writes=[bank[bi]])
                    S.op("pe", "matmul", bank[bi][:, u * 128:(u + 1) * 128], lhsT=VA[:, :, 65 + f], rhs=tabB[:, f, 128:256],
                         start=False, stop=True, reads=[buf3, tabB], writes=[bank[bi]])
                consume(bank[bi], f0, n)

        for ch in range(8):
            for j in range(2):
                S.op("act", "activation", out=E1[:, j, :], in_=tl1[:, j, :], func=AF.Exp, scale=dl[:, ch:ch + 1],
                     reads=[tl1, dl], writes=[E1])
                S.op("act", "activation", out=E2[:, j, :], in_=tl2[:], func=AF.Exp,
                     scale=(dl if j == 0 else ndl)[:, ch:ch + 1], reads=[tl2, dl, ndl], writes=[E2])
            S.op("pool", "memset", asum[:], 0.0, writes=[asum, sq2] + [(sq2.name, fi_, b_) for fi_ in range(2) for b_ in range(8)])
            for fi in range(2):
                for blk in range(L // 512):
                    ps = bank[nb()]
                    S.op("pe", "matmul", ps[:], lhsT=w4[:, fi * D + ch * 128:fi * D + (ch + 1) * 128], rhs=h3[:, fi * L + blk * 512:fi * L + (blk + 1) * 512],
                         start=True, stop=True, reads=[w4, h3], writes=[ps])
                    sqv = sq2[:, fi, blk * 512:(blk + 1) * 512]
                    S.op("dve", "scalar_tensor_tensor", out=sqv, in0=ps[:], scalar=E1[:, fi, blk:blk + 1], in1=E2[:, fi, :],
                         op0=ALU.mult, op1=ALU.mult, reads=[ps, E1, E2], writes=[(sq2.name, fi, blk)])
                    if fi == 1 and blk == 0:
                        S.op("dve", "memset", sq2[:, 1, 0:1], 0.0, writes=[(sq2.name, fi, blk)])
                    S.op("act", "activation", out=g.junk[:, 0:512], in_=sqv, func=AF.Abs,
                         accum_out=asum[:, fi * 8 + blk:fi * 8 + blk + 1], reads=[(sq2.name, fi, blk), asum], writes=[asum])
            S.op("pool", "memset", asum[:, 16:17], 0.0, reads=[(sq2.name, fi_, b_) for fi_ in range(2) for b_ in range(8)], writes=[sq2])
            S.op("dve", "reduce_sum", out=nrm[:], in_=asum[:, 0:16], axis=AX.X, reads=[asum], writes=[nrm])
            S.op("dve", "tensor_scalar", out=nrm[:], in0=nrm[:], scalar1=EPS, scalar2=None, op0=ALU.add, reads=[nrm], writes=[nrm])
            S.op("dve", "reciprocal", out=nrm[:], in_=nrm[:], reads=[nrm], writes=[nrm])

            def cons_f(bk, f0, n):
                bv = bk[:, 0:n * 128].rearrange("p (f k) -> p f k", k=128)
                S.op("act", "activation", out=Kf[:, f0:f0 + n, 0:64], in_=bv[:, :, 0:64], func=AF.Identity, scale=nrm[:, 0:1],
                     bias=hbias[:, ch:ch + 1], reads=[bk, nrm, hbias], writes=[Kf])
                S.op("act", "activation", out=Kf[:, f0:f0 + n, 64:128], in_=bv[:, :, 64:128], func=AF.Identity, scale=nrm[:, 0:1],
                     reads=[bk, nrm], writes=[Kf])

            def cons_b(bk, f0, n):
                bv = bk[:, 0:n * 128].rearrange("p (f k) -> p f k", k=128)
                S.op("dve", "scalar_tensor_tensor", out=Kf[:, f0:f0 + n, 0:64], in0=bv[:, :, 0:64], scalar=nrm[:, 0:1],
                     in1=Kf[:, f0:f0 + n, 0:64], op0=ALU.mult, op1=ALU.add, reads=[bk, nrm, Kf], writes=[Kf])
                S.op("dve", "scalar_tensor_tensor", out=Kf[:, f0:f0 + n, 64:128], in0=bv[:, :, 64:128], scalar=nrm[:, 0:1],
                     in1=Kf[:, f0:f0 + n, 64:128], op0=ALU.mult, op1=ALU.subtract, reads=[bk, nrm, Kf], writes=[Kf])
                S.op("pool", "tensor_scalar", out=Kf[:, f0:f0 + n, 64:128], in0=Kf[:, f0:f0 + n, 64:128], scalar1=-1.0,
                     scalar2=None, op0=ALU.mult, reads=[Kf], writes=[Kf])

            forward(True, cons_f)
            S.dma("sp", sq2[:, 0, :], d.u_d[:, ch, :], writes=[sq2])
            S.dma("sp", x0[:], d.x0_d[:, ch, :], writes=[x0])
            pc = [0]

            def cons_u(bk, f0, n):
                p1, p2 = P1[pc[0] % 2], P2[0]
                pc[0] += 1
                bv = bk[:, 0:n * 128].rearrange("p (f r k) -> p f r k", r=2, k=64)
                kr = Kf[:, f0:f0 + n, 0:64].unsqueeze(2).broadcast_to([128, n, 2, 64])
                ki = Kf[:, f0:f0 + n, 64:128].unsqueeze(2).broadcast_to([128, n, 2, 64])
                p1v = p1[:, 0:n, :].rearrange("p f (r k) -> p f r k", r=2)
                p2v = p2[:, 0:n, :].rearrange("p f (r k) -> p f r k", r=2)
                S.op("dve", "tensor_tensor", out=p1v, in0=bv, in1=kr, op=ALU.mult, reads=[bk, Kf], writes=[p1])
                S.op("dve", "tensor_tensor", out=p2v, in0=bv, in1=ki, op=ALU.mult, reads=[bk, Kf], writes=[p2])
                S.op("pool", "tensor_tensor", out=Y[:, f0:f0 + n, 0:64], in0=p1[:, 0:n, 0:64], in1=p2[:, 0:n, 64:128],
                     op=ALU.subtract, reads=[p1, p2], writes=[buf2])
                S.op("pool", "tensor_tensor", out=Y[:, f0:f0 + n, 64:128], in0=p2[:, 0:n, 0:64], in1=p1[:, 0:n, 64:128],
                     op=ALU.add, reads=[p1, p2], writes=[buf2])

            forward(False, cons_u)
            for f0 in range(0, 65, 8):
                n = min(8, 65 - f0)
                bi = nb()
                for u in range(n):
                    S.op("pe", "transpose", out=bankb[bi][:, u * 128:(u + 1) * 128], in_=Y[:, f0 + u, :], identity=g.identb[:],
                         reads=[buf2], writes=[bank[bi]])
                evac(YT[:, f0:f0 + n, :], bankb[bi][:, 0:n * 128].rearrange("p (f c) -> p f c", c=128), [bank[bi]], [buf1])
            for f0 in range(0, 65, 4):
                n = min(4, 65 - f0)
                bi = nb()
                for u in range(n):
                    S.op("pe", "matmul", bank[bi][:, u * 128:(u + 1) * 128], lhsT=tabBp[:, f0 + u, :], rhs=YT[:, f0 + u, :],
                         start=True, stop=True, reads=[tabBp, buf1], writes=[bank[bi]])
                evac(Q[:, f0:f0 + n, :], bank[bi][:, 0:n * 128].rearrange("p (f c) -> p f c", c=128), [bank[bi]], [buf2])
            for c0 in range(0, 128, 8):
                bi = nb()
                for u in range(8):
                    S.op("pe", "transpose", out=bankb[bi][0:65, u * 128:(u + 1) * 128], in_=Q[:, :, c0 + u], identity=g.identb[:],
                         reads=[buf2], writes=[bank[bi]])
                evac(QT[:, c0:c0 + 8, :], bankb[bi][0:65, 0:1024].rearrange("p (c a) -> p c a", a=128), [bank[bi]], [buf3])
            x0v = x0[:, :].rearrange("p (b a) -> p a b", a=64)
            hyv = hyo.rearrange("p (b a) -> p a b", a=64)
            for a0 in range(0, 64, 8):
                bi = nb()
                for u in range(8):
                    a = a0 + u
                    S.op("pe", "matmul", bank[bi][:, u * 64:(u + 1) * 64], lhsT=QT[:, :, a], rhs=tabAp[:, 0:64],
                         start=True, stop=False, reads=[buf3, tabAp], writes=[bank[bi]])
                    S.op("pe", "matmul", bank[bi][:, u * 64:(u + 1) * 64], lhsT=QT[:, :, 64 + a], rhs=tabAp[:, 64:128],
                         start=False, stop=True, reads=[buf3, tabAp], writes=[bank[bi]])
                S.op("dve", "tensor_tensor", out=hyv[:, a0:a0 + 8, :], in0=bank[bi][:].rearrange("p (a b) -> p a b", b=64),
                     in1=x0v[:, a0:a0 + 8, :], op=ALU.mult, reads=[bank[bi], x0], writes=[sq2])
            S.dma("sp", d.hyT_d[:, ch, :], hyo, reads=[sq2], writes=[("hyT_d", ch)])
        S.phase_end()


def attention_phase(S, nc, g, d):
    with contextlib.ExitStack() as st:
        def T(name, shape, dt):
            return st.enter_context(nc.sbuf_tensor("at" + name, shape, dt))

        def P(name):
            return S.reg_psum(st.enter_context(nc.psum_tensor("at" + name, [128, 512], F32)))

        hT = T("hT", [128, 8, L], BF16)
        hcT = T("hcT", [128, 8, LC], BF16)
        for kc in range(8):
            S.dma("sp", hT[:, kc, :], d.hT_d[:, kc, :], writes=[hT])
        S.dma("sp", hcT[:], d.hcT_d, writes=[hcT])
        if HYENA:
            hyena_proj(S, nc, g, d, hT, st)
        ropeC = T("ropeC", [128, L], BF16)
        ropeS = T("ropeS", [128, L], BF16)
        S.dma("pool", ropeC[:], d.ropec_d, writes=[ropeC])
        S.dma("pool", ropeS[:], d.ropes_d, writes=[ropeS])
        Rm = T("Rm", [128, 128], BF16)
        S.dma("pool", Rm[:], d.rotm_d, writes=[Rm])
        lq = T("lq", [128, 4, 64], F32)
        for i, ap in enumerate([d.lq1_d, d.lk1_d, d.lq2_d, d.lk2_d]):
            S.dma("sp", lq[:, i, :], ap.broadcast_to([128, 64]), writes=[lq])
        lprod = T("lprod", [128, 2, 64], F32)
        S.op("dve", "tensor_tensor", out=lprod[:, 0, :], in0=lq[:, 0, :], in1=lq[:, 1, :], op=ALU.mult,
             reads=[lq], writes=[lprod])
        S.op("dve", "tensor_tensor", out=lprod[:, 1, :], in0=lq[:, 2, :], in1=lq[:, 3, :], op=ALU.mult,
             reads=[lq], writes=[lprod])
        lsum = T("lsum", [128, 2], F32)
        S.op("dve", "reduce_sum", out=lsum[:], in_=lprod[:], axis=AX.X, reads=[lprod], writes=[lsum])
        S.op("act", "activation", out=lsum[:], in_=lsum[:], func=AF.Exp, reads=[lsum], writes=[lsum])
        nlam = T("nlam", [128, 1], F32)
        S.op("dve", "scalar_tensor_tensor", out=nlam[:], in0=lsum[:, 1:2], scalar=-LAM_INIT, in1=lsum[:, 0:1],
             op0=ALU.add, op1=ALU.subtract, reads=[lsum], writes=[nlam])

        wq = [T("wq%d" % i, [128, 8, 128], BF16) for i in range(2)]
        wk = [T("wk%d" % i, [128, 8, 128], BF16) for i in range(2)]
        wv = [T("wv%d" % i, [128, 8, 256], BF16) for i in range(2)]
        qT = T("qT", [128, L], BF16)
        kT = T("kT", [128, LK], BF16)
        NKT = LK // 128
        vaug = T("vaug", [128, NKT, 2, 132], BF16)
        S.op("pool", "memset", vaug[:], 1.0, writes=[vaug])
        attnT = [T("attnT%d" % i, [128, L], BF16) for i in range(2)]
        qsb = [T("qsb%d" % i, [128, 512], BF16) for i in range(2)]
        t1 = [T("t1_%d" % i, [128, 512], F32) for i in range(2)]
        t2 = [T("t2_%d" % i, [128, 512], F32) for i in range(2)]
        pT2 = [[T("pT%d_%d" % (m, i), [128, 512], BF16) for i in range(3)] for m in range(2)]
        ppair = [T("ppair%d" % m, [128, 512], BF16) for m in range(2)]
        pprev = [None, None]
        acc = [[T("acc%d_%d" % (m, i), [128, 512], F32) for i in range(1)] for m in range(2)]
        accb = T("accb", [128, 512], BF16)
        rr = T("rr", [128, 512], F32)
        om = [T("om%d" % m, [128, 512], F32) for m in range(2)]
        of_ = T("of", [128, 512], F32)
        psS2 = [[P("psS%d_%d" % (m, i)) for i in range(2)] for m in range(2)]
        OT = [P("OT%d" % m) for m in range(2)]
        psA = P("psA")
        psB = P("psB")
        w_in = d.w_in_d

        def load_w(h):
            i = h % 2
            for (w, c0) in ((wq[i], 3072), (wk[i], 4096)):
                S.dma("pool", w[:], w_in[:, c0 + h * 128:c0 + (h + 1) * 128].rearrange("(kc p) n -> p kc n", p=128),
                      writes=[w])
            if h % 2 == 0:
                w = wv[(h // 2) % 2]
                S.dma("pool", w[:], w_in[:, 5120 + h * 128:5120 + (h + 2) * 128].rearrange("(kc p) n -> p kc n", p=128),
                      writes=[w])

        load_w(0)
        ei = 0
        for h in range(ATH if STAGE >= 3 else 0):
            if h + 1 < ATH:
                load_w(h + 1)
            i2 = h % 2
            for (w, dstT) in ((wq[i2], qT), (wk[i2], kT)):
                for blk in range(L // 512 if ASUB >= 2 else 0):
                    cs = slice(blk * 512, (blk + 1) * 512)
                    for kc in range(8):
                        S.op("pe", "matmul", psA[:], lhsT=w[:, kc, :], rhs=hT[:, kc, cs], start=(kc == 0), stop=(kc == 7),
                             reads=[w, hT], writes=[psA])
                    qs_, ta, tb = qsb[ei % 2], t1[ei % 2], t2[ei % 2]
                    ei += 1
                    S.op("act", "copy", out=qs_[:], in_=psA[:], reads=[psA], writes=[qs_])
                    S.op("pe", "matmul", psB[:], lhsT=Rm[:], rhs=qs_[:], start=True, stop=True, reads=[Rm, qs_],
                         writes=[psB])
                    S.op("dve", "tensor_tensor", out=ta[:], in0=psA[:], in1=ropeC[:, cs], op=ALU.mult,
                         reads=[psA, ropeC], writes=[ta])
                    S.op("dve", "tensor_tensor", out=tb[:], in0=psB[:], in1=ropeS[:, cs], op=ALU.mult,
                         reads=[psB, ropeS], writes=[tb])
                    S.op("pool", "tensor_tensor", out=dstT[:, cs], in0=ta[:], in1=tb[:], op=ALU.add,
                         reads=[ta, tb], writes=[dstT])
            if ASUB < 3:
                continue
            for kc in range(8):
                S.op("pe", "matmul", psA[:, 0:LC], lhsT=wk[i2][:, kc, :], rhs=hcT[:, kc, :], start=(kc == 0), stop=(kc == 7),
                     reads=[wk[i2], hcT], writes=[psA])
            S.op("act", "copy", out=kT[:, L:LK], in_=psA[:, 0:LC], reads=[psA], writes=[kT])
            if h % 2 == 0:
                wvp = wv[(h // 2) % 2]
                for tp in range(0, NKT, 2):
                    for u in range(2):
                        kt = tp + u
                        src = hT[:, :, kt * 128:(kt + 1) * 128] if kt < L // 128 else hcT[:, :, (kt - L // 128) * 128:(kt - L // 128 + 1) * 128]
                        for kc in range(8):
                            S.op("pe", "matmul", psA[:, u * 256:(u + 1) * 256], lhsT=src[:, kc, :], rhs=wvp[:, kc, :],
                                 start=(kc == 0), stop=(kc == 7), reads=[hT, hcT, wvp], writes=[psA])
                    S.op("act", "copy", out=vaug[:, tp:tp + 2, :, 0:128],
                         in_=psA[:, :].rearrange("p (u hh e) -> p u hh e", hh=2, e=128), reads=[psA], writes=[vaug])
            vh = h % 2
            if ASUB < 4:
                continue
            aT = attnT[i2]
            for qb in range(L // 512):
                qcs = slice(qb * 512, (qb + 1) * 512)

                def qk(kt):
                    for m in range(2):
                        ms = slice(m * 64, (m + 1) * 64)
                        ps = psS2[m][kt % 2]
                        S.op("pe", "matmul", ps[:], lhsT=kT[ms, kt * 128:(kt + 1) * 128], rhs=qT[ms, qcs],
                             start=True, stop=True, reads=[kT, qT], writes=[ps])
                qk(0)
                for kt in range(NKT):
                    if kt + 1 < NKT:
                        qk(kt + 1)
                    for m in range(2):
                        p_ = pT2[m][kt % 3]
                        ps = psS2[m][kt % 2]
                        S.op("act", "activation", out=p_[:], in_=ps[:], func=AF.Exp, scale=0.125, reads=[ps], writes=[p_])
                        S.op("pe", "matmul", OT[m][:], lhsT=vaug[:, kt, vh, 0:128], rhs=p_[:], start=(kt == 0), stop=(kt == NKT - 1),
                             reads=[vaug, p_], writes=[OT[m]])
                        if kt % 2 == 1:
                            pr = ppair[m]
                            S.op("dve", "tensor_tensor", out=pr[:], in0=pprev[m][:], in1=p_[:], op=ALU.add,
                                 reads=[pprev[m], p_], writes=[pr])
                        pprev[m] = p_
                    if kt % 2 == 1:
                        for m in range(2):
                            pr = ppair[m]
                            if kt == 1:
                                S.op("dve", "tensor_copy", out=acc[m][0][:], in_=pr[:], reads=[pr], writes=[acc[m][0]])
                            else:
                                S.op("dve", "tensor_tensor", out=acc[m][0][:], in0=acc[m][0][:], in1=pr[:], op=ALU.add,
                                     reads=[acc[m][0], pr], writes=[acc[m][0]])
                for m in range(2):
                    S.op("dve", "tensor_copy", out=accb[:], in_=acc[m][0][:], reads=[acc[m][0]], writes=[accb])
                    S.op("pe", "matmul", psA[:], lhsT=g.onesbb[:], rhs=accb[:], start=True, stop=True, reads=[accb], writes=[psA])
                    S.op("act", "activation", out=rr[:], in_=psA[:], func=AF.Ln, reads=[psA], writes=[rr])
                    S.op("act", "activation", out=rr[:], in_=rr[:], func=AF.Exp, scale=-1.0, reads=[rr], writes=[rr])
                    S.op("dve", "tensor_tensor", out=om[m][:], in0=OT[m][:], in1=rr[:], op=ALU.mult, reads=[OT[m], rr],
                         writes=[om[m]])
                S.op("dve", "scalar_tensor_tensor", out=of_[:], in0=om[1][:], scalar=nlam[:, 0:1], in1=om[0][:], op0=ALU.mult,
                     op1=ALU.add, reads=[om[0], om[1], nlam], writes=[of_])
                S.op("act", "activation", out=accb[:], in_=of_[:], func=AF.Square, reads=[of_], writes=[accb])
                S.op("pe", "matmul", psA[:], lhsT=g.onesbb[:], rhs=accb[:], start=True, stop=True, reads=[accb], writes=[psA])
                S.op("act", "activation", out=rr[:], in_=psA[:], func=AF.Ln, scale=1.0 / 128, bias=g.eps_t[:, 0:1],
                     reads=[psA], writes=[rr])
                S.op("act", "activation", out=rr[:], in_=rr[:], func=AF.Exp, scale=-0.5, reads=[rr], writes=[rr])
                S.op("dve", "tensor_tensor", out=aT[:, qcs], in0=of_[:], in1=rr[:], op=ALU.mult, reads=[of_, rr], writes=[aT])
            S.dma("sp", d.attnT_d[:, h, :], aT[:], reads=[aT], writes=[("attnT_d", h)])
        S.phase_end()


def mixer_out_phase(S, nc, g, d):
    with contextlib.ExitStack() as st:
        def T(name, shape, dt):
            return st.enter_context(nc.sbuf_tensor("mo" + name, shape, dt))

        def P(name):
            return S.reg_psum(st.enter_context(nc.psum_tensor("mo" + name, [128, 512], F32)))

        wg = T("wg", [128, 8, 2048], BF16)
        why = T("why", [128, 8, D], BF16)
        wda = T("wda", [128, 8, D], BF16)
        wo_ = T("wo", [128, 8, D], BF16)
        wv_ = d.w_in_d.rearrange("(kc p) n -> p kc n", p=128)
        for kc in range(0, 8, 2):
            S.dma("pool", wg[:, kc:kc + 2, :], wv_[:, kc:kc + 2, 6144:8192], writes=[wg])
        for (w, ap) in ((why, d.w_hy_out_d), (wda, d.w_da_out_d), (wo_, d.w_o_d)):
            S.dma("pool", w[:], ap.rearrange("(kc p) n -> p kc n", p=128), writes=[w])
        sg_ = T("sg", [128, 1], F32)
        S.dma("sp", sg_[:], d.subln_g_d.rearrange("o p -> p o"), writes=[sg_], allow_slow_non_contiguous=True)
        S.op("dve", "tensor_scalar", out=why[:], in0=why[:], scalar1=0.5, scalar2=None, op0=ALU.mult,
             reads=[why], writes=[why])
        S.op("dve", "tensor_scalar", out=wda[:], in0=wda[:], scalar1=sg_[:, 0:1], scalar2=0.5 * (1.0 - LAM_INIT),
             op0=ALU.mult, op1=ALU.mult, reads=[wda, sg_], writes=[wda])
        TB = 256
        hTb = [T("hTb%d" % i, [128, 8, TB], BF16) for i in range(2)]
        hyb = [T("hyb%d" % i, [128, 8, TB], BF16) for i in range(2)]
        atb = [T("atb%d" % i, [128, 8, TB], BF16) for i in range(2)]
        mT = T("mT", [128, 8, TB], BF16)
        th = [T("th%d" % i, [128, 2 * TB], F32) for i in range(2)]
        mm = [T("mm%d" % i, [128, 2 * TB], F32) for i in range(2)]
        xres = [T("xres%d" % i, [128, D], F32) for i in range(2)]
        ytmp = [T("yt%d" % i, [128, D], F32) for i in range(2)]
        ss2 = [T("ss2_%d" % i, [128, 2], F32) for i in range(2)]
        rs2 = [T("rs2_%d" % i, [128, 1], F32) for i in range(2)]
        psY = [P("psY%d" % i) for i in range(2)]
        psG = [P("psG%d" % i) for i in range(2)]
        pso = [S.reg_psum(st.enter_context(nc.psum_tensor("mopso%d" % i, [128, D], F32))) for i in range(2)]
        ci = 0
        for b in range(L // TB):
            cs = slice(b * TB, (b + 1) * TB)
            hb, ab, hT = hyb[b % 2], atb[b % 2], hTb[b % 2]
            S.dma("sp", hT[:], d.hT_d[:, :, cs], writes=[hT])
            S.dma("sp", hb[:], d.hyT_d[:, :, cs], writes=[hb])
            S.dma("sp", ab[:], d.attnT_d[:, :, cs], writes=[ab])
            for dc in range(8):
                py, pg = psY[ci % 2], psG[ci % 2]
                th_, mm_ = th[ci % 2], mm[ci % 2]
                ci += 1
                dcs = slice(dc * 128, (dc + 1) * 128)
                for c in range(8):
                    S.op("pe", "matmul", py[:, 0:TB], lhsT=why[:, c, dcs], rhs=hb[:, c, :], start=(c == 0), stop=(c == 7),
                         reads=[why, hb], writes=[py])
                for c in range(8):
                    S.op("pe", "matmul", py[:, TB:2 * TB], lhsT=wda[:, c, dcs], rhs=ab[:, c, :], start=(c == 0), stop=(c == 7),
                         reads=[wda, ab], writes=[py])
                for half in range(2):
                    for kc in range(8):
                        S.op("pe", "matmul", pg[:, half * TB:(half + 1) * TB],
                             lhsT=wg[:, kc, half * 1024 + dc * 128:half * 1024 + (dc + 1) * 128], rhs=hT[:, kc, :],
                             start=(kc == 0), stop=(kc == 7), reads=[wg, hT], writes=[pg])
                S.op("act", "activation", out=th_[:], in_=pg[:], func=AF.Tanh, scale=0.5, reads=[pg], writes=[th_])
                S.op("dve", "scalar_tensor_tensor", out=mm_[:], in0=th_[:], scalar=1.0, in1=py[:], op0=ALU.add, op1=ALU.mult,
                     reads=[th_, py], writes=[mm_])
                S.op("pool", "tensor_tensor", out=mT[:, dc, :], in0=mm_[:, 0:TB], in1=mm_[:, TB:2 * TB], op=ALU.add,
                     reads=[mm_], writes=[mT])
            for ti in range(TB // 128):
                r0 = b * TB + ti * 128
                po = pso[ti % 2]
                for half in range(2):
                    for dc in range(8):
                        S.op("pe", "matmul", po[:, half * 512:(half + 1) * 512], lhsT=mT[:, dc, ti * 128:(ti + 1) * 128],
                             rhs=wo_[:, dc, half * 512:(half + 1) * 512], start=(dc == 0), stop=(dc == 7),
                             reads=[mT, wo_], writes=[po])
                xr = xres[ti % 2]
                S.dma("sp", xr[:], d.x1_d[r0:r0 + 128, :], writes=[xr])
                post_norm_residual(S, g, "mo", po, g.Gx[1], xr, ytmp[ti % 2], ss2[ti % 2], rs2[ti % 2],
                                   d.x2_d[r0:r0 + 128, :], ("x2_d", r0))
        S.phase_end()


def build(debug=False):
    nc = bass.Bass("TRN2", target_bir_lowering=False)

    def din(name, shape):
        return nc.dram_tensor(name, shape, F32, kind="ExternalInput").ap()

    x_d = din("x", [L, D])
    c_d = din("c", [1, D])
    ctx_d = din("ctx", [LC, D])
    cctx_d = din("c_ctx", [1, D])
    w_ada_d = din("w_ada", [D, 9 * D])
    b_ada_d = din("b_ada", [1, 9 * D])
    norm_g_d = din("norm_g", [6, D])
    w_ff_in_d = din("w_ff_in", [2, D, 2 * DFF])
    w_ff_out_d = din("w_ff_out", [2, DFF, D])
    w_in_d = din("w_in", [D, NPROJ])
    ident_d = din("ident", [128, 128])
    d = Ctx()
    d.w_in_d = w_in_d
    d.lq1_d = din("lambda_q1", [1, 64]); d.lk1_d = din("lambda_k1", [1, 64])
    d.lq2_d = din("lambda_q2", [1, 64]); d.lk2_d = din("lambda_k2", [1, 64])
    d.subln_g_d = din("subln_g", [1, 128])
    d.w_hy_out_d = din("w_hy_out", [D, D]); d.w_da_out_d = din("w_da_out", [D, D]); d.w_o_d = din("w_o", [D, D])
    d.hy_conv_w_d = din("hy_conv_w", [3, 3 * D]); d.hy_conv_b_d = din("hy_conv_b", [1, 3 * D])
    d.filt_w1_d = din("filt_w1", [33, 64]); d.filt_b1_d = din("filt_b1", [1, 64])
    d.filt_w2_d = din("filt_w2", [64, 64]); d.filt_b2_d = din("filt_b2", [1, 64])
    d.filt_w3_d = din("filt_w3", [64, 64]); d.filt_b3_d = din("filt_b3", [1, 64])
    d.filt_w4_d = din("filt_w4", [64, 2 * D]); d.filt_freq_d = din("filt_freq", [1, 64])
    d.hy_bias_d = din("hy_bias", [1, D])
    d.tabA_d = din("tabA", [128, 130]); d.tabB_d = din("tabB", [64, 65, 256])
    d.tabBp_d = din("tabBp", [128, 65, 128]); d.tabAp_d = din("tabAp", [65, 128])
    d.zemb_d = din("zemb", [33, 2 * L]); d.deltas_d = din("deltas", [128, 8]); d.tlin1_d = din("tlin1", [2, 8]); d.tlin2_d = din("tlin2", [2, 512])
    d.ropec_d = din("ropec", [128, L]); d.ropes_d = din("ropes", [128, L]); d.rotm_d = din("rotm", [128, 128])
    out_d = nc.dram_tensor("out", [L, D], F32, kind="ExternalOutput").ap()
    sk = "ExternalOutput" if debug else "Internal"
    x1_d = nc.dram_tensor("x1_s", [L, D], F32, kind=sk).ap()
    c1_d = nc.dram_tensor("c1_s", [LC, D], F32, kind=sk).ap()
    hT_d = nc.dram_tensor("hT_s", [128, 8, L], BF16, kind=sk).ap()
    hcT_d = nc.dram_tensor("hcT_s", [128, 8, LC], BF16, kind=sk).ap()
    d.hT_d, d.hcT_d, d.x1_d = hT_d, hcT_d, x1_d
    d.attnT_d = nc.dram_tensor("attnT_s", [128, 8, L], BF16, kind=sk).ap()
    d.hyT_d = nc.dram_tensor("hyT_s", [128, 8, L], BF16, kind=sk).ap()
    d.x2_d = nc.dram_tensor("x2_s", [L, D], F32, kind=sk).ap()
    d.u_d = nc.dram_tensor("u_s", [128, 8, L], BF16, kind=sk).ap()
    d.x0_d = nc.dram_tensor("x0_s", [128, 8, L], BF16, kind=sk).ap()

    with nc.cleanup_on_exit(), contextlib.ExitStack() as gst:
        S = Sched(nc)
        g = Ctx()

        def GT(name, shape, dt):
            return gst.enter_context(nc.sbuf_tensor(name, shape, dt))

        g.identb = GT("identb", [128, 128], BF16)
        g.eps_t = GT("eps_t", [128, 1], F32)
        g.onesb = GT("onesb", [1, 128], BF16)
        g.onesbb = GT("onesbb", [128, 128], BF16)
        g.junk = GT("junk", [128, D], BF16)
        g.Ax = [GT("Ax%d" % i, [128, 8], F32) for i in range(3)]
        g.Bx = [GT("Bx%d" % i, [128, 8], F32) for i in range(3)]
        g.Ac = [GT("Ac%d" % i, [128, 8], F32) for i in range(2)]
        g.Bc = [GT("Bc%d" % i, [128, 8], F32) for i in range(2)]
        g.Gx = [GT("Gx%d" % i, [128, D], F32) for i in range(3)]
        g.Gc = GT("Gc0", [128, D], F32)

        st_f1w = contextlib.ExitStack()
        f1w = ffn_alloc_wi(S, nc, st_f1w, "f1", w_ff_in_d[0], bg=True)

        with contextlib.ExitStack() as st:
            def T(name, shape, dt):
                return st.enter_context(nc.sbuf_tensor("p0" + name, shape, dt))

            identf = T("identf", [128, 128], F32)
            S.dma("sp", identf[:], ident_d, writes=[identf])
            S.op("dve", "tensor_copy", out=g.identb[:], in_=identf[:], reads=[identf], writes=[g.identb])
            S.op("dve", "memset", g.eps_t[:], EPS, writes=[g.eps_t])
            S.op("dve", "memset", g.onesb[:], 1.0, writes=[g.onesb])
            S.op("dve", "memset", g.onesbb[:], 1.0, writes=[g.onesbb])
            craw = T("craw", [128, 2, 8], F32)
            S.dma("sp", craw[:, 0, :], c_d.rearrange("o (kc p) -> p (o kc)", p=128), writes=[craw],
                  allow_slow_non_contiguous=True)
            S.dma("sp", craw[:, 1, :], cctx_d.rearrange("o (kc p) -> p (o kc)", p=128), writes=[craw],
                  allow_slow_non_contiguous=True)
            csil = T("csil", [128, 2, 8], F32)
            S.op("act", "activation", out=csil[:], in_=craw[:], func=AF.Silu, reads=[craw], writes=[csil])
            sT = T("sT", [128, 8, 2], BF16)
            for j in range(2):
                S.op("dve", "tensor_copy", out=sT[:, :, j], in_=csil[:, j, :], reads=[csil], writes=[sT])
            sbc = T("sbc", [128, 2, 8, 128], BF16)
            for j in range(2):
                for kc in range(8):
                    S.op("dve", "tensor_copy", out=sbc[:, j, kc, :], in_=csil[:, j, kc:kc + 1].broadcast_to([128, 128]),
                         reads=[csil], writes=[sbc])
            badaf = [T("badaf%d" % i, [1, D], F32) for i in range(2)]
            badab = [T("badab%d" % i, [1, D], BF16) for i in range(2)]
            gpre = T("gpre", [128, 3, 8], F32)
            for i in range(3):
                S.dma("sp", gpre[:, i, :], norm_g_d[2 * i:2 * i + 1, :].rearrange("o (kc p) -> p (o kc)", p=128),
                      writes=[gpre], allow_slow_non_contiguous=True)
            gpost = [T("gpost%d" % i, [128, D], F32) for i in range(3)]
            for i in range(3):
                S.dma("sp", gpost[i][:], norm_g_d[2 * i + 1:2 * i + 2, :].broadcast_to([128, D]), writes=[gpost[i]])
            wa = [T("wa%d" % i, [128, 8, D], BF16) for i in range(2)]
            waf = [T("waf%d" % i, [128, 2, D], F32) for i in range(3)]
            wfi = [0]
            modp = S.reg_psum(st.enter_context(nc.psum_tensor("p0modp", [128, 8, 2], F32)))
            psG = [S.reg_psum(st.enter_context(nc.psum_tensor("p0psG%d" % i, [128, 512], F32))) for i in range(2)]
            wav = w_ada_d.rearrange("(kc p) n -> p kc n", p=128)
            gi = 0
            for m in range(9):
                w = wa[m % 2]
                bada = badab[m % 2]
                S.dma("sp", badaf[m % 2][:], b_ada_d[0:1, m * D:(m + 1) * D], writes=[badaf[m % 2]])
                S.op("dve", "tensor_copy", out=bada[:], in_=badaf[m % 2][:], reads=[badaf[m % 2]], writes=[bada])
                for kc in range(0, 8, 2):
                    wf = waf[wfi[0] % 3]
                    S.dma("sp", wf[:], wav[:, kc:kc + 2, m * D:(m + 1) * D], writes=[wf])
                    if wfi[0] % 2 == 0:
                        S.op("dve", "tensor_copy", out=w[:, kc:kc + 2, :], in_=wf[:], reads=[wf], writes=[(w.name, kc)])
                    else:
                        S.op("act", "copy", out=w[:, kc:kc + 2, :], in_=wf[:], reads=[wf], writes=[(w.name, kc)])
                    wfi[0] += 1
                grp, which = m // 3, m % 3
                if which < 2:
                    for j in range(8):
                        for kc in range(8):
                            S.op("pe", "matmul", modp[:, j, :], lhsT=w[:, kc, j * 128:(j + 1) * 128], rhs=sT[:, kc, :],
                                 start=(kc == 0), stop=False, reads=[(w.name, (kc // 2) * 2), sT], writes=[modp])
                        S.op("pe", "matmul", modp[:, j, :], lhsT=bada[0:1, j * 128:(j + 1) * 128],
                             rhs=g.onesb[0:1, 0:2], start=False, stop=True, reads=[bada, g.onesb], writes=[modp])
                    if which == 0:
                        S.op("dve", "tensor_copy", out=g.Bx[grp][:], in_=modp[:, :, 0], reads=[modp], writes=[g.Bx[grp]])
                        if grp < 2:
                            S.op("dve", "tensor_copy", out=g.Bc[grp][:], in_=modp[:, :, 1], reads=[modp],
                                 writes=[g.Bc[grp]])
                    else:
                        S.op("dve", "scalar_tensor_tensor", out=g.Ax[grp][:], in0=modp[:, :, 0], scalar=1.0,
                             in1=gpre[:, grp, :], op0=ALU.add, op1=ALU.mult, reads=[modp, gpre], writes=[g.Ax[grp]])
                        if grp < 2:
                            S.op("dve", "scalar_tensor_tensor", out=g.Ac[grp][:], in0=modp[:, :, 1], scalar=1.0,
                                 in1=gpre[:, grp, :], op0=ALU.add, op1=ALU.mult, reads=[modp, gpre],
                                 writes=[g.Ac[grp]])
                else:
                    fac = 1.0 if grp == 1 else 0.5
                    targets = [(0, g.Gx[grp])] + ([(1, g.Gc)] if grp == 0 else [])
                    for (j, Gdst) in targets:
                        for half in range(2):
                            pg = psG[gi % 2]
                            gi += 1
                            for kc in range(8):
                                S.op("pe", "matmul", pg[:], lhsT=sbc[:, j, kc, :], rhs=w[:, kc, half * 512:(half + 1) * 512],
                                     start=(kc == 0), stop=False, reads=[(w.name, (kc // 2) * 2), sbc], writes=[pg])
                            S.op("pe", "matmul", pg[:], lhsT=g.onesb[0:1, :],
                                 rhs=bada[0:1, half * 512:(half + 1) * 512],
                                 start=False, stop=True, reads=[bada, g.onesb], writes=[pg])
                            S.op("dve", "scalar_tensor_tensor", out=Gdst[:, half * 512:(half + 1) * 512], in0=pg[:],
                                 scalar=fac, in1=gpost[grp][:, half * 512:(half + 1) * 512], op0=ALU.mult, op1=ALU.mult,
                                 reads=[pg, gpost[grp]], writes=[Gdst])
            S.phase_end(skip_bg=True)

        if debug:
            dbg_d = nc.dram_tensor("dbg", [128, 10, 8], F32, kind="ExternalOutput").ap()
            dbgG_d = nc.dram_tensor("dbgG", [4, 128, D], F32, kind="ExternalOutput").ap()
            for i, t in enumerate(g.Ax + g.Bx + g.Ac + g.Bc):
                S.dma("sp", dbg_d[:, i, :], t[:], reads=[t], writes=[("dbg", i)])
            for i, t in enumerate(g.Gx + [g.Gc]):
                S.dma("sp", dbgG_d[i], t[:], reads=[t], writes=[("dbgG", i)])
        if STAGE >= 1:
          ffn_phase(S, nc, g, "f1", w_ff_in_d[0], w_ff_out_d[0], preloaded=f1w, streams=[
            dict(T=LC, src=ctx_d, dst=c1_d, A=g.Ac[0], B=g.Bc[0], G=g.Gc,
                 nxt=dict(A=g.Ac[1], B=g.Bc[1], dst=hcT_d)),
        ] + ([dict(T=L, src=x_d, dst=x1_d, A=g.Ax[0], B=g.Bx[0], G=g.Gx[0],
                 nxt=dict(A=g.Ax[1], B=g.Bx[1], dst=hT_d))] if STAGE >= 2 else []))

        st_f1w.close()

        if STAGE >= 3:
            if not HYENA:
                with contextlib.ExitStack() as st:
                    z = st.enter_context(nc.sbuf_tensor("zt", [128, 8, 512], BF16))
                    S.op("dve", "memset", z[:], 0.0, writes=[z])
                    for i in range(L // 512):
                        S.dma("sp", d.hyT_d[:, :, i * 512:(i + 1) * 512], z[:], reads=[z], writes=[("hyz", i)])
                    S.phase_end()
            attention_phase(S, nc, g, d)
            if HYENA:
                hyena_fft_phase(S, nc, g, d)
        if STAGE >= 4:
            mixer_out_phase(S, nc, g, d)
        if STAGE >= 5:
            ffn_phase(S, nc, g, "f2", w_ff_in_d[1], w_ff_out_d[1], [
                dict(T=L, src=d.x2_d, dst=out_d, A=g.Ax[2], B=g.Bx[2], G=g.Gx[2], nxt=None)])
        nc.all_engine_barrier()
        print("instructions", S.n_ins, "waits", S.n_wait)
    return nc


_NC_CACHE = {}


def _host_consts():
    p = np.arange(128)
    dd = p % 64
    axis, half, fr = dd // 32, (dd % 32) // 16, dd % 16
    rotm = np.zeros((128, 128), np.float32)
    for po in range(128):
        if half[po] == 0:
            rotm[po + 16, po] = -1.0
        else:
            rotm[po - 16, po] = 1.0
    t = np.arange(L)
    pos = np.stack([(t // 64).astype(np.float32), (t % 64).astype(np.float32)], 0)
    inv = (np.float32(10000.0) ** (-np.arange(0, 32, 2, dtype=np.float32) / np.float32(32))).astype(np.float32)
    ang = pos[axis, :] * inv[fr][:, None]
    out = {"ident": np.eye(128, dtype=np.float32), "rotm": rotm,
           "ropec": np.cos(ang).astype(np.float32), "ropes": np.sin(ang).astype(np.float32)}
    N = 2 * L
    bp = np.arange(128)[:, None]; f2 = np.arange(65)[None, :]
    phi = 2 * np.pi * ((bp * f2) % 128) / 128
    out["tabA"] = np.concatenate([np.cos(phi), -np.sin(phi)], 1).astype(np.float32)
    ap = np.arange(64)[:, None, None]; f2_ = np.arange(65)[None, :, None]; f1 = np.arange(64)[None, None, :]
    th = 2 * np.pi * ((ap * (128 * f1 + f2_)) % N) / N
    out["tabB"] = np.concatenate([np.cos(th), -np.sin(th), np.sin(th), np.cos(th)], 2).astype(np.float32)
    thT = np.transpose(th, (2, 1, 0))
    top = np.concatenate([np.cos(thT), np.sin(thT)], 2)
    bot = np.concatenate([-np.sin(thT), np.cos(thT)], 2)
    out["tabBp"] = np.concatenate([top, bot], 0).astype(np.float32)
    w = np.full(65, 2.0); w[0] = 1; w[64] = 1
    phiT = 2 * np.pi * np.arange(65)[:, None] * np.arange(64)[None, :] / 128
    out["tabAp"] = (np.concatenate([w[:, None] * np.cos(phiT), -w[:, None] * np.sin(phiT)], 1) / N).astype(np.float32)
    tt = np.linspace(0.0, 1.0, L, dtype=np.float32)[None, :]
    ww = (2.0 * np.pi * np.arange(L, dtype=np.float32) / L)[None, :]
    ff = np.linspace(1e-4, 15, 16, dtype=np.float32)[:, None]
    zemb = np.concatenate([tt, np.cos(ff * ww), -np.sin(ff * ww)], 0).astype(np.float32)
    ridx = (L - np.arange(L)) % L
    out["zemb"] = np.ascontiguousarray(np.concatenate([zemb, zemb[:, ridx]], 1))
    deltas = np.abs(np.linspace(math.log(1e-2) / 1.5, math.log(1e-2) / 0.3, D, dtype=np.float32))
    out["deltas"] = np.ascontiguousarray(-deltas.reshape(8, 128).T).astype(np.float32)
    a64 = np.arange(64, dtype=np.float64)
    b8 = np.arange(8, dtype=np.float64); r512 = np.arange(512, dtype=np.float64)
    out["tlin1"] = np.stack([512 * b8 / (L - 1), (L - 512 * b8) / (L - 1)], 0).astype(np.float32)
    out["tlin2"] = np.stack([r512 / (L - 1), -r512 / (L - 1)], 0).astype(np.float32)
    return out


def make_in_maps(inputs, cores):
    consts = _host_consts()
    maps = []
    f = lambda a: np.ascontiguousarray(np.asarray(a, dtype=np.float32))
    for b in cores:
        m = {
            "x": f(inputs["x"][b]), "c": f(inputs["c"][b:b + 1]), "ctx": f(inputs["ctx"][b]),
            "c_ctx": f(np.asarray(inputs["c_ctx"]).reshape(1, D)),
            "w_ada": f(inputs["w_ada"][0]), "b_ada": f(inputs["b_ada"][0:1]), "norm_g": f(inputs["norm_g"][0]),
            "w_ff_in": f(inputs["w_ff_in"][0]), "w_ff_out": f(inputs["w_ff_out"][0]), "w_in": f(inputs["w_in"][0]),
            "lambda_q1": f(inputs["lambda_q1"][0:1]), "lambda_k1": f(inputs["lambda_k1"][0:1]),
            "lambda_q2": f(inputs["lambda_q2"][0:1]), "lambda_k2": f(inputs["lambda_k2"][0:1]),
            "subln_g": f(inputs["subln_g"][0:1]), "w_hy_out": f(inputs["w_hy_out"][0]),
            "hy_conv_w": f(inputs["hy_conv_w"][0]), "hy_conv_b": f(inputs["hy_conv_b"][0:1]),
            "filt_w1": f(inputs["filt_w1"][0]), "filt_b1": f(inputs["filt_b1"][0:1]),
            "filt_w2": f(inputs["filt_w2"][0]), "filt_b2": f(inputs["filt_b2"][0:1]),
            "filt_w3": f(inputs["filt_w3"][0]), "filt_b3": f(inputs["filt_b3"][0:1]),
            "filt_w4": f(inputs["filt_w4"][0]), "filt_freq": f(inputs["filt_freq"][0:1]),
            "hy_bias": f(inputs["hy_bias"][0:1]),
            "w_da_out": f(inputs["w_da_out"][0]), "w_o": f(inputs["w_o"][0]),
        }
        m.update(consts)
        maps.append(m)
    return maps


def kernel(**inputs):
    if "nc" not in _NC_CACHE:
        _NC_CACHE["nc"] = build()
    nc = _NC_CACHE["nc"]
    in_maps = make_in_maps(inputs, list(range(8)))
    res = run_bass_kernel_spmd(nc, in_maps, core_ids=list(range(8)))
    return np.stack([np.asarray(r["out"], dtype=np.float32) for r in res.results], axis=0)
```

```python
import contextlib
import math
import os
STAGE = int(os.environ.get('KSTAGE', '99'))
FSUB = int(os.environ.get('FSUB', '99'))
FV = os.environ.get('FV', '')
ATH = int(os.environ.get('ATH', '8'))
HYENA = int(os.environ.get('HYENA', '1'))
ASUB = int(os.environ.get('ASUB', '99'))
import numpy as np
import concourse.bass as bass
import concourse.mybir as mybir
from concourse.bass_utils import run_bass_kernel_spmd

F32 = mybir.dt.float32
BF16 = mybir.dt.bfloat16
AF = mybir.ActivationFunctionType
ALU = mybir.AluOpType
AX = mybir.AxisListType

D = 1024
L = 4096
LC = 256
LK = L + LC
DFF = 2816
NJ = DFF // 128
NH = 8
EPS = 1e-6
NPROJ = 8192
N_DMA_LANES = 40
N_BG_LANES = 24
LAM_INIT = 0.8 - 0.6 * math.exp(-0.3 * 0)


class Sched:
    def __init__(self, nc):
        self.nc = nc
        self.eng = {"pe": nc.tensor, "act": nc.scalar, "dve": nc.vector,
                    "pool": nc.gpsimd, "sp": nc.sync}
        self.sem = {}
        self.cnt = {}
        for e in self.eng:
            self.sem[e] = nc.alloc_semaphore("s_" + e)
            self.cnt[e] = 0
        for i in range(N_DMA_LANES):
            self.sem["d%d" % i] = nc.alloc_semaphore("d%d" % i)
            self.cnt["d%d" % i] = 0
        for i in range(N_BG_LANES):
            self.sem["b%d" % i] = nc.alloc_semaphore("b%d" % i)
            self.cnt["b%d" % i] = 0
        self.next_bg = 0
        self.next_lane = 0
        for k in self.sem:
            nc.gpsimd.sem_clear(self.sem[k])
        nc.all_engine_barrier()
        self.waited = {e: {} for e in self.eng}
        self.W = {}
        self.R = {}
        self.q = {e: [] for e in self.eng}
        self.n_ins = 0
        self.n_wait = 0
        self.psum = set()

    def reg_psum(self, t):
        self.psum.add(t.name)
        return t

    def _wait(self, eng, deps):
        need = {}
        for (k, v) in deps:
            if v > need.get(k, 0):
                need[k] = v
        w = self.waited[eng]
        for k, v in need.items():
            if k == eng and eng in ("pe", "sp"):
                continue
            if w.get(k, 0) >= v:
                continue
            self.q[eng].append(("w", self.sem[k], v))
            self.n_wait += 1
            w[k] = v

    @staticmethod
    def _key(k):
        if isinstance(k, tuple):
            return tuple(Sched._key(x) for x in k)
        if isinstance(k, (str, int)):
            return k
        return k.name

    def _deps(self, reads, writes):
        deps = set()
        for k in reads:
            if k in self.W:
                deps.add(self.W[k])
        for k in writes:
            if k in self.W:
                deps.add(self.W[k])
            for ev in self.R.get(k, {}).items():
                deps.add(ev)
        return deps

    def _commit(self, ev, reads, writes):
        for k in reads:
            r = self.R.setdefault(k, {})
            if ev[1] > r.get(ev[0], 0):
                r[ev[0]] = ev[1]
        for k in writes:
            self.W[k] = ev
            self.R[k] = {}

    def op(self, eng, meth, *args, reads=(), writes=(), **kw):
        reads = [self._key(k) for k in reads]
        writes = [self._key(k) for k in writes]
        writes = writes + [k for k in reads if k in self.psum]
        reads = [k for k in reads if k not in self.psum]
        self._wait(eng, self._deps(reads, writes))
        fn = (lambda e, meth=meth, args=args, kw=kw: getattr(e, meth)(*args, **kw))
        self.cnt[eng] += 1
        self.q[eng].append(("i", fn, self.sem[eng], 1))
        self.n_ins += 1
        self._commit((eng, self.cnt[eng]), reads, writes)

    def dma(self, eng, out, in_, reads=(), writes=(), bg=False, **kw):
        reads = [self._key(k) for k in reads]
        writes = [self._key(k) for k in writes]
        if bg:
            lane = "b%d" % self.next_bg
            self.next_bg = (self.next_bg + 1) % N_BG_LANES
        else:
            lane = "d%d" % self.next_lane
            self.next_lane = (self.next_lane + 1) % N_DMA_LANES
        deps = self._deps(reads, writes)
        if self.cnt[lane] > 0:
            deps.add((lane, self.cnt[lane]))
        self._wait(eng, deps)
        self.cnt[lane] += 16
        self.q[eng].append(("i", (lambda e, out=out, in_=in_, kw=kw: e.dma_start(out=out, in_=in_, **kw)),
                            self.sem[lane], 16))
        self.n_ins += 1
        self._commit((lane, self.cnt[lane]), reads, writes)

    def emit(self):
        nc = self.nc
        q = self.q
        self.q = {e: [] for e in self.eng}

        def run(e, items):
            for it in items:
                if it[0] == "w":
                    e.wait_ge(it[1], it[2])
                else:
                    it[1](e).then_inc(it[2], it[3])

        with nc.Block() as block:
            @block.sync
            def _(e):
                run(e, q["sp"])

            @block.scalar
            def _(e):
                run(e, q["act"])

            @block.vector
            def _(e):
                run(e, q["dve"])

            @block.gpsimd
            def _(e):
                run(e, q["pool"])

            @block.tensor
            def _(e):
                run(e, q["pe"])

    def barrier(self, skip_bg=False):
        allev = set((k, v) for k, v in self.cnt.items() if v > 0 and not (skip_bg and k[0] == "b"))
        for e in self.eng:
            self._wait(e, allev)

    def phase_end(self, skip_bg=False):
        self.barrier(skip_bg)
        self.emit()


class Ctx:
    pass


def _pn_stats(S, g, xin, xin_key, bs):
    ss, rs, xs = bs["ss"], bs["rs"], bs["xs"]
    S.op("pool", "memset", ss[:], 0.0, writes=[ss])
    S.op("act", "activation", out=g.junk[:], in_=xin, func=AF.Square, accum_out=ss[:],
         reads=[xin_key, ss], writes=[ss])
    S.op("act", "activation", out=rs[:], in_=ss[:], func=AF.Ln, scale=1.0 / D, bias=g.eps_t[:, 0:1],
         reads=[ss], writes=[rs])
    S.op("act", "activation", out=rs[:], in_=rs[:], func=AF.Exp, scale=-0.5, reads=[rs], writes=[rs])
    S.op("dve", "tensor_scalar", out=xs[:], in0=xin, scalar1=rs[:, 0:1], scalar2=None, op0=ALU.mult,
         reads=[xin_key, rs], writes=[xs])


def _pn_transpose(S, g, bs, psT, A, B, dst, dst_key, col0):
    xs = bs["xs"]
    for kc in range(8):
        S.op("pe", "transpose", out=psT[:, kc, :], in_=xs[:, kc * 128:(kc + 1) * 128], identity=g.identb[:],
             reads=[xs], writes=[psT])
    for kc in range(8):
        if kc % 2 == 0:
            S.op("act", "activation", out=dst[:, kc, col0:col0 + 128], in_=psT[:, kc, :], func=AF.Identity,
                 scale=A[:, kc:kc + 1], bias=B[:, kc:kc + 1], reads=[psT], writes=[dst_key])
        else:
            S.op("dve", "tensor_scalar", out=dst[:, kc, col0:col0 + 128], in0=psT[:, kc, :], scalar1=A[:, kc:kc + 1],
                 scalar2=B[:, kc:kc + 1], op0=ALU.mult, op1=ALU.add, reads=[psT], writes=[dst_key])


def ffn_alloc_wi(S, nc, st, tag, w_in_d, from_bf16=False, bg=False):
    wi = st.enter_context(nc.sbuf_tensor(tag + "wi", [128, 8, 2 * DFF], BF16))
    q = "sp" if from_bf16 else "pool"
    for kc in range(8):
        S.dma(q, wi[:, kc, :], w_in_d[kc * 128:(kc + 1) * 128, :], reads=[("wsrc", tag, "i", kc)], writes=[wi], bg=bg)
    return wi


def ffn_alloc_wo(S, nc, st, tag, w_out_d, from_bf16=False, bg=False):
    wo = st.enter_context(nc.sbuf_tensor(tag + "wo", [128, NJ, D], BF16))
    q = "sp" if from_bf16 else "pool"
    wov = w_out_d.rearrange("(j p) n -> p j n", p=128)
    for j0 in range(0, NJ, 2):
        S.dma(q, wo[:, j0:j0 + 2, :], wov[:, j0:j0 + 2, :], reads=[("wsrc", tag, "o", j0)], writes=[wo], bg=bg)
    return wo


def ffn_phase(S, nc, g, tag, w_in_d, w_out_d, streams, preloaded=None, from_bf16=False):
    with contextlib.ExitStack() as st:
        def T(name, shape, dt):
            return st.enter_context(nc.sbuf_tensor(tag + name, shape, dt))

        def P(name, shape, dt):
            return S.reg_psum(st.enter_context(nc.psum_tensor(tag + name, shape, dt)))

        g.psT = [P("psT%d" % i, [128, 8, 128], BF16) for i in range(2)]
        wi = preloaded if preloaded is not None else ffn_alloc_wi(S, nc, st, tag, w_in_d, from_bf16=from_bf16)
        wo = ffn_alloc_wo(S, nc, st, tag, w_out_d, from_bf16=from_bf16)
        TB = 256
        xin = [T("xin%d" % i, [128, D], F32) for i in range(2)]
        xres = [T("xres%d" % i, [128, D], F32) for i in range(1)]
        xT = [T("xT%d" % i, [128, 8, TB], BF16) for i in range(2)]
        hT = T("hT", [128, NJ, TB], BF16)
        sg = [T("sg%d" % i, [128, TB], F32) for i in range(2)]
        ytmp = [T("yt%d" % i, [128, D], F32) for i in range(2)]
        h2 = [T("h2_%d" % i, [128, 8, 128], BF16) for i in range(2)]
        ss2 = [T("ss2_%d" % i, [128, 2], F32) for i in range(2)]
        rs2 = [T("rs2_%d" % i, [128, 1], F32) for i in range(2)]
        psgu = [P("psgu%d" % i, [128, 2, TB], F32) for i in range(2)]
        pso = [P("pso%d" % i, [128, D], F32) for i in range(2)]

        cnt = dict(gu=0)
        blocks = []
        for sdef in streams:
            nblk = (sdef["T"] + TB - 1) // TB
            for b_ in range(nblk):
                blocks.append((sdef, b_))
        bs_pre = [dict(ss=T("ssp%d" % i, [128, 1], F32), rs=T("rsp%d" % i, [128, 1], F32), xs=T("xsp%d" % i, [128, D], BF16))
                  for i in range(2)]
        bs_nxt = [dict(ss=T("ssn%d" % i, [128, 1], F32), rs=T("rsn%d" % i, [128, 1], F32), xs=T("xsn%d" % i, [128, D], BF16))
                  for i in range(2)]

        def geom(bi):
            sdef, b_ = blocks[bi]
            t0 = b_ * TB
            return sdef, t0, min(TB, sdef["T"] - t0) // 128

        def stats(bi):
            sdef, t0, ntile = geom(bi)
            for ti in range(ntile):
                r0 = t0 + ti * 128
                xi = xin[ti % 2]
                S.dma("sp", xi[:], sdef["src"][r0:r0 + 128, :], writes=[xi])
                _pn_stats(S, g, xi[:], xi.name, bs_pre[ti % 2])

        def transposes(bi):
            sdef, t0, ntile = geom(bi)
            xTb = xT[bi % 2]
            for ti in range(ntile):
                _pn_transpose(S, g, bs_pre[ti % 2], g.psT[ti % 2], sdef["A"], sdef["B"], xTb, xTb.name, ti * 128)

        def gateup(bi):
            sdef, t0, ntile = geom(bi)
            nt = ntile * 128
            xTb = xT[bi % 2]
            for j in range(NJ):
                pg = psgu[cnt["gu"] % 2]
                sgi = sg[cnt["gu"] % 2]
                cnt["gu"] += 1
                for kc in range(8):
                    S.op("pe", "matmul", pg[:, 0, :nt], lhsT=wi[:, kc, j * 128:(j + 1) * 128], rhs=xTb[:, kc, :nt],
                         start=(kc == 0), stop=(kc == 7), reads=[wi, xTb], writes=[pg])
                for kc in range(8):
                    S.op("pe", "matmul", pg[:, 1, :nt], lhsT=wi[:, kc, DFF + j * 128:DFF + (j + 1) * 128],
                         rhs=xTb[:, kc, :nt], start=(kc == 0), stop=(kc == 7), reads=[wi, xTb], writes=[pg])
                S.op("act", "activation", out=sgi[:, :nt], in_=pg[:, 0, :nt], func=AF.Silu, reads=[pg], writes=[sgi])
                S.op("dve", "tensor_tensor", out=hT[:, j, :nt], in0=sgi[:, :nt], in1=pg[:, 1, :nt], op=ALU.mult,
                     reads=[sgi, pg], writes=[hT])

        def outmm(bi):
            sdef, t0, ntile = geom(bi)
            for ti in range(ntile):
                po = pso[ti % 2]
                for half in range(2):
                    for j in range(NJ):
                        S.op("pe", "matmul", po[:, half * 512:(half + 1) * 512],
                             lhsT=hT[:, j, ti * 128:(ti + 1) * 128], rhs=wo[:, j, half * 512:(half + 1) * 512],
                             start=(j == 0), stop=(j == NJ - 1), reads=[hT, wo], writes=[po])

        def post(bi):
            sdef, t0, ntile = geom(bi)
            for ti in range(ntile):
                r0 = t0 + ti * 128
                xr = xres[0]
                S.dma("sp", xr[:], sdef["src"][r0:r0 + 128, :], writes=[xr])
                yt = ytmp[ti % 2]
                post_norm_residual(S, g, tag, pso[ti % 2], sdef["G"], xr, yt, ss2[ti % 2], rs2[ti % 2],
                                   sdef["dst"][r0:r0 + 128, :], (tag, "dst", sdef["T"], r0))
                if sdef.get("nxt") is not None:
                    _pn_stats(S, g, yt[:], yt.name, bs_nxt[ti % 2])

        def nxt_transposes(bi):
            sdef, t0, ntile = geom(bi)
            if sdef.get("nxt") is None:
                return
            nx = sdef["nxt"]
            for ti in range(ntile):
                r0 = t0 + ti * 128
                hh = h2[ti % 2]
                _pn_transpose(S, g, bs_nxt[ti % 2], g.psT[ti % 2], nx["A"], nx["B"], hh, hh.name, 0)
                S.dma("sp", nx["dst"][:, :, r0:r0 + 128], hh[:], reads=[hh], writes=[(tag, "hdst", sdef["T"], r0)])

        nb = len(blocks)
        stats(0)
        transposes(0)
        for bi in range(nb):
            if bi + 1 < nb:
                stats(bi + 1)
            gateup(bi)
            if bi + 1 < nb:
                transposes(bi + 1)
            if bi >= 1:
                nxt_transposes(bi - 1)
            outmm(bi)
            post(bi)
        nxt_transposes(nb - 1)
        S.phase_end()


def post_norm_residual(S, g, tag, po, G, xr, yt, s2, r2, dst_ap, dst_key):
    S.op("pool", "memset", s2[:], 0.0, writes=[s2])
    for half in range(2):
        S.op("act", "activation", out=g.junk[:, 0:512], in_=po[:, half * 512:(half + 1) * 512],
             func=AF.Square, accum_out=s2[:, half:half + 1], reads=[po, s2], writes=[s2])
    S.op("dve", "tensor_tensor", out=r2[:], in0=s2[:, 0:1], in1=s2[:, 1:2], op=ALU.add, reads=[s2], writes=[r2])
    S.op("act", "activation", out=r2[:], in_=r2[:], func=AF.Ln, scale=1.0 / D, bias=g.eps_t[:, 0:1],
         reads=[r2], writes=[r2])
    S.op("act", "activation", out=r2[:], in_=r2[:], func=AF.Exp, scale=-0.5, reads=[r2], writes=[r2])
    S.op("dve", "scalar_tensor_tensor", out=yt[:], in0=po[:], scalar=r2[:, 0:1], in1=G[:],
         op0=ALU.mult, op1=ALU.mult, reads=[po, r2, G], writes=[yt])
    S.op("pool", "tensor_tensor", out=yt[:], in0=yt[:], in1=xr[:], op=ALU.add, reads=[yt, xr], writes=[yt])
    S.dma("sp", dst_ap, yt[:], reads=[yt], writes=[dst_key])


def hyena_proj(S, nc, g, d, hT, st_outer):
    with contextlib.ExitStack() as st:
        def T(name, shape, dt):
            return st.enter_context(nc.sbuf_tensor("hp" + name, shape, dt))

        psA = S.reg_psum(st.enter_context(nc.psum_tensor("hppsA", [128, 512], F32)))
        psA2 = S.reg_psum(st.enter_context(nc.psum_tensor("hppsA2", [128, 512], F32)))
        pss = [psA, psA2]
        cw = T("cw", [128, 3, 3, 8], F32)
        cb = T("cb", [128, 3, 8], F32)
        for j in range(3):
            for p in range(3):
                S.dma("sp", cw[:, j, p, :], d.hy_conv_w_d[j:j + 1, p * D:(p + 1) * D].rearrange("o (ch q) -> q (o ch)", q=128),
                      writes=[cw], allow_slow_non_contiguous=True)
        for p in range(3):
            S.dma("sp", cb[:, p, :], d.hy_conv_b_d[0:1, p * D:(p + 1) * D].rearrange("o (ch q) -> q (o ch)", q=128),
                  writes=[cb], allow_slow_non_contiguous=True)
        w3 = [T("w3_%d" % i, [128, 8, 3, 128], BF16) for i in range(2)]
        zhs = [T("zh%d" % i, [128, L + 2], F32) for i in range(2)]
        zc = [T("zc%d" % i, [128, L], F32) for i in range(2)]
        ob = [T("ob%d" % i, [128, L], BF16) for i in range(2)]
        for zh in zhs:
            S.op("pool", "memset", zh[:, 0:1], 0.0, writes=[zh])
            S.op("pool", "memset", zh[:, L + 1:L + 2], 0.0, writes=[zh])
        zi = 0
        wv_ = d.w_in_d.rearrange("(kc p) n -> p kc n", p=128)

        def load_w(ch):
            for p in range(3):
                S.dma("pool", w3[ch % 2][:, :, p, :], wv_[:, :, p * D + ch * 128:p * D + (ch + 1) * 128], writes=[w3[ch % 2]])
        load_w(0)
        pi = 0
        for ch in range(8):
            if ch + 1 < 8:
                load_w(ch + 1)
            w = w3[ch % 2]
            for part in (0, 1, 2):
                zh = zhs[zi % 2]
                zi += 1
                for blk in range(L // 512):
                    ps = pss[pi % 2]
                    pi += 1
                    for kc in range(8):
                        S.op("pe", "matmul", ps[:], lhsT=w[:, kc, part, :], rhs=hT[:, kc, blk * 512:(blk + 1) * 512],
                             start=(kc == 0), stop=(kc == 7), reads=[w, hT], writes=[ps])
                    S.op("act", "copy", out=zh[:, 1 + blk * 512:1 + (blk + 1) * 512], in_=ps[:], reads=[ps], writes=[zh])
                z = zc[1] if part == 1 else zc[0]
                S.op("act", "activation", out=z[:], in_=zh[:, 1:L + 1], func=AF.Identity, scale=cw[:, 1, part, ch:ch + 1],
                     bias=cb[:, part, ch:ch + 1], reads=[zh, cw, cb], writes=[z])
                S.op("dve", "scalar_tensor_tensor", out=z[:], in0=zh[:, 0:L], scalar=cw[:, 0, part, ch:ch + 1], in1=z[:],
                     op0=ALU.mult, op1=ALU.add, reads=[zh, cw, z], writes=[z])
                S.op("dve", "scalar_tensor_tensor", out=z[:], in0=zh[:, 2:L + 2], scalar=cw[:, 2, part, ch:ch + 1], in1=z[:],
                     op0=ALU.mult, op1=ALU.add, reads=[zh, cw, z], writes=[z])
                if part == 0:
                    S.op("act", "copy", out=ob[0][:], in_=z[:], reads=[z], writes=[ob[0]])
                    S.dma("sp", d.x0_d[:, ch, :], ob[0][:], reads=[ob[0]], writes=[("x0_d", ch)])
                elif part == 2:
                    S.op("pool", "tensor_tensor", out=ob[1][:], in0=z[:], in1=zc[1][:], op=ALU.mult, reads=[z, zc[1]],
                         writes=[ob[1]])
                    S.dma("sp", d.u_d[:, ch, :], ob[1][:], reads=[ob[1]], writes=[("u_d", ch)])
        S.barrier()


def hyena_fft_phase(S, nc, g, d):
    TWO_PI = 2.0 * math.pi
    MAGIC = 12582912.0
    with contextlib.ExitStack() as st:
        def T(name, shape, dt):
            return st.enter_context(nc.sbuf_tensor("hf" + name, shape, dt))

        bank = [S.reg_psum(st.enter_context(nc.psum_tensor("hfbank%d" % i, [128, 512], F32))) for i in range(8)]
        bankb = [b[:].bitcast(BF16) for b in bank]
        tabA = T("tabA", [128, 130], BF16)
        tabB = T("tabB", [64, 65, 256], BF16)
        tabBp = T("tabBp", [128, 65, 128], BF16)
        tabAp = T("tabAp", [65, 128], BF16)
        S.dma("pool", tabA[:], d.tabA_d, writes=[tabA])
        for f0 in range(0, 65, 13):
            S.dma("pool", tabB[:, f0:f0 + 13, :], d.tabB_d[:, f0:f0 + 13, :], writes=[tabB])
            S.dma("pool", tabBp[:, f0:f0 + 13, :], d.tabBp_d[:, f0:f0 + 13, :], writes=[tabBp])
        S.dma("pool", tabAp[:], d.tabAp_d, writes=[tabAp])
        w4 = T("w4", [64, 2 * D], BF16)
        S.dma("pool", w4[:], d.filt_w4_d, writes=[w4])
        hA = T("hA", [64, 2 * L], BF16)
        with contextlib.ExitStack() as st2:
            def T2(name, shape, dt):
                return st2.enter_context(nc.sbuf_tensor("hf" + name, shape, dt))
            zT = T2("zT", [33, 2 * L], BF16)
            S.dma("pool", zT[:], d.zemb_d, writes=[zT])
            w1 = T2("w1", [33, 64], BF16)
            w2 = T2("w2", [64, 64], BF16)
            w3_ = T2("w3", [64, 64], BF16)
            S.dma("pool", w1[:], d.filt_w1_d, writes=[w1])
            S.dma("pool", w2[:], d.filt_w2_d, writes=[w2])
            S.dma("pool", w3_[:], d.filt_w3_d, writes=[w3_])
            fb = T2("fb", [64, 4], F32)
            for i, ap in enumerate([d.filt_b1_d, d.filt_b2_d, d.filt_b3_d, d.filt_freq_d]):
                S.dma("sp", fb[:, i:i + 1], ap.rearrange("o p -> p o"), writes=[fb], allow_slow_non_contiguous=True)
            fbb = T2("fbb", [64, 3], F32)
            S.op("dve", "tensor_scalar", out=fbb[:], in0=fb[:, 0:3], scalar1=fb[:, 3:4], scalar2=None, op0=ALU.mult,
                 reads=[fb], writes=[fbb])
            hB = T2("hB", [64, 2 * L], BF16)
            ra = [T2("ra%d" % i, [64, 512], F32) for i in range(2)]
            rb = [T2("rb%d" % i, [64, 512], F32) for i in range(2)]
            li = 0
            for layer, (wl, src, dst) in enumerate(((w1, zT, hA), (w2, hA, hB), (w3_, hB, hA))):
                kdim = 33 if layer == 0 else 64
                for blk in range(2 * L // 512):
                    cs = slice(blk * 512, (blk + 1) * 512)
                    ps = bank[6 + li % 2]
                    a_, b_ = ra[li % 2], rb[li % 2]
                    li += 1
                    S.op("pe", "matmul", ps[0:64, :], lhsT=wl[0:kdim, :], rhs=src[0:kdim, cs], start=True, stop=True,
                         reads=[wl, src], writes=[ps])
                    S.op("act", "activation", out=a_[:], in_=ps[0:64, :], func=AF.Identity, scale=fb[:, 3:4],
                         bias=fbb[:, layer:layer + 1], reads=[ps, fb, fbb], writes=[a_])
                    S.op("dve", "tensor_scalar", out=b_[:], in0=a_[:], scalar1=1.0 / TWO_PI, scalar2=MAGIC, op0=ALU.mult,
                         op1=ALU.add, reads=[a_], writes=[b_])
                    S.op("dve", "tensor_scalar", out=b_[:], in0=b_[:], scalar1=MAGIC, scalar2=-TWO_PI, op0=ALU.subtract,
                         op1=ALU.mult, reads=[b_], writes=[b_])
                    S.op("dve", "tensor_tensor", out=a_[:], in0=a_[:], in1=b_[:], op=ALU.add, reads=[a_, b_], writes=[a_])
                    S.op("dve", "tensor_scalar", out=a_[:], in0=a_[:], scalar1=3.1415925, scalar2=-3.1415925, op0=ALU.min,
                         op1=ALU.max, reads=[a_], writes=[a_])
                    S.op("act", "activation", out=dst[:, cs], in_=a_[:], func=AF.Sin, reads=[a_], writes=[dst])
            S.barrier()
        h3 = hA
        dl = T("dl", [128, 8], F32)
        S.dma("sp", dl[:], d.deltas_d, writes=[dl])
        tl1 = T("tl1", [128, 2, 8], F32)
        tl2 = T("tl2", [128, 512], F32)
        for j in range(2):
            S.dma("sp", tl1[:, j, :], d.tlin1_d[j:j + 1, :].broadcast_to([128, 8]), writes=[tl1])
        S.dma("sp", tl2[:], d.tlin2_d[0:1, :].broadcast_to([128, 512]), writes=[tl2])
        ndl = T("ndl", [128, 8], F32)
        S.op("dve", "tensor_scalar", out=ndl[:], in0=dl[:], scalar1=-1.0, scalar2=None, op0=ALU.mult, reads=[dl], writes=[ndl])
        hbias = T("hbias", [128, 8], F32)
        S.dma("sp", hbias[:], d.hy_bias_d.rearrange("o (ch q) -> q (o ch)", q=128), writes=[hbias],
              allow_slow_non_contiguous=True)
        E1 = T("E1", [128, 2, 8], F32)
        E2 = T("E2", [128, 2, 512], F32)
        sq2 = T("sq2", [128, 2, L], BF16)
        x0 = T("x0", [128, L], BF16)
        hyo = sq2[:, 1, :]
        buf1 = T("buf1", [128, 65 * 128], BF16)
        buf2 = T("buf2", [128, 65 * 128], BF16)
        buf3 = T("buf3", [128, 128 * 130], BF16)
        Kf = T("Kf", [128, 65, 128], BF16)
        P1 = [T("P1_%d" % i, [128, 4, 128], F32) for i in range(2)]
        P2 = [T("P2_%d" % i, [128, 4, 128], F32) for i in range(1)]
        asum = T("asum", [128, 17], F32)
        nrm = T("nrm", [128, 1], F32)
        UB = buf1[0:64, 0:64 * 128].rearrange("p (a c) -> p a c", c=128)
        UBf = buf1[:, 0:64 * 128].rearrange("p (a c) -> p a c", c=128)
        YT = buf1[:, :].rearrange("p (f c) -> p f c", c=128)
        Y = buf2[:, :].rearrange("p (f k) -> p f k", k=128)
        Q = Y
        VA = buf3[0:64, :].rearrange("p (c k) -> p c k", k=130)
        QT = buf3[0:65, 0:128 * 128].rearrange("p (c a) -> p c a", a=128)
        ev = [0]
        rot = [0]

        def nb():
            rot[0] += 1
            return (rot[0] - 1) % 8

        def evac(out, in_, reads, writes):
            if ev[0] % 4 != 3:
                S.op("act", "copy", out=out, in_=in_, reads=reads, writes=writes)
            else:
                S.op("dve", "tensor_copy", out=out, in_=in_, reads=reads, writes=writes)
            ev[0] += 1

        def forward(full, consume):
            K = 128 if full else 64
            sv = sq2[:, :, :].rearrange("p h (b a) -> p a h b", a=64)
            ub = UBf if full else UB
            for a0 in range(0, 64, 8):
                bi = nb()
                for u in range(8):
                    in_ = sv[:, a0 + u, :, :] if full else sv[:, a0 + u, 0, :]
                    S.op("pe", "transpose", out=bankb[bi][0:K, u * 128:(u + 1) * 128], in_=in_,
                         identity=g.identb[:], reads=[sq2], writes=[bank[bi]])
                evac(ub[:, a0:a0 + 8, :], bankb[bi][0:K, 0:1024].rearrange("p (a c) -> p a c", c=128), [bank[bi]], [buf1])
            for c0 in range(0, 128, 3):
                n = min(3, 128 - c0)
                bi = nb()
                for u in range(n):
                    S.op("pe", "matmul", bank[bi][0:64, u * 130:(u + 1) * 130], lhsT=ub[:, :, c0 + u], rhs=tabA[0:K, :],
                         start=True, stop=True, reads=[buf1, tabA], writes=[bank[bi]])
                evac(VA[:, c0:c0 + n, :], bank[bi][0:64, 0:n * 130].rearrange("p (c k) -> p c k", k=130), [bank[bi]], [buf3])
            for f0 in range(0, 65, 4):
                n = min(4, 65 - f0)
                bi = nb()
                for u in range(n):
                    f = f0 + u
                    S.op("pe", "matmul", bank[bi][:, u * 128:(u + 1) * 128], lhsT=VA[:, :, f], rhs=tabB[:, f, 0:128],
                         start=True, stop=False, reads=[buf3, tabB], writes=[bank[bi]])
                    S.op("pe", "matmul", bank[bi][:, u * 128:(u + 1) * 128], lhsT=VA[:, :, 65 + f], rhs=tabB[:, f, 128:256],
                         start=False, stop=True, reads=[buf3, tabB], writes=[bank[bi]])
                consume(bank[bi], f0, n)

        for ch in range(8):
            for j in range(2):
                S.op("act", "activation", out=E1[:, j, :], in_=tl1[:, j, :], func=AF.Exp, scale=dl[:, ch:ch + 1],
                     reads=[tl1, dl], writes=[E1])
                S.op("act", "activation", out=E2[:, j, :], in_=tl2[:], func=AF.Exp,
                     scale=(dl if j == 0 else ndl)[:, ch:ch + 1], reads=[tl2, dl, ndl], writes=[E2])
            S.op("pool", "memset", asum[:], 0.0, writes=[asum, sq2] + [(sq2.name, fi_, b_) for fi_ in range(2) for b_ in range(8)])
            for fi in range(2):
                for blk in range(L // 512):
                    ps = bank[nb()]
                    S.op("pe", "matmul", ps[:], lhsT=w4[:, fi * D + ch * 128:fi * D + (ch + 1) * 128], rhs=h3[:, fi * L + blk * 512:fi * L + (blk + 1) * 512],
                         start=True, stop=True, reads=[w4, h3], writes=[ps])
                    sqv = sq2[:, fi, blk * 512:(blk + 1) * 512]
                    S.op("dve", "scalar_tensor_tensor", out=sqv, in0=ps[:], scalar=E1[:, fi, blk:blk + 1], in1=E2[:, fi, :],
                         op0=ALU.mult, op1=ALU.mult, reads=[ps, E1, E2], writes=[(sq2.name, fi, blk)])
                    if fi == 1 and blk == 0:
                        S.op("dve", "memset", sq2[:, 1, 0:1], 0.0, writes=[(sq2.name, fi, blk)])
                    S.op("act", "activation", out=g.junk[:, 0:512], in_=sqv, func=AF.Abs,
                         accum_out=asum[:, fi * 8 + blk:fi * 8 + blk + 1], reads=[(sq2.name, fi, blk), asum], writes=[asum])
            S.op("pool", "memset", asum[:, 16:17], 0.0, reads=[(sq2.name, fi_, b_) for fi_ in range(2) for b_ in range(8)], writes=[sq2])
            S.op("dve", "reduce_sum", out=nrm[:], in_=asum[:, 0:16], axis=AX.X, reads=[asum], writes=[nrm])
            S.op("dve", "tensor_scalar", out=nrm[:], in0=nrm[:], scalar1=EPS, scalar2=None, op0=ALU.add, reads=[nrm], writes=[nrm])
            S.op("dve", "reciprocal", out=nrm[:], in_=nrm[:], reads=[nrm], writes=[nrm])

            def cons_f(bk, f0, n):
                bv = bk[:, 0:n * 128].rearrange("p (f k) -> p f k", k=128)
                S.op("act", "activation", out=Kf[:, f0:f0 + n, 0:64], in_=bv[:, :, 0:64], func=AF.Identity, scale=nrm[:, 0:1],
                     bias=hbias[:, ch:ch + 1], reads=[bk, nrm, hbias], writes=[Kf])
                S.op("act", "activation", out=Kf[:, f0:f0 + n, 64:128], in_=bv[:, :, 64:128], func=AF.Identity, scale=nrm[:, 0:1],
                     reads=[bk, nrm], writes=[Kf])

            def cons_b(bk, f0, n):
                bv = bk[:, 0:n * 128].rearrange("p (f k) -> p f k", k=128)
                S.op("dve", "scalar_tensor_tensor", out=Kf[:, f0:f0 + n, 0:64], in0=bv[:, :, 0:64], scalar=nrm[:, 0:1],
                     in1=Kf[:, f0:f0 + n, 0:64], op0=ALU.mult, op1=ALU.add, reads=[bk, nrm, Kf], writes=[Kf])
                S.op("dve", "scalar_tensor_tensor", out=Kf[:, f0:f0 + n, 64:128], in0=bv[:, :, 64:128], scalar=nrm[:, 0:1],
                     in1=Kf[:, f0:f0 + n, 64:128], op0=ALU.mult, op1=ALU.subtract, reads=[bk, nrm, Kf], writes=[Kf])
                S.op("pool", "tensor_scalar", out=Kf[:, f0:f0 + n, 64:128], in0=Kf[:, f0:f0 + n, 64:128], scalar1=-1.0,
                     scalar2=None, op0=ALU.mult, reads=[Kf], writes=[Kf])

            forward(True, cons_f)
            S.dma("sp", sq2[:, 0, :], d.u_d[:, ch, :], writes=[sq2])
            S.dma("sp", x0[:], d.x0_d[:, ch, :], writes=[x0])
            pc = [0]

            def cons_u(bk, f0, n):
                p1, p2 = P1[pc[0] % 2], P2[0]
                pc[0] += 1
                bv = bk[:, 0:n * 128].rearrange("p (f r k) -> p f r k", r=2, k=64)
                kr = Kf[:, f0:f0 + n, 0:64].unsqueeze(2).broadcast_to([128, n, 2, 64])
                ki = Kf[:, f0:f0 + n, 64:128].unsqueeze(2).broadcast_to([128, n, 2, 64])
                p1v = p1[:, 0:n, :].rearrange("p f (r k) -> p f r k", r=2)
                p2v = p2[:, 0:n, :].rearrange("p f (r k) -> p f r k", r=2)
                S.op("dve", "tensor_tensor", out=p1v, in0=bv, in1=kr, op=ALU.mult, reads=[bk, Kf], writes=[p1])
                S.op("dve", "tensor_tensor", out=p2v, in0=bv, in1=ki, op=ALU.mult, reads=[bk, Kf], writes=[p2])
                S.op("pool", "tensor_tensor", out=Y[:, f0:f0 + n, 0:64], in0=p1[:, 0:n, 0:64], in1=p2[:, 0:n, 64:128],
                     op=ALU.subtract, reads=[p1, p2], writes=[buf2])
                S.op("pool", "tensor_tensor", out=Y[:, f0:f0 + n, 64:128], in0=p2[:, 0:n, 0:64], in1=p1[:, 0:n, 64:128],
                     op=ALU.add, reads=[p1, p2], writes=[buf2])

            forward(False, cons_u)
            for f0 in range(0, 65, 8):
                n = min(8, 65 - f0)
                bi = nb()
                for u in range(n):
                    S.op("pe", "transpose", out=bankb[bi][:, u * 128:(u + 1) * 128], in_=Y[:, f0 + u, :], identity=g.identb[:],
                         reads=[buf2], writes=[bank[bi]])
                evac(YT[:, f0:f0 + n, :], bankb[bi][:, 0:n * 128].rearrange("p (f c) -> p f c", c=128), [bank[bi]], [buf1])
            for f0 in range(0, 65, 4):
                n = min(4, 65 - f0)
                bi = nb()
                for u in range(n):
                    S.op("pe", "matmul", bank[bi][:, u * 128:(u + 1) * 128], lhsT=tabBp[:, f0 + u, :], rhs=YT[:, f0 + u, :],
                         start=True, stop=True, reads=[tabBp, buf1], writes=[bank[bi]])
                evac(Q[:, f0:f0 + n, :], bank[bi][:, 0:n * 128].rearrange("p (f c) -> p f c", c=128), [bank[bi]], [buf2])
            for c0 in range(0, 128, 8):
                bi = nb()
                for u in range(8):
                    S.op("pe", "transpose", out=bankb[bi][0:65, u * 128:(u + 1) * 128], in_=Q[:, :, c0 + u], identity=g.identb[:],
                         reads=[buf2], writes=[bank[bi]])
                evac(QT[:, c0:c0 + 8, :], bankb[bi][0:65, 0:1024].rearrange("p (c a) -> p c a", a=128), [bank[bi]], [buf3])
            x0v = x0[:, :].rearrange("p (b a) -> p a b", a=64)
            hyv = hyo.rearrange("p (b a) -> p a b", a=64)
            for a0 in range(0, 64, 8):
                bi = nb()
                for u in range(8):
                    a = a0 + u
                    S.op("pe", "matmul", bank[bi][:, u * 64:(u + 1) * 64], lhsT=QT[:, :, a], rhs=tabAp[:, 0:64],
                         start=True, stop=False, reads=[buf3, tabAp], writes=[bank[bi]])
                    S.op("pe", "matmul", bank[bi][:, u * 64:(u + 1) * 64], lhsT=QT[:, :, 64 + a], rhs=tabAp[:, 64:128],
                         start=False, stop=True, reads=[buf3, tabAp], writes=[bank[bi]])
                S.op("dve", "tensor_tensor", out=hyv[:, a0:a0 + 8, :], in0=bank[bi][:].rearrange("p (a b) -> p a b", b=64),
                     in1=x0v[:, a0:a0 + 8, :], op=ALU.mult, reads=[bank[bi], x0], writes=[sq2])
            S.dma("sp", d.hyT_d[:, ch, :], hyo, reads=[sq2], writes=[("hyT_d", ch)])
        S.phase_end()


def attention_phase(S, nc, g, d):
    with contextlib.ExitStack() as st:
        def T(name, shape, dt):
            return st.enter_context(nc.sbuf_tensor("at" + name, shape, dt))

        def P(name):
            return S.reg_psum(st.enter_context(nc.psum_tensor("at" + name, [128, 512], F32)))

        hT = T("hT", [128, 8, L], BF16)
        hcT = T("hcT", [128, 8, LC], BF16)
        for kc in range(8):
            S.dma("sp", hT[:, kc, :], d.hT_d[:, kc, :], writes=[hT])
        S.dma("sp", hcT[:], d.hcT_d, writes=[hcT])
        if HYENA:
            hyena_proj(S, nc, g, d, hT, st)
        ropeC = T("ropeC", [128, L], BF16)
        ropeS = T("ropeS", [128, L], BF16)
        S.dma("pool", ropeC[:], d.ropec_d, writes=[ropeC])
        S.dma("pool", ropeS[:], d.ropes_d, writes=[ropeS])
        Rm = T("Rm", [128, 128], BF16)
        S.dma("pool", Rm[:], d.rotm_d, writes=[Rm])
        lq = T("lq", [128, 4, 64], F32)
        for i, ap in enumerate([d.lq1_d, d.lk1_d, d.lq2_d, d.lk2_d]):
            S.dma("sp", lq[:, i, :], ap.broadcast_to([128, 64]), writes=[lq])
        lprod = T("lprod", [128, 2, 64], F32)
        S.op("dve", "tensor_tensor", out=lprod[:, 0, :], in0=lq[:, 0, :], in1=lq[:, 1, :], op=ALU.mult,
             reads=[lq], writes=[lprod])
        S.op("dve", "tensor_tensor", out=lprod[:, 1, :], in0=lq[:, 2, :], in1=lq[:, 3, :], op=ALU.mult,
             reads=[lq], writes=[lprod])
        lsum = T("lsum", [128, 2], F32)
        S.op("dve", "reduce_sum", out=lsum[:], in_=lprod[:], axis=AX.X, reads=[lprod], writes=[lsum])
        S.op("act", "activation", out=lsum[:], in_=lsum[:], func=AF.Exp, reads=[lsum], writes=[lsum])
        nlam = T("nlam", [128, 1], F32)
        S.op("dve", "scalar_tensor_tensor", out=nlam[:], in0=lsum[:, 1:2], scalar=-LAM_INIT, in1=lsum[:, 0:1],
             op0=ALU.add, op1=ALU.subtract, reads=[lsum], writes=[nlam])

        wq = [T("wq%d" % i, [128, 8, 128], BF16) for i in range(2)]
        wk = [T("wk%d" % i, [128, 8, 128], BF16) for i in range(2)]
        wv = [T("wv%d" % i, [128, 8, 256], BF16) for i in range(2)]
        qT = T("qT", [128, L], BF16)
        kT = T("kT", [128, LK], BF16)
        NKT = LK // 128
        vaug = T("vaug", [128, NKT, 2, 132], BF16)
        S.op("pool", "memset", vaug[:], 1.0, writes=[vaug])
        attnT = [T("attnT%d" % i, [128, L], BF16) for i in range(2)]
        qsb = [T("qsb%d" % i, [128, 512], BF16) for i in range(2)]
        t1 = [T("t1_%d" % i, [128, 512], F32) for i in range(2)]
        t2 = [T("t2_%d" % i, [128, 512], F32) for i in range(2)]
        pT2 = [[T("pT%d_%d" % (m, i), [128, 512], BF16) for i in range(3)] for m in range(2)]
        ppair = [T("ppair%d" % m, [128, 512], BF16) for m in range(2)]
        pprev = [None, None]
        acc = [[T("acc%d_%d" % (m, i), [128, 512], F32) for i in range(1)] for m in range(2)]
        accb = T("accb", [128, 512], BF16)
        rr = T("rr", [128, 512], F32)
        om = [T("om%d" % m, [128, 512], F32) for m in range(2)]
        of_ = T("of", [128, 512], F32)
        psS2 = [[P("psS%d_%d" % (m, i)) for i in range(2)] for m in range(2)]
        OT = [P("OT%d" % m) for m in range(2)]
        psA = P("psA")
        psB = P("psB")
        w_in = d.w_in_d

        def load_w(h):
            i = h % 2
            for (w, c0) in ((wq[i], 3072), (wk[i], 4096)):
                S.dma("pool", w[:], w_in[:, c0 + h * 128:c0 + (h + 1) * 128].rearrange("(kc p) n -> p kc n", p=128),
                      writes=[w])
            if h % 2 == 0:
                w = wv[(h // 2) % 2]
                S.dma("pool", w[:], w_in[:, 5120 + h * 128:5120 + (h + 2) * 128].rearrange("(kc p) n -> p kc n", p=128),
                      writes=[w])

        load_w(0)
        ei = 0
        for h in range(ATH if STAGE >= 3 else 0):
            if h + 1 < ATH:
                load_w(h + 1)
            i2 = h % 2
            for (w, dstT) in ((wq[i2], qT), (wk[i2], kT)):
                for blk in range(L // 512 if ASUB >= 2 else 0):
                    cs = slice(blk * 512, (blk + 1) * 512)
                    for kc in range(8):
                        S.op("pe", "matmul", psA[:], lhsT=w[:, kc, :], rhs=hT[:, kc, cs], start=(kc == 0), stop=(kc == 7),
                             reads=[w, hT], writes=[psA])
                    qs_, ta, tb = qsb[ei % 2], t1[ei % 2], t2[ei % 2]
                    ei += 1
                    S.op("act", "copy", out=qs_[:], in_=psA[:], reads=[psA], writes=[qs_])
                    S.op("pe", "matmul", psB[:], lhsT=Rm[:], rhs=qs_[:], start=True, stop=True, reads=[Rm, qs_],
                         writes=[psB])
                    S.op("dve", "tensor_tensor", out=ta[:], in0=psA[:], in1=ropeC[:, cs], op=ALU.mult,
                         reads=[psA, ropeC], writes=[ta])
                    S.op("dve", "tensor_tensor", out=tb[:], in0=psB[:], in1=ropeS[:, cs], op=ALU.mult,
                         reads=[psB, ropeS], writes=[tb])
                    S.op("pool", "tensor_tensor", out=dstT[:, cs], in0=ta[:], in1=tb[:], op=ALU.add,
                         reads=[ta, tb], writes=[dstT])
            if ASUB < 3:
                continue
            for kc in range(8):
                S.op("pe", "matmul", psA[:, 0:LC], lhsT=wk[i2][:, kc, :], rhs=hcT[:, kc, :], start=(kc == 0), stop=(kc == 7),
                     reads=[wk[i2], hcT], writes=[psA])
            S.op("act", "copy", out=kT[:, L:LK], in_=psA[:, 0:LC], reads=[psA], writes=[kT])
            if h % 2 == 0:
                wvp = wv[(h // 2) % 2]
                for tp in range(0, NKT, 2):
                    for u in range(2):
                        kt = tp + u
                        src = hT[:, :, kt * 128:(kt + 1) * 128] if kt < L // 128 else hcT[:, :, (kt - L // 128) * 128:(kt - L // 128 + 1) * 128]
                        for kc in range(8):
                            S.op("pe", "matmul", psA[:, u * 256:(u + 1) * 256], lhsT=src[:, kc, :], rhs=wvp[:, kc, :],
                                 start=(kc == 0), stop=(kc == 7), reads=[hT, hcT, wvp], writes=[psA])
                    S.op("act", "copy", out=vaug[:, tp:tp + 2, :, 0:128],
                         in_=psA[:, :].rearrange("p (u hh e) -> p u hh e", hh=2, e=128), reads=[psA], writes=[vaug])
            vh = h % 2
            if ASUB < 4:
                continue
            aT = attnT[i2]
            for qb in range(L // 512):
                qcs = slice(qb * 512, (qb + 1) * 512)

                def qk(kt):
                    for m in range(2):
                        ms = slice(m * 64, (m + 1) * 64)
                        ps = psS2[m][kt % 2]
                        S.op("pe", "matmul", ps[:], lhsT=kT[ms, kt * 128:(kt + 1) * 128], rhs=qT[ms, qcs],
                             start=True, stop=True, reads=[kT, qT], writes=[ps])
                qk(0)
                for kt in range(NKT):
                    if kt + 1 < NKT:
                        qk(kt + 1)
                    for m in range(2):
                        p_ = pT2[m][kt % 3]
                        ps = psS2[m][kt % 2]
                        S.op("act", "activation", out=p_[:], in_=ps[:], func=AF.Exp, scale=0.125, reads=[ps], writes=[p_])
                        S.op("pe", "matmul", OT[m][:], lhsT=vaug[:, kt, vh, 0:128], rhs=p_[:], start=(kt == 0), stop=(kt == NKT - 1),
                             reads=[vaug, p_], writes=[OT[m]])
                        if kt % 2 == 1:
                            pr = ppair[m]
                            S.op("dve", "tensor_tensor", out=pr[:], in0=pprev[m][:], in1=p_[:], op=ALU.add,
                                 reads=[pprev[m], p_], writes=[pr])
                        pprev[m] = p_
                    if kt % 2 == 1:
                        for m in range(2):
                            pr = ppair[m]
                            if kt == 1:
                                S.op("dve", "tensor_copy", out=acc[m][0][:], in_=pr[:], reads=[pr], writes=[acc[m][0]])
                            else:
                                S.op("dve", "tensor_tensor", out=acc[m][0][:], in0=acc[m][0][:], in1=pr[:], op=ALU.add,
                                     reads=[acc[m][0], pr], writes=[acc[m][0]])
                for m in range(2):
                    S.op("dve", "tensor_copy", out=accb[:], in_=acc[m][0][:], reads=[acc[m][0]], writes=[accb])
                    S.op("pe", "matmul", psA[:], lhsT=g.onesbb[:], rhs=accb[:], start=True, stop=True, reads=[accb], writes=[psA])
                    S.op("act", "activation", out=rr[:], in_=psA[:], func=AF.Ln, reads=[psA], writes=[rr])
                    S.op("act", "activation", out=rr[:], in_=rr[:], func=AF.Exp, scale=-1.0, reads=[rr], writes=[rr])
                    S.op("dve", "tensor_tensor", out=om[m][:], in0=OT[m][:], in1=rr[:], op=ALU.mult, reads=[OT[m], rr],
                         writes=[om[m]])
                S.op("dve", "scalar_tensor_tensor", out=of_[:], in0=om[1][:], scalar=nlam[:, 0:1], in1=om[0][:], op0=ALU.mult,
                     op1=ALU.add, reads=[om[0], om[1], nlam], writes=[of_])
                S.op("act", "activation", out=accb[:], in_=of_[:], func=AF.Square, reads=[of_], writes=[accb])
                S.op("pe", "matmul", psA[:], lhsT=g.onesbb[:], rhs=accb[:], start=True, stop=True, reads=[accb], writes=[psA])
                S.op("act", "activation", out=rr[:], in_=psA[:], func=AF.Ln, scale=1.0 / 128, bias=g.eps_t[:, 0:1],
                     reads=[psA], writes=[rr])
                S.op("act", "activation", out=rr[:], in_=rr[:], func=AF.Exp, scale=-0.5, reads=[rr], writes=[rr])
                S.op("dve", "tensor_tensor", out=aT[:, qcs], in0=of_[:], in1=rr[:], op=ALU.mult, reads=[of_, rr], writes=[aT])
            S.dma("sp", d.attnT_d[:, h, :], aT[:], reads=[aT], writes=[("attnT_d", h)])
        S.phase_end()


def mixer_out_phase(S, nc, g, d):
    with contextlib.ExitStack() as st:
        def T(name, shape, dt):
            return st.enter_context(nc.sbuf_tensor("mo" + name, shape, dt))

        def P(name):
            return S.reg_psum(st.enter_context(nc.psum_tensor("mo" + name, [128, 512], F32)))

        wg = T("wg", [128, 8, 2048], BF16)
        why = T("why", [128, 8, D], BF16)
        wda = T("wda", [128, 8, D], BF16)
        wo_ = T("wo", [128, 8, D], BF16)
        wv_ = d.w_in_d.rearrange("(kc p) n -> p kc n", p=128)
        for kc in range(0, 8, 2):
            S.dma("pool", wg[:, kc:kc + 2, :], wv_[:, kc:kc + 2, 6144:8192], writes=[wg])
        for (w, ap) in ((why, d.w_hy_out_d), (wda, d.w_da_out_d), (wo_, d.w_o_d)):
            S.dma("pool", w[:], ap.rearrange("(kc p) n -> p kc n", p=128), writes=[w])
        sg_ = T("sg", [128, 1], F32)
        S.dma("sp", sg_[:], d.subln_g_d.rearrange("o p -> p o"), writes=[sg_], allow_slow_non_contiguous=True)
        S.op("dve", "tensor_scalar", out=why[:], in0=why[:], scalar1=0.5, scalar2=None, op0=ALU.mult,
             reads=[why], writes=[why])
        S.op("dve", "tensor_scalar", out=wda[:], in0=wda[:], scalar1=sg_[:, 0:1], scalar2=0.5 * (1.0 - LAM_INIT),
             op0=ALU.mult, op1=ALU.mult, reads=[wda, sg_], writes=[wda])
        TB = 256
        hTb = [T("hTb%d" % i, [128, 8, TB], BF16) for i in range(2)]
        hyb = [T("hyb%d" % i, [128, 8, TB], BF16) for i in range(2)]
        atb = [T("atb%d" % i, [128, 8, TB], BF16) for i in range(2)]
        mT = T("mT", [128, 8, TB], BF16)
        th = [T("th%d" % i, [128, 2 * TB], F32) for i in range(2)]
        mm = [T("mm%d" % i, [128, 2 * TB], F32) for i in range(2)]
        xres = [T("xres%d" % i, [128, D], F32) for i in range(2)]
        ytmp = [T("yt%d" % i, [128, D], F32) for i in range(2)]
        ss2 = [T("ss2_%d" % i, [128, 2], F32) for i in range(2)]
        rs2 = [T("rs2_%d" % i, [128, 1], F32) for i in range(2)]
        psY = [P("psY%d" % i) for i in range(2)]
        psG = [P("psG%d" % i) for i in range(2)]
        pso = [S.reg_psum(st.enter_context(nc.psum_tensor("mopso%d" % i, [128, D], F32))) for i in range(2)]
        ci = 0
        for b in range(L // TB):
            cs = slice(b * TB, (b + 1) * TB)
            hb, ab, hT = hyb[b % 2], atb[b % 2], hTb[b % 2]
            S.dma("sp", hT[:], d.hT_d[:, :, cs], writes=[hT])
            S.dma("sp", hb[:], d.hyT_d[:, :, cs], writes=[hb])
            S.dma("sp", ab[:], d.attnT_d[:, :, cs], writes=[ab])
            for dc in range(8):
                py, pg = psY[ci % 2], psG[ci % 2]
                th_, mm_ = th[ci % 2], mm[ci % 2]
                ci += 1
                dcs = slice(dc * 128, (dc + 1) * 128)
                for c in range(8):
                    S.op("pe", "matmul", py[:, 0:TB], lhsT=why[:, c, dcs], rhs=hb[:, c, :], start=(c == 0), stop=(c == 7),
                         reads=[why, hb], writes=[py])
                for c in range(8):
                    S.op("pe", "matmul", py[:, TB:2 * TB], lhsT=wda[:, c, dcs], rhs=ab[:, c, :], start=(c == 0), stop=(c == 7),
                         reads=[wda, ab], writes=[py])
                for half in range(2):
                    for kc in range(8):
                        S.op("pe", "matmul", pg[:, half * TB:(half + 1) * TB],
                             lhsT=wg[:, kc, half * 1024 + dc * 128:half * 1024 + (dc + 1) * 128], rhs=hT[:, kc, :],
                             start=(kc == 0), stop=(kc == 7), reads=[wg, hT], writes=[pg])
                S.op("act", "activation", out=th_[:], in_=pg[:], func=AF.Tanh, scale=0.5, reads=[pg], writes=[th_])
                S.op("dve", "scalar_tensor_tensor", out=mm_[:], in0=th_[:], scalar=1.0, in1=py[:], op0=ALU.add, op1=ALU.mult,
                     reads=[th_, py], writes=[mm_])
                S.op("pool", "tensor_tensor", out=mT[:, dc, :], in0=mm_[:, 0:TB], in1=mm_[:, TB:2 * TB], op=ALU.add,
                     reads=[mm_], writes=[mT])
            for ti in range(TB // 128):
                r0 = b * TB + ti * 128
                po = pso[ti % 2]
                for half in range(2):
                    for dc in range(8):
                        S.op("pe", "matmul", po[:, half * 512:(half + 1) * 512], lhsT=mT[:, dc, ti * 128:(ti + 1) * 128],
                             rhs=wo_[:, dc, half * 512:(half + 1) * 512], start=(dc == 0), stop=(dc == 7),
                             reads=[mT, wo_], writes=[po])
                xr = xres[ti % 2]
                S.dma("sp", xr[:], d.x1_d[r0:r0 + 128, :], writes=[xr])
                post_norm_residual(S, g, "mo", po, g.Gx[1], xr, ytmp[ti % 2], ss2[ti % 2], rs2[ti % 2],
                                   d.x2_d[r0:r0 + 128, :], ("x2_d", r0))
        S.phase_end()


def build(debug=False):
    nc = bass.Bass("TRN2", target_bir_lowering=False)

    def din(name, shape):
        return nc.dram_tensor(name, shape, F32, kind="ExternalInput").ap()

    x_d = din("x", [L, D])
    c_d = din("c", [1, D])
    ctx_d = din("ctx", [LC, D])
    cctx_d = din("c_ctx", [1, D])
    w_ada_d = din("w_ada", [D, 9 * D])
    b_ada_d = din("b_ada", [1, 9 * D])
    norm_g_d = din("norm_g", [6, D])
    w_ff_in_d = din("w_ff_in", [2, D, 2 * DFF])
    w_ff_out_d = din("w_ff_out", [2, DFF, D])
    w_in_d = din("w_in", [D, NPROJ])
    ident_d = din("ident", [128, 128])
    d = Ctx()
    d.w_in_d = w_in_d
    d.lq1_d = din("lambda_q1", [1, 64]); d.lk1_d = din("lambda_k1", [1, 64])
    d.lq2_d = din("lambda_q2", [1, 64]); d.lk2_d = din("lambda_k2", [1, 64])
    d.subln_g_d = din("subln_g", [1, 128])
    d.w_hy_out_d = din("w_hy_out", [D, D]); d.w_da_out_d = din("w_da_out", [D, D]); d.w_o_d = din("w_o", [D, D])
    d.hy_conv_w_d = din("hy_conv_w", [3, 3 * D]); d.hy_conv_b_d = din("hy_conv_b", [1, 3 * D])
    d.filt_w1_d = din("filt_w1", [33, 64]); d.filt_b1_d = din("filt_b1", [1, 64])
    d.filt_w2_d = din("filt_w2", [64, 64]); d.filt_b2_d = din("filt_b2", [1, 64])
    d.filt_w3_d = din("filt_w3", [64, 64]); d.filt_b3_d = din("filt_b3", [1, 64])
    d.filt_w4_d = din("filt_w4", [64, 2 * D]); d.filt_freq_d = din("filt_freq", [1, 64])
    d.hy_bias_d = din("hy_bias", [1, D])
    d.tabA_d = din("tabA", [128, 130]); d.tabB_d = din("tabB", [64, 65, 256])
    d.tabBp_d = din("tabBp", [128, 65, 128]); d.tabAp_d = din("tabAp", [65, 128])
    d.zemb_d = din("zemb", [33, 2 * L]); d.deltas_d = din("deltas", [128, 8]); d.tlin1_d = din("tlin1", [2, 8]); d.tlin2_d = din("tlin2", [2, 512])
    d.ropec_d = din("ropec", [128, L]); d.ropes_d = din("ropes", [128, L]); d.rotm_d = din("rotm", [128, 128])
    out_d = nc.dram_tensor("out", [L, D], F32, kind="ExternalOutput").ap()
    sk = "ExternalOutput" if debug else "Internal"
    x1_d = nc.dram_tensor("x1_s", [L, D], F32, kind=sk).ap()
    c1_d = nc.dram_tensor("c1_s", [LC, D], F32, kind=sk).ap()
    hT_d = nc.dram_tensor("hT_s", [128, 8, L], BF16, kind=sk).ap()
    hcT_d = nc.dram_tensor("hcT_s", [128, 8, LC], BF16, kind=sk).ap()
    d.hT_d, d.hcT_d, d.x1_d = hT_d, hcT_d, x1_d
    d.attnT_d = nc.dram_tensor("attnT_s", [128, 8, L], BF16, kind=sk).ap()
    d.hyT_d = nc.dram_tensor("hyT_s", [128, 8, L], BF16, kind=sk).ap()
    d.x2_d = nc.dram_tensor("x2_s", [L, D], F32, kind=sk).ap()
    d.u_d = nc.dram_tensor("u_s", [128, 8, L], BF16, kind=sk).ap()
    d.x0_d = nc.dram_tensor("x0_s", [128, 8, L], BF16, kind=sk).ap()

    with nc.cleanup_on_exit(), contextlib.ExitStack() as gst:
        S = Sched(nc)
        g = Ctx()

        def GT(name, shape, dt):
            return gst.enter_context(nc.sbuf_tensor(name, shape, dt))

        g.identb = GT("identb", [128, 128], BF16)
        g.eps_t = GT("eps_t", [128, 1], F32)
        g.onesb = GT("onesb", [1, 128], BF16)
        g.onesbb = GT("onesbb", [128, 128], BF16)
        g.junk = GT("junk", [128, D], BF16)
        g.Ax = [GT("Ax%d" % i, [128, 8], F32) for i in range(3)]
        g.Bx = [GT("Bx%d" % i, [128, 8], F32) for i in range(3)]
        g.Ac = [GT("Ac%d" % i, [128, 8], F32) for i in range(2)]
        g.Bc = [GT("Bc%d" % i, [128, 8], F32) for i in range(2)]
        g.Gx = [GT("Gx%d" % i, [128, D], F32) for i in range(3)]
        g.Gc = GT("Gc0", [128, D], F32)

        st_f1w = contextlib.ExitStack()
        f1w = ffn_alloc_wi(S, nc, st_f1w, "f1", w_ff_in_d[0], bg=True)

        with contextlib.ExitStack() as st:
            def T(name, shape, dt):
                return st.enter_context(nc.sbuf_tensor("p0" + name, shape, dt))

            identf = T("identf", [128, 128], F32)
            S.dma("sp", identf[:], ident_d, writes=[identf])
            S.op("dve", "tensor_copy", out=g.identb[:], in_=identf[:], reads=[identf], writes=[g.identb])
            S.op("dve", "memset", g.eps_t[:], EPS, writes=[g.eps_t])
            S.op("dve", "memset", g.onesb[:], 1.0, writes=[g.onesb])
            S.op("dve", "memset", g.onesbb[:], 1.0, writes=[g.onesbb])
            craw = T("craw", [128, 2, 8], F32)
            S.dma("sp", craw[:, 0, :], c_d.rearrange("o (kc p) -> p (o kc)", p=128), writes=[craw],
                  allow_slow_non_contiguous=True)
            S.dma("sp", craw[:, 1, :], cctx_d.rearrange("o (kc p) -> p (o kc)", p=128), writes=[craw],
                  allow_slow_non_contiguous=True)
            csil = T("csil", [128, 2, 8], F32)
            S.op("act", "activation", out=csil[:], in_=craw[:], func=AF.Silu, reads=[craw], writes=[csil])
            sT = T("sT", [128, 8, 2], BF16)
            for j in range(2):
                S.op("dve", "tensor_copy", out=sT[:, :, j], in_=csil[:, j, :], reads=[csil], writes=[sT])
            sbc = T("sbc", [128, 2, 8, 128], BF16)
            for j in range(2):
                for kc in range(8):
                    S.op("dve", "tensor_copy", out=sbc[:, j, kc, :], in_=csil[:, j, kc:kc + 1].broadcast_to([128, 128]),
                         reads=[csil], writes=[sbc])
            badaf = [T("badaf%d" % i, [1, D], F32) for i in range(2)]
            badab = [T("badab%d" % i, [1, D], BF16) for i in range(2)]
            gpre = T("gpre", [128, 3, 8], F32)
            for i in range(3):
                S.dma("sp", gpre[:, i, :], norm_g_d[2 * i:2 * i + 1, :].rearrange("o (kc p) -> p (o kc)", p=128),
                      writes=[gpre], allow_slow_non_contiguous=True)
            gpost = [T("gpost%d" % i, [128, D], F32) for i in range(3)]
            for i in range(3):
                S.dma("sp", gpost[i][:], norm_g_d[2 * i + 1:2 * i + 2, :].broadcast_to([128, D]), writes=[gpost[i]])
            wa = [T("wa%d" % i, [128, 8, D], BF16) for i in range(2)]
            waf = [T("waf%d" % i, [128, 2, D], F32) for i in range(3)]
            wfi = [0]
            modp = S.reg_psum(st.enter_context(nc.psum_tensor("p0modp", [128, 8, 2], F32)))
            psG = [S.reg_psum(st.enter_context(nc.psum_tensor("p0psG%d" % i, [128, 512], F32))) for i in range(2)]
            wav = w_ada_d.rearrange("(kc p) n -> p kc n", p=128)
            gi = 0
            for m in range(9):
                w = wa[m % 2]
                bada = badab[m % 2]
                S.dma("sp", badaf[m % 2][:], b_ada_d[0:1, m * D:(m + 1) * D], writes=[badaf[m % 2]])
                S.op("dve", "tensor_copy", out=bada[:], in_=badaf[m % 2][:], reads=[badaf[m % 2]], writes=[bada])
                for kc in range(0, 8, 2):
                    wf = waf[wfi[0] % 3]
                    S.dma("sp", wf[:], wav[:, kc:kc + 2, m * D:(m + 1) * D], writes=[wf])
                    if wfi[0] % 2 == 0:
                        S.op("dve", "tensor_copy", out=w[:, kc:kc + 2, :], in_=wf[:], reads=[wf], writes=[(w.name, kc)])
                    else:
                        S.op("act", "copy", out=w[:, kc:kc + 2, :], in_=wf[:], reads=[wf], writes=[(w.name, kc)])
                    wfi[0] += 1
                grp, which = m // 3, m % 3
                if which < 2:
                    for j in range(8):
                        for kc in range(8):
                            S.op("pe", "matmul", modp[:, j, :], lhsT=w[:, kc, j * 128:(j + 1) * 128], rhs=sT[:, kc, :],
                                 start=(kc == 0), stop=False, reads=[(w.name, (kc // 2) * 2), sT], writes=[modp])
                        S.op("pe", "matmul", modp[:, j, :], lhsT=bada[0:1, j * 128:(j + 1) * 128],
                             rhs=g.onesb[0:1, 0:2], start=False, stop=True, reads=[bada, g.onesb], writes=[modp])
                    if which == 0:
                        S.op("dve", "tensor_copy", out=g.Bx[grp][:], in_=modp[:, :, 0], reads=[modp], writes=[g.Bx[grp]])
                        if grp < 2:
                            S.op("dve", "tensor_copy", out=g.Bc[grp][:], in_=modp[:, :, 1], reads=[modp],
                                 writes=[g.Bc[grp]])
                    else:
                        S.op("dve", "scalar_tensor_tensor", out=g.Ax[grp][:], in0=modp[:, :, 0], scalar=1.0,
                             in1=gpre[:, grp, :], op0=ALU.add, op1=ALU.mult, reads=[modp, gpre], writes=[g.Ax[grp]])
                        if grp < 2:
                            S.op("dve", "scalar_tensor_tensor", out=g.Ac[grp][:], in0=modp[:, :, 1], scalar=1.0,
                                 in1=gpre[:, grp, :], op0=ALU.add, op1=ALU.mult, reads=[modp, gpre],
                                 writes=[g.Ac[grp]])
                else:
                    fac = 1.0 if grp == 1 else 0.5
                    targets = [(0, g.Gx[grp])] + ([(1, g.Gc)] if grp == 0 else [])
                    for (j, Gdst) in targets:
                        for half in range(2):
                            pg = psG[gi % 2]
                            gi += 1
                            for kc in range(8):
                                S.op("pe", "matmul", pg[:], lhsT=sbc[:, j, kc, :], rhs=w[:, kc, half * 512:(half + 1) * 512],
                                     start=(kc == 0), stop=False, reads=[(w.name, (kc // 2) * 2), sbc], writes=[pg])
                            S.op("pe", "matmul", pg[:], lhsT=g.onesb[0:1, :],
                                 rhs=bada[0:1, half * 512:(half + 1) * 512],
                                 start=False, stop=True, reads=[bada, g.onesb], writes=[pg])
                            S.op("dve", "scalar_tensor_tensor", out=Gdst[:, half * 512:(half + 1) * 512], in0=pg[:],
                                 scalar=fac, in1=gpost[grp][:, half * 512:(half + 1) * 512], op0=ALU.mult, op1=ALU.mult,
                                 reads=[pg, gpost[grp]], writes=[Gdst])
            S.phase_end(skip_bg=True)

        if debug:
            dbg_d = nc.dram_tensor("dbg", [128, 10, 8], F32, kind="ExternalOutput").ap()
            dbgG_d = nc.dram_tensor("dbgG", [4, 128, D], F32, kind="ExternalOutput").ap()
            for i, t in enumerate(g.Ax + g.Bx + g.Ac + g.Bc):
                S.dma("sp", dbg_d[:, i, :], t[:], reads=[t], writes=[("dbg", i)])
            for i, t in enumerate(g.Gx + [g.Gc]):
                S.dma("sp", dbgG_d[i], t[:], reads=[t], writes=[("dbgG", i)])
        if STAGE >= 1:
          ffn_phase(S, nc, g, "f1", w_ff_in_d[0], w_ff_out_d[0], preloaded=f1w, streams=[
            dict(T=LC, src=ctx_d, dst=c1_d, A=g.Ac[0], B=g.Bc[0], G=g.Gc,
                 nxt=dict(A=g.Ac[1], B=g.Bc[1], dst=hcT_d)),
        ] + ([dict(T=L, src=x_d, dst=x1_d, A=g.Ax[0], B=g.Bx[0], G=g.Gx[0],
                 nxt=dict(A=g.Ax[1], B=g.Bx[1], dst=hT_d))] if STAGE >= 2 else []))

        st_f1w.close()

        if STAGE >= 3:
            if not HYENA:
                with contextlib.ExitStack() as st:
                    z = st.enter_context(nc.sbuf_tensor("zt", [128, 8, 512], BF16))
                    S.op("dve", "memset", z[:], 0.0, writes=[z])
                    for i in range(L // 512):
                        S.dma("sp", d.hyT_d[:, :, i * 512:(i + 1) * 512], z[:], reads=[z], writes=[("hyz", i)])
                    S.phase_end()
            attention_phase(S, nc, g, d)
            if HYENA:
                hyena_fft_phase(S, nc, g, d)
        if STAGE >= 4:
            mixer_out_phase(S, nc, g, d)
        if STAGE >= 5:
            ffn_phase(S, nc, g, "f2", w_ff_in_d[1], w_ff_out_d[1], [
                dict(T=L, src=d.x2_d, dst=out_d, A=g.Ax[2], B=g.Bx[2], G=g.Gx[2], nxt=None)])
        nc.all_engine_barrier()
        print("instructions", S.n_ins, "waits", S.n_wait)
    return nc


_NC_CACHE = {}


def _host_consts():
    p = np.arange(128)
    dd = p % 64
    axis, half, fr = dd // 32, (dd % 32) // 16, dd % 16
    rotm = np.zeros((128, 128), np.float32)
    for po in range(128):
        if half[po] == 0:
            rotm[po + 16, po] = -1.0
        else:
            rotm[po - 16, po] = 1.0
    t = np.arange(L)
    pos = np.stack([(t // 64).astype(np.float32), (t % 64).astype(np.float32)], 0)
    inv = (np.float32(10000.0) ** (-np.arange(0, 32, 2, dtype=np.float32) / np.float32(32))).astype(np.float32)
    ang = pos[axis, :] * inv[fr][:, None]
    out = {"ident": np.eye(128, dtype=np.float32), "rotm": rotm,
           "ropec": np.cos(ang).astype(np.float32), "ropes": np.sin(ang).astype(np.float32)}
    N = 2 * L
    bp = np.arange(128)[:, None]; f2 = np.arange(65)[None, :]
    phi = 2 * np.pi * ((bp * f2) % 128) / 128
    out["tabA"] = np.concatenate([np.cos(phi), -np.sin(phi)], 1).astype(np.float32)
    ap = np.arange(64)[:, None, None]; f2_ = np.arange(65)[None, :, None]; f1 = np.arange(64)[None, None, :]
    th = 2 * np.pi * ((ap * (128 * f1 + f2_)) % N) / N
    out["tabB"] = np.concatenate([np.cos(th), -np.sin(th), np.sin(th), np.cos(th)], 2).astype(np.float32)
    thT = np.transpose(th, (2, 1, 0))
    top = np.concatenate([np.cos(thT), np.sin(thT)], 2)
    bot = np.concatenate([-np.sin(thT), np.cos(thT)], 2)
    out["tabBp"] = np.concatenate([top, bot], 0).astype(np.float32)
    w = np.full(65, 2.0); w[0] = 1; w[64] = 1
    phiT = 2 * np.pi * np.arange(65)[:, None] * np.arange(64)[None, :] / 128
    out["tabAp"] = (np.concatenate([w[:, None] * np.cos(phiT), -w[:, None] * np.sin(phiT)], 1) / N).astype(np.float32)
    tt = np.linspace(0.0, 1.0, L, dtype=np.float32)[None, :]
    ww = (2.0 * np.pi * np.arange(L, dtype=np.float32) / L)[None, :]
    ff = np.linspace(1e-4, 15, 16, dtype=np.float32)[:, None]
    zemb = np.concatenate([tt, np.cos(ff * ww), -np.sin(ff * ww)], 0).astype(np.float32)
    ridx = (L - np.arange(L)) % L
    out["zemb"] = np.ascontiguousarray(np.concatenate([zemb, zemb[:, ridx]], 1))
    deltas = np.abs(np.linspace(math.log(1e-2) / 1.5, math.log(1e-2) / 0.3, D, dtype=np.float32))
    out["deltas"] = np.ascontiguousarray(-deltas.reshape(8, 128).T).astype(np.float32)
    a64 = np.arange(64, dtype=np.float64)
    b8 = np.arange(8, dtype=np.float64); r512 = np.arange(512, dtype=np.float64)
    out["tlin1"] = np.stack([512 * b8 / (L - 1), (L - 512 * b8) / (L - 1)], 0).astype(np.float32)
    out["tlin2"] = np.stack([r512 / (L - 1), -r512 / (L - 1)], 0).astype(np.float32)
    return out


def make_in_maps(inputs, cores):
    consts = _host_consts()
    maps = []
    f = lambda a: np.ascontiguousarray(np.asarray(a, dtype=np.float32))
    for b in cores:
        m = {
            "x": f(inputs["x"][b]), "c": f(inputs["c"][b:b + 1]), "ctx": f(inputs["ctx"][b]),
            "c_ctx": f(np.asarray(inputs["c_ctx"]).reshape(1, D)),
            "w_ada": f(inputs["w_ada"][0]), "b_ada": f(inputs["b_ada"][0:1]), "norm_g": f(inputs["norm_g"][0]),
            "w_ff_in": f(inputs["w_ff_in"][0]), "w_ff_out": f(inputs["w_ff_out"][0]), "w_in": f(inputs["w_in"][0]),
            "lambda_q1": f(inputs["lambda_q1"][0:1]), "lambda_k1": f(inputs["lambda_k1"][0:1]),
            "lambda_q2": f(inputs["lambda_q2"][0:1]), "lambda_k2": f(inputs["lambda_k2"][0:1]),
            "subln_g": f(inputs["subln_g"][0:1]), "w_hy_out": f(inputs["w_hy_out"][0]),
            "hy_conv_w": f(inputs["hy_conv_w"][0]), "hy_conv_b": f(inputs["hy_conv_b"][0:1]),
            "filt_w1": f(inputs["filt_w1"][0]), "filt_b1": f(inputs["filt_b1"][0:1]),
            "filt_w2": f(inputs["filt_w2"][0]), "filt_b2": f(inputs["filt_b2"][0:1]),
            "filt_w3": f(inputs["filt_w3"][0]), "filt_b3": f(inputs["filt_b3"][0:1]),
            "filt_w4": f(inputs["filt_w4"][0]), "filt_freq": f(inputs["filt_freq"][0:1]),
            "hy_bias": f(inputs["hy_bias"][0:1]),
            "w_da_out": f(inputs["w_da_out"][0]), "w_o": f(inputs["w_o"][0]),
        }
        m.update(consts)
        maps.append(m)
    return maps


def kernel(**inputs):
    if "nc" not in _NC_CACHE:
        _NC_CACHE["nc"] = build()
    nc = _NC_CACHE["nc"]
    in_maps = make_in_maps(inputs, list(range(8)))
    res = run_bass_kernel_spmd(nc, in_maps, core_ids=list(range(8)))
    return np.stack([np.asarray(r["out"], dtype=np.float32) for r in res.results], axis=0)
```

```python
import contextlib
import math
import os
STAGE = int(os.environ.get('KSTAGE', '99'))
FSUB = int(os.environ.get('FSUB', '99'))
FV = os.environ.get('FV', '')
ATH = int(os.environ.get('ATH', '8'))
HYENA = int(os.environ.get('HYENA', '1'))
ASUB = int(os.environ.get('ASUB', '99'))
import numpy as np
import concourse.bass as bass
import concourse.mybir as mybir
from concourse.bass_utils import run_bass_kernel_spmd

F32 = mybir.dt.float32
BF16 = mybir.dt.bfloat16
AF = mybir.ActivationFunctionType
ALU = mybir.AluOpType
AX = mybir.AxisListType

D = 1024
L = 4096
LC = 256
LK = L + LC
DFF = 2816
NJ = DFF // 128
NH = 8
EPS = 1e-6
NPROJ = 8192
N_DMA_LANES = 40
N_BG_LANES = 24
LAM_INIT = 0.8 - 0.6 * math.exp(-0.3 * 0)


class Sched:
    def __init__(self, nc):
        self.nc = nc
        self.eng = {"pe": nc.tensor, "act": nc.scalar, "dve": nc.vector,
                    "pool": nc.gpsimd, "sp": nc.sync}
        self.sem = {}
        self.cnt = {}
        for e in self.eng:
            self.sem[e] = nc.alloc_semaphore("s_" + e)
            self.cnt[e] = 0
        for i in range(N_DMA_LANES):
            self.sem["d%d" % i] = nc.alloc_semaphore("d%d" % i)
            self.cnt["d%d" % i] = 0
        for i in range(N_BG_LANES):
            self.sem["b%d" % i] = nc.alloc_semaphore("b%d" % i)
            self.cnt["b%d" % i] = 0
        self.next_bg = 0
        self.next_lane = 0
        for k in self.sem:
            nc.gpsimd.sem_clear(self.sem[k])
        nc.all_engine_barrier()
        self.waited = {e: {} for e in self.eng}
        self.W = {}
        self.R = {}
        self.q = {e: [] for e in self.eng}
        self.n_ins = 0
        self.n_wait = 0
        self.snap = {}
        self.psum = set()

    def reg_psum(self, t):
        self.psum.add(t.name)
        return t

    def _wait(self, eng, deps):
        need = {}
        for (k, v) in deps:
            if v > need.get(k, 0):
                need[k] = v
        w = self.waited[eng]
        for k, v in need.items():
            if k == eng and eng in ("pe", "sp"):
                continue
            if w.get(k, 0) >= v:
                continue
            self.q[eng].append(("w", self.sem[k], v))
            self.n_wait += 1
            w[k] = v
            for kk, vv in self.snap.get((k, v), {}).items():
                if vv > w.get(kk, 0):
                    w[kk] = vv

    @staticmethod
    def _key(k):
        if isinstance(k, tuple):
            return tuple(Sched._key(x) for x in k)
        if isinstance(k, (str, int)):
            return k
        return k.name

    def _deps(self, reads, writes):
        deps = set()
        for k in reads:
            if k in self.W:
                deps.add(self.W[k])
        for k in writes:
            if k in self.W:
                deps.add(self.W[k])
            for ev in self.R.get(k, {}).items():
                deps.add(ev)
        return deps

    def _commit(self, ev, reads, writes):
        for k in reads:
            r = self.R.setdefault(k, {})
            if ev[1] > r.get(ev[0], 0):
                r[ev[0]] = ev[1]
        for k in writes:
            self.W[k] = ev
            self.R[k] = {}

    def op(self, eng, meth, *args, reads=(), writes=(), **kw):
        reads = [self._key(k) for k in reads]
        writes = [self._key(k) for k in writes]
        writes = writes + [k for k in reads if k in self.psum]
        reads = [k for k in reads if k not in self.psum]
        self._wait(eng, self._deps(reads, writes))
        fn = (lambda e, meth=meth, args=args, kw=kw: getattr(e, meth)(*args, **kw))
        self.cnt[eng] += 1
        self.q[eng].append(("i", fn, self.sem[eng], 1))
        self.n_ins += 1
        self.snap[(eng, self.cnt[eng])] = dict(self.waited[eng])
        self._commit((eng, self.cnt[eng]), reads, writes)

    def dma(self, eng, out, in_, reads=(), writes=(), bg=False, **kw):
        reads = [self._key(k) for k in reads]
        writes = [self._key(k) for k in writes]
        if bg:
            lane = "b%d" % self.next_bg
            self.next_bg = (self.next_bg + 1) % N_BG_LANES
        else:
            lane = "d%d" % self.next_lane
            self.next_lane = (self.next_lane + 1) % N_DMA_LANES
        deps = self._deps(reads, writes)
        if self.cnt[lane] > 0:
            deps.add((lane, self.cnt[lane]))
        self._wait(eng, deps)
        self.cnt[lane] += 16
        self.q[eng].append(("i", (lambda e, out=out, in_=in_, kw=kw: e.dma_start(out=out, in_=in_, **kw)),
                            self.sem[lane], 16))
        self.n_ins += 1
        self.snap[(lane, self.cnt[lane])] = dict(self.waited[eng])
        self._commit((lane, self.cnt[lane]), reads, writes)

    def emit(self):
        nc = self.nc
        q = self.q
        self.q = {e: [] for e in self.eng}

        def run(e, items):
            for it in items:
                if it[0] == "w":
                    e.wait_ge(it[1], it[2])
                else:
                    it[1](e).then_inc(it[2], it[3])

        with nc.Block() as block:
            @block.sync
            def _(e):
                run(e, q["sp"])

            @block.scalar
            def _(e):
                run(e, q["act"])

            @block.vector
            def _(e):
                run(e, q["dve"])

            @block.gpsimd
            def _(e):
                run(e, q["pool"])

            @block.tensor
            def _(e):
                run(e, q["pe"])

    def barrier(self, skip_bg=False):
        allev = set((k, v) for k, v in self.cnt.items() if v > 0 and not (skip_bg and k[0] == "b"))
        for e in self.eng:
            self._wait(e, allev)

    def phase_end(self, skip_bg=False):
        self.barrier(skip_bg)
        self.emit()


class Ctx:
    pass


def _pn_stats(S, g, xin, xin_key, bs):
    ss, rs, xs = bs["ss"], bs["rs"], bs["xs"]
    S.op("pool", "memset", ss[:], 0.0, writes=[ss])
    S.op("act", "activation", out=g.junk[:], in_=xin, func=AF.Square, accum_out=ss[:],
         reads=[xin_key, ss], writes=[ss])
    S.op("act", "activation", out=rs[:], in_=ss[:], func=AF.Ln, scale=1.0 / D, bias=g.eps_t[:, 0:1],
         reads=[ss], writes=[rs])
    S.op("act", "activation", out=rs[:], in_=rs[:], func=AF.Exp, scale=-0.5, reads=[rs], writes=[rs])
    S.op("dve", "tensor_scalar", out=xs[:], in0=xin, scalar1=rs[:, 0:1], scalar2=None, op0=ALU.mult,
         reads=[xin_key, rs], writes=[xs])


def _pn_transpose(S, g, bs, psT, A, B, dst, dst_key, col0):
    xs = bs["xs"]
    for kc in range(8):
        S.op("pe", "transpose", out=psT[:, kc, :], in_=xs[:, kc * 128:(kc + 1) * 128], identity=g.identb[:],
             reads=[xs], writes=[psT])
    for kc in range(8):
        S.op("act", "activation", out=dst[:, kc, col0:col0 + 128], in_=psT[:, kc, :], func=AF.Identity,
             scale=A[:, kc:kc + 1], bias=B[:, kc:kc + 1], reads=[psT], writes=[dst_key])


def ffn_alloc_wi(S, nc, st, tag, w_in_d, from_bf16=False, bg=False):
    wi = st.enter_context(nc.sbuf_tensor(tag + "wi", [128, 8, 2 * DFF], BF16))
    q = "sp" if from_bf16 else "pool"
    for kc in range(8):
        S.dma(q, wi[:, kc, :], w_in_d[kc * 128:(kc + 1) * 128, :], reads=[("wsrc", tag, "i", kc)], writes=[wi], bg=bg)
    return wi


def ffn_alloc_wo(S, nc, st, tag, w_out_d, from_bf16=False, bg=False):
    wo = st.enter_context(nc.sbuf_tensor(tag + "wo", [128, NJ, D], BF16))
    q = "sp" if from_bf16 else "pool"
    wov = w_out_d.rearrange("(j p) n -> p j n", p=128)
    for j0 in range(0, NJ, 2):
        S.dma(q, wo[:, j0:j0 + 2, :], wov[:, j0:j0 + 2, :], reads=[("wsrc", tag, "o", j0)], writes=[wo], bg=bg)
    return wo


def ffn_phase(S, nc, g, tag, w_in_d, w_out_d, streams, preloaded=None, from_bf16=False):
    with contextlib.ExitStack() as st:
        def T(name, shape, dt):
            return st.enter_context(nc.sbuf_tensor(tag + name, shape, dt))

        def P(name, shape, dt):
            return S.reg_psum(st.enter_context(nc.psum_tensor(tag + name, shape, dt)))

        g.psT = [P("psT%d" % i, [128, 8, 128], BF16) for i in range(2)]
        wi = preloaded if preloaded is not None else ffn_alloc_wi(S, nc, st, tag, w_in_d, from_bf16=from_bf16)
        wo = ffn_alloc_wo(S, nc, st, tag, w_out_d, from_bf16=from_bf16)
        TB = 256
        xin = [T("xin%d" % i, [128, D], F32) for i in range(2)]
        xres = [T("xres%d" % i, [128, D], F32) for i in range(1)]
        xT = [T("xT%d" % i, [128, 8, TB], BF16) for i in range(2)]
        hT = T("hT", [128, NJ, TB], BF16)
        sg = [T("sg%d" % i, [128, TB], F32) for i in range(2)]
        ytmp = [T("yt%d" % i, [128, D], F32) for i in range(2)]
        h2 = [T("h2_%d" % i, [128, 8, 128], BF16) for i in range(2)]
        ss2 = [T("ss2_%d" % i, [128, 2], F32) for i in range(2)]
        rs2 = [T("rs2_%d" % i, [128, 1], F32) for i in range(2)]
        psgu = [P("psgu%d" % i, [128, 2, TB], F32) for i in range(2)]
        pso = [P("pso%d" % i, [128, D], F32) for i in range(2)]

        cnt = dict(gu=0)
        blocks = []
        for sdef in streams:
            nblk = (sdef["T"] + TB - 1) // TB
            for b_ in range(nblk):
                blocks.append((sdef, b_))
        bs_pre = [dict(ss=T("ssp%d" % i, [128, 1], F32), rs=T("rsp%d" % i, [128, 1], F32), xs=T("xsp%d" % i, [128, D], BF16))
                  for i in range(2)]
        bs_nxt = [dict(ss=T("ssn%d" % i, [128, 1], F32), rs=T("rsn%d" % i, [128, 1], F32), xs=T("xsn%d" % i, [128, D], BF16))
                  for i in range(2)]

        def geom(bi):
            sdef, b_ = blocks[bi]
            t0 = b_ * TB
            return sdef, t0, min(TB, sdef["T"] - t0) // 128

        def stats(bi):
            sdef, t0, ntile = geom(bi)
            for ti in range(ntile):
                r0 = t0 + ti * 128
                xi = xin[ti % 2]
                S.dma("sp", xi[:], sdef["src"][r0:r0 + 128, :], writes=[xi])
                _pn_stats(S, g, xi[:], xi.name, bs_pre[ti % 2])

        def transposes(bi):
            sdef, t0, ntile = geom(bi)
            xTb = xT[bi % 2]
            for ti in range(ntile):
                _pn_transpose(S, g, bs_pre[ti % 2], g.psT[ti % 2], sdef["A"], sdef["B"], xTb, xTb.name, ti * 128)

        def gateup(bi):
            sdef, t0, ntile = geom(bi)
            nt = ntile * 128
            xTb = xT[bi % 2]
            for j in range(NJ):
                pg = psgu[cnt["gu"] % 2]
                sgi = sg[cnt["gu"] % 2]
                cnt["gu"] += 1
                for kc in range(8):
                    S.op("pe", "matmul", pg[:, 0, :nt], lhsT=wi[:, kc, j * 128:(j + 1) * 128], rhs=xTb[:, kc, :nt],
                         start=(kc == 0), stop=(kc == 7), reads=[wi, xTb], writes=[pg])
                for kc in range(8):
                    S.op("pe", "matmul", pg[:, 1, :nt], lhsT=wi[:, kc, DFF + j * 128:DFF + (j + 1) * 128],
                         rhs=xTb[:, kc, :nt], start=(kc == 0), stop=(kc == 7), reads=[wi, xTb], writes=[pg])
                S.op("act", "activation", out=sgi[:, :nt], in_=pg[:, 0, :nt], func=AF.Silu, reads=[pg], writes=[sgi])
                S.op("dve", "tensor_tensor", out=hT[:, j, :nt], in0=sgi[:, :nt], in1=pg[:, 1, :nt], op=ALU.mult,
                     reads=[sgi, pg], writes=[hT])

        def outmm(bi):
            sdef, t0, ntile = geom(bi)
            for ti in range(ntile):
                po = pso[ti % 2]
                for half in range(2):
                    for j in range(NJ):
                        S.op("pe", "matmul", po[:, half * 512:(half + 1) * 512],
                             lhsT=hT[:, j, ti * 128:(ti + 1) * 128], rhs=wo[:, j, half * 512:(half + 1) * 512],
                             start=(j == 0), stop=(j == NJ - 1), reads=[hT, wo], writes=[po])

        def post(bi):
            sdef, t0, ntile = geom(bi)
            for ti in range(ntile):
                r0 = t0 + ti * 128
                xr = xres[0]
                S.dma("sp", xr[:], sdef["src"][r0:r0 + 128, :], writes=[xr])
                yt = ytmp[ti % 2]
                post_norm_residual(S, g, tag, pso[ti % 2], sdef["G"], xr, yt, ss2[ti % 2], rs2[ti % 2],
                                   sdef["dst"][r0:r0 + 128, :], (tag, "dst", sdef["T"], r0))
                if sdef.get("nxt") is not None:
                    _pn_stats(S, g, yt[:], yt.name, bs_nxt[ti % 2])

        def nxt_transposes(bi):
            sdef, t0, ntile = geom(bi)
            if sdef.get("nxt") is None:
                return
            nx = sdef["nxt"]
            for ti in range(ntile):
                r0 = t0 + ti * 128
                hh = h2[ti % 2]
                _pn_transpose(S, g, bs_nxt[ti % 2], g.psT[ti % 2], nx["A"], nx["B"], hh, hh.name, 0)
                S.dma("sp", nx["dst"][:, :, r0:r0 + 128], hh[:], reads=[hh], writes=[(tag, "hdst", sdef["T"], r0)])

        nb = len(blocks)
        stats(0)
        transposes(0)
        for bi in range(nb):
            if bi + 1 < nb:
                stats(bi + 1)
            gateup(bi)
            if bi + 1 < nb:
                transposes(bi + 1)
            if bi >= 1:
                nxt_transposes(bi - 1)
            outmm(bi)
            post(bi)
        nxt_transposes(nb - 1)
        S.phase_end()


def post_norm_residual(S, g, tag, po, G, xr, yt, s2, r2, dst_ap, dst_key):
    S.op("pool", "memset", s2[:], 0.0, writes=[s2])
    for half in range(2):
        S.op("act", "activation", out=g.junk[:, 0:512], in_=po[:, half * 512:(half + 1) * 512],
             func=AF.Square, accum_out=s2[:, half:half + 1], reads=[po, s2], writes=[s2])
    S.op("dve", "tensor_tensor", out=r2[:], in0=s2[:, 0:1], in1=s2[:, 1:2], op=ALU.add, reads=[s2], writes=[r2])
    S.op("act", "activation", out=r2[:], in_=r2[:], func=AF.Ln, scale=1.0 / D, bias=g.eps_t[:, 0:1],
         reads=[r2], writes=[r2])
    S.op("act", "activation", out=r2[:], in_=r2[:], func=AF.Exp, scale=-0.5, reads=[r2], writes=[r2])
    S.op("dve", "scalar_tensor_tensor", out=yt[:], in0=po[:], scalar=r2[:, 0:1], in1=G[:],
         op0=ALU.mult, op1=ALU.mult, reads=[po, r2, G], writes=[yt])
    S.op("pool", "tensor_tensor", out=yt[:], in0=yt[:], in1=xr[:], op=ALU.add, reads=[yt, xr], writes=[yt])
    S.dma("sp", dst_ap, yt[:], reads=[yt], writes=[dst_key])


def hyena_proj(S, nc, g, d, hT, st_outer):
    with contextlib.ExitStack() as st:
        def T(name, shape, dt):
            return st.enter_context(nc.sbuf_tensor("hp" + name, shape, dt))

        psA = S.reg_psum(st.enter_context(nc.psum_tensor("hppsA", [128, 512], F32)))
        psA2 = S.reg_psum(st.enter_context(nc.psum_tensor("hppsA2", [128, 512], F32)))
        pss = [psA, psA2]
        cw = T("cw", [128, 3, 3, 8], F32)
        cb = T("cb", [128, 3, 8], F32)
        for j in range(3):
            for p in range(3):
                S.dma("sp", cw[:, j, p, :], d.hy_conv_w_d[j:j + 1, p * D:(p + 1) * D].rearrange("o (ch q) -> q (o ch)", q=128),
                      writes=[cw], allow_slow_non_contiguous=True)
        for p in range(3):
            S.dma("sp", cb[:, p, :], d.hy_conv_b_d[0:1, p * D:(p + 1) * D].rearrange("o (ch q) -> q (o ch)", q=128),
                  writes=[cb], allow_slow_non_contiguous=True)
        w3 = [T("w3_%d" % i, [128, 8, 3, 128], BF16) for i in range(2)]
        zhs = [T("zh%d" % i, [128, L + 2], F32) for i in range(2)]
        zc = [T("zc%d" % i, [128, L], F32) for i in range(2)]
        ob = [T("ob%d" % i, [128, L], BF16) for i in range(2)]
        for zh in zhs:
            S.op("pool", "memset", zh[:, 0:1], 0.0, writes=[zh])
            S.op("pool", "memset", zh[:, L + 1:L + 2], 0.0, writes=[zh])
        zi = 0
        wv_ = d.w_in_d.rearrange("(kc p) n -> p kc n", p=128)

        def load_w(ch):
            for p in range(3):
                S.dma("pool", w3[ch % 2][:, :, p, :], wv_[:, :, p * D + ch * 128:p * D + (ch + 1) * 128], writes=[w3[ch % 2]])
        load_w(0)
        pi = 0
        for ch in range(8):
            if ch + 1 < 8:
                load_w(ch + 1)
            w = w3[ch % 2]
            for part in (0, 1, 2):
                zh = zhs[zi % 2]
                zi += 1
                for blk in range(L // 512):
                    ps = pss[pi % 2]
                    pi += 1
                    for kc in range(8):
                        S.op("pe", "matmul", ps[:], lhsT=w[:, kc, part, :], rhs=hT[:, kc, blk * 512:(blk + 1) * 512],
                             start=(kc == 0), stop=(kc == 7), reads=[w, hT], writes=[ps])
                    S.op("act", "copy", out=zh[:, 1 + blk * 512:1 + (blk + 1) * 512], in_=ps[:], reads=[ps], writes=[zh])
                z = zc[1] if part == 1 else zc[0]
                S.op("act", "activation", out=z[:], in_=zh[:, 1:L + 1], func=AF.Identity, scale=cw[:, 1, part, ch:ch + 1],
                     bias=cb[:, part, ch:ch + 1], reads=[zh, cw, cb], writes=[z])
                S.op("dve", "scalar_tensor_tensor", out=z[:], in0=zh[:, 0:L], scalar=cw[:, 0, part, ch:ch + 1], in1=z[:],
                     op0=ALU.mult, op1=ALU.add, reads=[zh, cw, z], writes=[z])
                S.op("dve", "scalar_tensor_tensor", out=z[:], in0=zh[:, 2:L + 2], scalar=cw[:, 2, part, ch:ch + 1], in1=z[:],
                     op0=ALU.mult, op1=ALU.add, reads=[zh, cw, z], writes=[z])
                if part == 0:
                    S.op("act", "copy", out=ob[0][:], in_=z[:], reads=[z], writes=[ob[0]])
                    S.dma("sp", d.x0_d[:, ch, :], ob[0][:], reads=[ob[0]], writes=[("x0_d", ch)])
                elif part == 2:
                    S.op("pool", "tensor_tensor", out=ob[1][:], in0=z[:], in1=zc[1][:], op=ALU.mult, reads=[z, zc[1]],
                         writes=[ob[1]])
                    S.dma("sp", d.u_d[:, ch, :], ob[1][:], reads=[ob[1]], writes=[("u_d", ch)])
        S.barrier()


def hyena_fft_phase(S, nc, g, d):
    TWO_PI = 2.0 * math.pi
    MAGIC = 12582912.0
    with contextlib.ExitStack() as st:
        def T(name, shape, dt):
            return st.enter_context(nc.sbuf_tensor("hf" + name, shape, dt))

        bank = [S.reg_psum(st.enter_context(nc.psum_tensor("hfbank%d" % i, [128, 512], F32))) for i in range(8)]
        bankb = [b[:].bitcast(BF16) for b in bank]
        tabA = T("tabA", [128, 130], BF16)
        tabB = T("tabB", [64, 65, 256], BF16)
        tabBp = T("tabBp", [128, 65, 128], BF16)
        tabAp = T("tabAp", [65, 128], BF16)
        S.dma("pool", tabA[:], d.tabA_d, writes=[tabA])
        for f0 in range(0, 65, 13):
            S.dma("pool", tabB[:, f0:f0 + 13, :], d.tabB_d[:, f0:f0 + 13, :], writes=[tabB])
            S.dma("pool", tabBp[:, f0:f0 + 13, :], d.tabBp_d[:, f0:f0 + 13, :], writes=[tabBp])
        S.dma("pool", tabAp[:], d.tabAp_d, writes=[tabAp])
        w4 = T("w4", [64, 2 * D], BF16)
        S.dma("pool", w4[:], d.filt_w4_d, writes=[w4])
        hA = T("hA", [64, 2 * L], BF16)
        with contextlib.ExitStack() as st2:
            def T2(name, shape, dt):
                return st2.enter_context(nc.sbuf_tensor("hf" + name, shape, dt))
            zT = T2("zT", [33, 2 * L], BF16)
            S.dma("pool", zT[:], d.zemb_d, writes=[zT])
            w1 = T2("w1", [33, 64], BF16)
            w2 = T2("w2", [64, 64], BF16)
            w3_ = T2("w3", [64, 64], BF16)
            S.dma("pool", w1[:], d.filt_w1_d, writes=[w1])
            S.dma("pool", w2[:], d.filt_w2_d, writes=[w2])
            S.dma("pool", w3_[:], d.filt_w3_d, writes=[w3_])
            fb = T2("fb", [64, 4], F32)
            for i, ap in enumerate([d.filt_b1_d, d.filt_b2_d, d.filt_b3_d, d.filt_freq_d]):
                S.dma("sp", fb[:, i:i + 1], ap.rearrange("o p -> p o"), writes=[fb], allow_slow_non_contiguous=True)
            fbb = T2("fbb", [64, 3], F32)
            S.op("dve", "tensor_scalar", out=fbb[:], in0=fb[:, 0:3], scalar1=fb[:, 3:4], scalar2=None, op0=ALU.mult,
                 reads=[fb], writes=[fbb])
            hB = T2("hB", [64, 2 * L], BF16)
            ra = [T2("ra%d" % i, [64, 512], F32) for i in range(2)]
            rb = [T2("rb%d" % i, [64, 512], F32) for i in range(2)]
            li = 0
            for layer, (wl, src, dst) in enumerate(((w1, zT, hA), (w2, hA, hB), (w3_, hB, hA))):
                kdim = 33 if layer == 0 else 64
                for blk in range(2 * L // 512):
                    cs = slice(blk * 512, (blk + 1) * 512)
                    ps = bank[6 + li % 2]
                    a_, b_ = ra[li % 2], rb[li % 2]
                    li += 1
                    S.op("pe", "matmul", ps[0:64, :], lhsT=wl[0:kdim, :], rhs=src[0:kdim, cs], start=True, stop=True,
                         reads=[wl, src], writes=[ps])
                    S.op("act", "activation", out=a_[:], in_=ps[0:64, :], func=AF.Identity, scale=fb[:, 3:4],
                         bias=fbb[:, layer:layer + 1], reads=[ps, fb, fbb], writes=[a_])
                    S.op("dve", "tensor_scalar", out=b_[:], in0=a_[:], scalar1=1.0 / TWO_PI, scalar2=MAGIC, op0=ALU.mult,
                         op1=ALU.add, reads=[a_], writes=[b_])
                    S.op("dve", "tensor_scalar", out=b_[:], in0=b_[:], scalar1=MAGIC, scalar2=-TWO_PI, op0=ALU.subtract,
                         op1=ALU.mult, reads=[b_], writes=[b_])
                    S.op("dve", "tensor_tensor", out=a_[:], in0=a_[:], in1=b_[:], op=ALU.add, reads=[a_, b_], writes=[a_])
                    S.op("dve", "tensor_scalar", out=a_[:], in0=a_[:], scalar1=3.1415925, scalar2=-3.1415925, op0=ALU.min,
                         op1=ALU.max, reads=[a_], writes=[a_])
                    S.op("act", "activation", out=dst[:, cs], in_=a_[:], func=AF.Sin, reads=[a_], writes=[dst])
            S.barrier()
        h3 = hA
        dl = T("dl", [128, 8], F32)
        S.dma("sp", dl[:], d.deltas_d, writes=[dl])
        tl1 = T("tl1", [128, 2, 8], F32)
        tl2 = T("tl2", [128, 512], F32)
        for j in range(2):
            S.dma("sp", tl1[:, j, :], d.tlin1_d[j:j + 1, :].broadcast_to([128, 8]), writes=[tl1])
        S.dma("sp", tl2[:], d.tlin2_d[0:1, :].broadcast_to([128, 512]), writes=[tl2])
        ndl = T("ndl", [128, 8], F32)
        S.op("dve", "tensor_scalar", out=ndl[:], in0=dl[:], scalar1=-1.0, scalar2=None, op0=ALU.mult, reads=[dl], writes=[ndl])
        hbias = T("hbias", [128, 8], F32)
        S.dma("sp", hbias[:], d.hy_bias_d.rearrange("o (ch q) -> q (o ch)", q=128), writes=[hbias],
              allow_slow_non_contiguous=True)
        E1 = T("E1", [128, 2, 8], F32)
        E2 = T("E2", [128, 2, 512], F32)
        sq2 = T("sq2", [128, 2, L], BF16)
        x0 = T("x0", [128, L], BF16)
        hyo = sq2[:, 1, :]
        buf1 = T("buf1", [128, 65 * 128], BF16)
        buf2 = T("buf2", [128, 65 * 128], BF16)
        buf3 = T("buf3", [128, 128 * 130], BF16)
        Kf = T("Kf", [128, 65, 128], BF16)
        P1 = [T("P1_%d" % i, [128, 4, 128], F32) for i in range(2)]
        P2 = [T("P2_%d" % i, [128, 4, 128], F32) for i in range(1)]
        asum = T("asum", [128, 17], F32)
        nrm = T("nrm", [128, 1], F32)
        UB = buf1[0:64, 0:64 * 128].rearrange("p (a c) -> p a c", c=128)
        UBf = buf1[:, 0:64 * 128].rearrange("p (a c) -> p a c", c=128)
        YT = buf1[:, :].rearrange("p (f c) -> p f c", c=128)
        Y = buf2[:, :].rearrange("p (f k) -> p f k", k=128)
        Q = Y
        VA = buf3[0:64, :].rearrange("p (c k) -> p c k", k=130)
        QT = buf3[0:65, 0:128 * 128].rearrange("p (c a) -> p c a", a=128)
        ev = [0]
        rot = [0]

        def nb():
            rot[0] += 1
            return (rot[0] - 1) % 8

        def evac(out, in_, reads, writes):
            if ev[0] % 4 != 3:
                S.op("act", "copy", out=out, in_=in_, reads=reads, writes=writes)
            else:
                S.op("dve", "tensor_copy", out=out, in_=in_, reads=reads, writes=writes)
            ev[0] += 1

        def forward(full, consume):
            K = 128 if full else 64
            sv = sq2[:, :, :].rearrange("p h (b a) -> p a h b", a=64)
            ub = UBf if full else UB
            for a0 in range(0, 64, 8):
                bi = nb()
                for u in range(8):
                    in_ = sv[:, a0 + u, :, :] if full else sv[:, a0 + u, 0, :]
                    S.op("pe", "transpose", out=bankb[bi][0:K, u * 128:(u + 1) * 128], in_=in_,
                         identity=g.identb[:], reads=[sq2], writes=[bank[bi]])
                evac(ub[:, a0:a0 + 8, :], bankb[bi][0:K, 0:1024].rearrange("p (a c) -> p a c", c=128), [bank[bi]], [buf1])
            for c0 in range(0, 128, 3):
                n = min(3, 128 - c0)
                bi = nb()
                for u in range(n):
                    S.op("pe", "matmul", bank[bi][0:64, u * 130:(u + 1) * 130], lhsT=ub[:, :, c0 + u], rhs=tabA[0:K, :],
                         start=True, stop=True, reads=[buf1, tabA], writes=[bank[bi]])
                evac(VA[:, c0:c0 + n, :], bank[bi][0:64, 0:n * 130].rearrange("p (c k) -> p c k", k=130), [bank[bi]], [buf3])
            for f0 in range(0, 65, 4):
                n = min(4, 65 - f0)
                bi = nb()
                for u in range(n):
                    f = f0 + u
                    S.op("pe", "matmul", bank[bi][:, u * 128:(u + 1) * 128], lhsT=VA[:, :, f], rhs=tabB[:, f, 0:128],
                         start=True, stop=False, reads=[buf3, tabB], writes=[bank[bi]])
                    S.op("pe", "matmul", bank[bi][:, u * 128:(u + 1) * 128], lhsT=VA[:, :, 65 + f], rhs=tabB[:, f, 128:256],
                         start=False, stop=True, reads=[buf3, tabB], writes=[bank[bi]])
                consume(bank[bi], f0, n)

        for ch in range(8):
            for j in range(2):
                S.op("act", "activation", out=E1[:, j, :], in_=tl1[:, j, :], func=AF.Exp, scale=dl[:, ch:ch + 1],
                     reads=[tl1, dl], writes=[E1])
                S.op("act", "activation", out=E2[:, j, :], in_=tl2[:], func=AF.Exp,
                     scale=(dl if j == 0 else ndl)[:, ch:ch + 1], reads=[tl2, dl, ndl], writes=[E2])
            S.op("pool", "memset", asum[:], 0.0, writes=[asum, sq2] + [(sq2.name, fi_, b_) for fi_ in range(2) for b_ in range(8)])
            for fi in range(2):
                for blk in range(L // 512):
                    ps = bank[nb()]
                    S.op("pe", "matmul", ps[:], lhsT=w4[:, fi * D + ch * 128:fi * D + (ch + 1) * 128], rhs=h3[:, fi * L + blk * 512:fi * L + (blk + 1) * 512],
                         start=True, stop=True, reads=[w4, h3], writes=[ps])
                    sqv = sq2[:, fi, blk * 512:(blk + 1) * 512]
                    S.op("dve", "scalar_tensor_tensor", out=sqv, in0=ps[:], scalar=E1[:, fi, blk:blk + 1], in1=E2[:, fi, :],
                         op0=ALU.mult, op1=ALU.mult, reads=[ps, E1, E2], writes=[(sq2.name, fi, blk)])
                    if fi == 1 and blk == 0:
                        S.op("dve", "memset", sq2[:, 1, 0:1], 0.0, writes=[(sq2.name, fi, blk)])
                    S.op("act", "activation", out=g.junk[:, 0:512], in_=sqv, func=AF.Abs,
                         accum_out=asum[:, fi * 8 + blk:fi * 8 + blk + 1], reads=[(sq2.name, fi, blk), asum], writes=[asum])
            S.op("pool", "memset", asum[:, 16:17], 0.0, reads=[(sq2.name, fi_, b_) for fi_ in range(2) for b_ in range(8)], writes=[sq2])
            S.op("dve", "reduce_sum", out=nrm[:], in_=asum[:, 0:16], axis=AX.X, reads=[asum], writes=[nrm])
            S.op("dve", "tensor_scalar", out=nrm[:], in0=nrm[:], scalar1=EPS, scalar2=None, op0=ALU.add, reads=[nrm], writes=[nrm])
            S.op("dve", "reciprocal", out=nrm[:], in_=nrm[:], reads=[nrm], writes=[nrm])

            def cons_f(bk, f0, n):
                bv = bk[:, 0:n * 128].rearrange("p (f k) -> p f k", k=128)
                S.op("act", "activation", out=Kf[:, f0:f0 + n, 0:64], in_=bv[:, :, 0:64], func=AF.Identity, scale=nrm[:, 0:1],
                     bias=hbias[:, ch:ch + 1], reads=[bk, nrm, hbias], writes=[Kf])
                S.op("act", "activation", out=Kf[:, f0:f0 + n, 64:128], in_=bv[:, :, 64:128], func=AF.Identity, scale=nrm[:, 0:1],
                     reads=[bk, nrm], writes=[Kf])

            def cons_b(bk, f0, n):
                bv = bk[:, 0:n * 128].rearrange("p (f k) -> p f k", k=128)
                S.op("dve", "scalar_tensor_tensor", out=Kf[:, f0:f0 + n, 0:64], in0=bv[:, :, 0:64], scalar=nrm[:, 0:1],
                     in1=Kf[:, f0:f0 + n, 0:64], op0=ALU.mult, op1=ALU.add, reads=[bk, nrm, Kf], writes=[Kf])
                S.op("dve", "scalar_tensor_tensor", out=Kf[:, f0:f0 + n, 64:128], in0=bv[:, :, 64:128], scalar=nrm[:, 0:1],
                     in1=Kf[:, f0:f0 + n, 64:128], op0=ALU.mult, op1=ALU.subtract, reads=[bk, nrm, Kf], writes=[Kf])
                S.op("pool", "tensor_scalar", out=Kf[:, f0:f0 + n, 64:128], in0=Kf[:, f0:f0 + n, 64:128], scalar1=-1.0,
                     scalar2=None, op0=ALU.mult, reads=[Kf], writes=[Kf])

            forward(True, cons_f)
            S.dma("sp", sq2[:, 0, :], d.u_d[:, ch, :], writes=[sq2])
            S.dma("sp", x0[:], d.x0_d[:, ch, :], writes=[x0])
            pc = [0]

            def cons_u(bk, f0, n):
                p1, p2 = P1[pc[0] % 2], P2[0]
                pc[0] += 1
                bv = bk[:, 0:n * 128].rearrange("p (f r k) -> p f r k", r=2, k=64)
                kr = Kf[:, f0:f0 + n, 0:64].unsqueeze(2).broadcast_to([128, n, 2, 64])
                ki = Kf[:, f0:f0 + n, 64:128].unsqueeze(2).broadcast_to([128, n, 2, 64])
                p1v = p1[:, 0:n, :].rearrange("p f (r k) -> p f r k", r=2)
                p2v = p2[:, 0:n, :].rearrange("p f (r k) -> p f r k", r=2)
                S.op("dve", "tensor_tensor", out=p1v, in0=bv, in1=kr, op=ALU.mult, reads=[bk, Kf], writes=[p1])
                S.op("dve", "tensor_tensor", out=p2v, in0=bv, in1=ki, op=ALU.mult, reads=[bk, Kf], writes=[p2])
                S.op("pool", "tensor_tensor", out=Y[:, f0:f0 + n, 0:64], in0=p1[:, 0:n, 0:64], in1=p2[:, 0:n, 64:128],
                     op=ALU.subtract, reads=[p1, p2], writes=[buf2])
                S.op("pool", "tensor_tensor", out=Y[:, f0:f0 + n, 64:128], in0=p2[:, 0:n, 0:64], in1=p1[:, 0:n, 64:128],
                     op=ALU.add, reads=[p1, p2], writes=[buf2])

            forward(False, cons_u)
            for f0 in range(0, 65, 8):
                n = min(8, 65 - f0)
                bi = nb()
                for u in range(n):
                    S.op("pe", "transpose", out=bankb[bi][:, u * 128:(u + 1) * 128], in_=Y[:, f0 + u, :], identity=g.identb[:],
                         reads=[buf2], writes=[bank[bi]])
                evac(YT[:, f0:f0 + n, :], bankb[bi][:, 0:n * 128].rearrange("p (f c) -> p f c", c=128), [bank[bi]], [buf1])
            for f0 in range(0, 65, 4):
                n = min(4, 65 - f0)
                bi = nb()
                for u in range(n):
                    S.op("pe", "matmul", bank[bi][:, u * 128:(u + 1) * 128], lhsT=tabBp[:, f0 + u, :], rhs=YT[:, f0 + u, :],
                         start=True, stop=True, reads=[tabBp, buf1], writes=[bank[bi]])
                evac(Q[:, f0:f0 + n, :], bank[bi][:, 0:n * 128].rearrange("p (f c) -> p f c", c=128), [bank[bi]], [buf2])
            for c0 in range(0, 128, 8):
                bi = nb()
                for u in range(8):
                    S.op("pe", "transpose", out=bankb[bi][0:65, u * 128:(u + 1) * 128], in_=Q[:, :, c0 + u], identity=g.identb[:],
                         reads=[buf2], writes=[bank[bi]])
                evac(QT[:, c0:c0 + 8, :], bankb[bi][0:65, 0:1024].rearrange("p (c a) -> p c a", a=128), [bank[bi]], [buf3])
            x0v = x0[:, :].rearrange("p (b a) -> p a b", a=64)
            hyv = hyo.rearrange("p (b a) -> p a b", a=64)
            for a0 in range(0, 64, 8):
                bi = nb()
                for u in range(8):
                    a = a0 + u
                    S.op("pe", "matmul", bank[bi][:, u * 64:(u + 1) * 64], lhsT=QT[:, :, a], rhs=tabAp[:, 0:64],
                         start=True, stop=False, reads=[buf3, tabAp], writes=[bank[bi]])
                    S.op("pe", "matmul", bank[bi][:, u * 64:(u + 1) * 64], lhsT=QT[:, :, 64 + a], rhs=tabAp[:, 64:128],
                         start=False, stop=True, reads=[buf3, tabAp], writes=[bank[bi]])
                S.op("dve", "tensor_tensor", out=hyv[:, a0:a0 + 8, :], in0=bank[bi][:].rearrange("p (a b) -> p a b", b=64),
                     in1=x0v[:, a0:a0 + 8, :], op=ALU.mult, reads=[bank[bi], x0], writes=[sq2])
            S.dma("sp", d.hyT_d[:, ch, :], hyo, reads=[sq2], writes=[("hyT_d", ch)])
        S.phase_end()


def attention_phase(S, nc, g, d):
    with contextlib.ExitStack() as st:
        def T(name, shape, dt):
            return st.enter_context(nc.sbuf_tensor("at" + name, shape, dt))

        def P(name):
            return S.reg_psum(st.enter_context(nc.psum_tensor("at" + name, [128, 512], F32)))

        hT = T("hT", [128, 8, L], BF16)
        hcT = T("hcT", [128, 8, LC], BF16)
        for kc in range(8):
            S.dma("sp", hT[:, kc, :], d.hT_d[:, kc, :], writes=[hT])
        S.dma("sp", hcT[:], d.hcT_d, writes=[hcT])
        if HYENA:
            hyena_proj(S, nc, g, d, hT, st)
        ropeC = T("ropeC", [128, L], BF16)
        ropeS = T("ropeS", [128, L], BF16)
        S.dma("pool", ropeC[:], d.ropec_d, writes=[ropeC])
        S.dma("pool", ropeS[:], d.ropes_d, writes=[ropeS])
        Rm = T("Rm", [128, 128], BF16)
        S.dma("pool", Rm[:], d.rotm_d, writes=[Rm])
        lq = T("lq", [128, 4, 64], F32)
        for i, ap in enumerate([d.lq1_d, d.lk1_d, d.lq2_d, d.lk2_d]):
            S.dma("sp", lq[:, i, :], ap.broadcast_to([128, 64]), writes=[lq])
        lprod = T("lprod", [128, 2, 64], F32)
        S.op("dve", "tensor_tensor", out=lprod[:, 0, :], in0=lq[:, 0, :], in1=lq[:, 1, :], op=ALU.mult,
             reads=[lq], writes=[lprod])
        S.op("dve", "tensor_tensor", out=lprod[:, 1, :], in0=lq[:, 2, :], in1=lq[:, 3, :], op=ALU.mult,
             reads=[lq], writes=[lprod])
        lsum = T("lsum", [128, 2], F32)
        S.op("dve", "reduce_sum", out=lsum[:], in_=lprod[:], axis=AX.X, reads=[lprod], writes=[lsum])
        S.op("act", "activation", out=lsum[:], in_=lsum[:], func=AF.Exp, reads=[lsum], writes=[lsum])
        nlam = T("nlam", [128, 1], F32)
        S.op("dve", "scalar_tensor_tensor", out=nlam[:], in0=lsum[:, 1:2], scalar=-LAM_INIT, in1=lsum[:, 0:1],
             op0=ALU.add, op1=ALU.subtract, reads=[lsum], writes=[nlam])

        wq = [T("wq%d" % i, [128, 8, 128], BF16) for i in range(2)]
        wk = [T("wk%d" % i, [128, 8, 128], BF16) for i in range(2)]
        wv = [T("wv%d" % i, [128, 8, 256], BF16) for i in range(2)]
        qT = T("qT", [128, L], BF16)
        kT = T("kT", [128, LK], BF16)
        NKT = LK // 128
        vaug = T("vaug", [128, NKT, 2, 132], BF16)
        S.op("pool", "memset", vaug[:], 1.0, writes=[vaug])
        attnT = [T("attnT%d" % i, [128, L], BF16) for i in range(2)]
        qsb = [T("qsb%d" % i, [128, 512], BF16) for i in range(2)]
        t1 = [T("t1_%d" % i, [128, 512], F32) for i in range(2)]
        t2 = [T("t2_%d" % i, [128, 512], F32) for i in range(2)]
        pT2 = [[T("pT%d_%d" % (m, i), [128, 512], BF16) for i in range(3)] for m in range(2)]
        ppair = [T("ppair%d" % m, [128, 512], BF16) for m in range(2)]
        pprev = [None, None]
        acc = [[T("acc%d_%d" % (m, i), [128, 512], F32) for i in range(1)] for m in range(2)]
        accb = T("accb", [128, 512], BF16)
        rr = T("rr", [128, 512], F32)
        om = [T("om%d" % m, [128, 512], F32) for m in range(2)]
        of_ = T("of", [128, 512], F32)
        psS2 = [[P("psS%d_%d" % (m, i)) for i in range(2)] for m in range(2)]
        OT = [P("OT%d" % m) for m in range(2)]
        psA = P("psA")
        psB = P("psB")
        w_in = d.w_in_d

        def load_w(h):
            i = h % 2
            for (w, c0) in ((wq[i], 3072), (wk[i], 4096)):
                S.dma("pool", w[:], w_in[:, c0 + h * 128:c0 + (h + 1) * 128].rearrange("(kc p) n -> p kc n", p=128),
                      writes=[w])
            if h % 2 == 0:
                w = wv[(h // 2) % 2]
                S.dma("pool", w[:], w_in[:, 5120 + h * 128:5120 + (h + 2) * 128].rearrange("(kc p) n -> p kc n", p=128),
                      writes=[w])

        load_w(0)
        ei = 0
        for h in range(ATH if STAGE >= 3 else 0):
            if h + 1 < ATH:
                load_w(h + 1)
            i2 = h % 2
            for (w, dstT) in ((wq[i2], qT), (wk[i2], kT)):
                for blk in range(L // 512 if ASUB >= 2 else 0):
                    cs = slice(blk * 512, (blk + 1) * 512)
                    for kc in range(8):
                        S.op("pe", "matmul", psA[:], lhsT=w[:, kc, :], rhs=hT[:, kc, cs], start=(kc == 0), stop=(kc == 7),
                             reads=[w, hT], writes=[psA])
                    qs_, ta, tb = qsb[ei % 2], t1[ei % 2], t2[ei % 2]
                    ei += 1
                    S.op("act", "copy", out=qs_[:], in_=psA[:], reads=[psA], writes=[qs_])
                    S.op("pe", "matmul", psB[:], lhsT=Rm[:], rhs=qs_[:], start=True, stop=True, reads=[Rm, qs_],
                         writes=[psB])
                    S.op("dve", "tensor_tensor", out=ta[:], in0=psA[:], in1=ropeC[:, cs], op=ALU.mult,
                         reads=[psA, ropeC], writes=[ta])
                    S.op("dve", "tensor_tensor", out=tb[:], in0=psB[:], in1=ropeS[:, cs], op=ALU.mult,
                         reads=[psB, ropeS], writes=[tb])
                    S.op("pool", "tensor_tensor", out=dstT[:, cs], in0=ta[:], in1=tb[:], op=ALU.add,
                         reads=[ta, tb], writes=[dstT])
            if ASUB < 3:
                continue
            for kc in range(8):
                S.op("pe", "matmul", psA[:, 0:LC], lhsT=wk[i2][:, kc, :], rhs=hcT[:, kc, :], start=(kc == 0), stop=(kc == 7),
                     reads=[wk[i2], hcT], writes=[psA])
            S.op("act", "copy", out=kT[:, L:LK], in_=psA[:, 0:LC], reads=[psA], writes=[kT])
            if h % 2 == 0:
                wvp = wv[(h // 2) % 2]
                for tp in range(0, NKT, 2):
                    for u in range(2):
                        kt = tp + u
                        src = hT[:, :, kt * 128:(kt + 1) * 128] if kt < L // 128 else hcT[:, :, (kt - L // 128) * 128:(kt - L // 128 + 1) * 128]
                        for kc in range(8):
                            S.op("pe", "matmul", psA[:, u * 256:(u + 1) * 256], lhsT=src[:, kc, :], rhs=wvp[:, kc, :],
                                 start=(kc == 0), stop=(kc == 7), reads=[hT, hcT, wvp], writes=[psA])
                    S.op("act", "copy", out=vaug[:, tp:tp + 2, :, 0:128],
                         in_=psA[:, :].rearrange("p (u hh e) -> p u hh e", hh=2, e=128), reads=[psA], writes=[vaug])
            vh = h % 2
            if ASUB < 4:
                continue
            aT = attnT[i2]
            for qb in range(L // 512):
                qcs = slice(qb * 512, (qb + 1) * 512)

                def qk(kt):
                    for m in range(2):
                        ms = slice(m * 64, (m + 1) * 64)
                        ps = psS2[m][kt % 2]
                        S.op("pe", "matmul", ps[:], lhsT=kT[ms, kt * 128:(kt + 1) * 128], rhs=qT[ms, qcs],
                             start=True, stop=True, reads=[kT, qT], writes=[ps])
                qk(0)
                for kt in range(NKT):
                    if kt + 1 < NKT:
                        qk(kt + 1)
                    for m in range(2):
                        p_ = pT2[m][kt % 3]
                        ps = psS2[m][kt % 2]
                        S.op("act", "activation", out=p_[:], in_=ps[:], func=AF.Exp, scale=0.125, reads=[ps], writes=[p_])
                        S.op("pe", "matmul", OT[m][:], lhsT=vaug[:, kt, vh, 0:128], rhs=p_[:], start=(kt == 0), stop=(kt == NKT - 1),
                             reads=[vaug, p_], writes=[OT[m]])
                        if kt % 2 == 1:
                            pr = ppair[m]
                            S.op("dve", "tensor_tensor", out=pr[:], in0=pprev[m][:], in1=p_[:], op=ALU.add,
                                 reads=[pprev[m], p_], writes=[pr])
                        pprev[m] = p_
                    if kt % 2 == 1:
                        for m in range(2):
                            pr = ppair[m]
                            if kt == 1:
                                S.op("dve", "tensor_copy", out=acc[m][0][:], in_=pr[:], reads=[pr], writes=[acc[m][0]])
                            else:
                                S.op("dve", "tensor_tensor", out=acc[m][0][:], in0=acc[m][0][:], in1=pr[:], op=ALU.add,
                                     reads=[acc[m][0], pr], writes=[acc[m][0]])
                for m in range(2):
                    S.op("dve", "tensor_copy", out=accb[:], in_=acc[m][0][:], reads=[acc[m][0]], writes=[accb])
                    S.op("pe", "matmul", psA[:], lhsT=g.onesbb[:], rhs=accb[:], start=True, stop=True, reads=[accb], writes=[psA])
                    S.op("act", "activation", out=rr[:], in_=psA[:], func=AF.Ln, reads=[psA], writes=[rr])
                    S.op("act", "activation", out=rr[:], in_=rr[:], func=AF.Exp, scale=-1.0, reads=[rr], writes=[rr])
                    S.op("dve", "tensor_tensor", out=om[m][:], in0=OT[m][:], in1=rr[:], op=ALU.mult, reads=[OT[m], rr],
                         writes=[om[m]])
                S.op("dve", "scalar_tensor_tensor", out=of_[:], in0=om[1][:], scalar=nlam[:, 0:1], in1=om[0][:], op0=ALU.mult,
                     op1=ALU.add, reads=[om[0], om[1], nlam], writes=[of_])
                S.op("act", "activation", out=accb[:], in_=of_[:], func=AF.Square, reads=[of_], writes=[accb])
                S.op("pe", "matmul", psA[:], lhsT=g.onesbb[:], rhs=accb[:], start=True, stop=True, reads=[accb], writes=[psA])
                S.op("act", "activation", out=rr[:], in_=psA[:], func=AF.Ln, scale=1.0 / 128, bias=g.eps_t[:, 0:1],
                     reads=[psA], writes=[rr])
                S.op("act", "activation", out=rr[:], in_=rr[:], func=AF.Exp, scale=-0.5, reads=[rr], writes=[rr])
                S.op("dve", "tensor_tensor", out=aT[:, qcs], in0=of_[:], in1=rr[:], op=ALU.mult, reads=[of_, rr], writes=[aT])
            S.dma("sp", d.attnT_d[:, h, :], aT[:], reads=[aT], writes=[("attnT_d", h)])
        S.phase_end()


def mixer_out_phase(S, nc, g, d):
    with contextlib.ExitStack() as st:
        def T(name, shape, dt):
            return st.enter_context(nc.sbuf_tensor("mo" + name, shape, dt))

        def P(name):
            return S.reg_psum(st.enter_context(nc.psum_tensor("mo" + name, [128, 512], F32)))

        wg = T("wg", [128, 8, 2048], BF16)
        why = T("why", [128, 8, D], BF16)
        wda = T("wda", [128, 8, D], BF16)
        wo_ = T("wo", [128, 8, D], BF16)
        wv_ = d.w_in_d.rearrange("(kc p) n -> p kc n", p=128)
        for kc in range(0, 8, 2):
            S.dma("pool", wg[:, kc:kc + 2, :], wv_[:, kc:kc + 2, 6144:8192], writes=[wg])
        for (w, ap) in ((why, d.w_hy_out_d), (wda, d.w_da_out_d), (wo_, d.w_o_d)):
            S.dma("pool", w[:], ap.rearrange("(kc p) n -> p kc n", p=128), writes=[w])
        sg_ = T("sg", [128, 1], F32)
        S.dma("sp", sg_[:], d.subln_g_d.rearrange("o p -> p o"), writes=[sg_], allow_slow_non_contiguous=True)
        S.op("dve", "tensor_scalar", out=why[:], in0=why[:], scalar1=0.5, scalar2=None, op0=ALU.mult,
             reads=[why], writes=[why])
        S.op("dve", "tensor_scalar", out=wda[:], in0=wda[:], scalar1=sg_[:, 0:1], scalar2=0.5 * (1.0 - LAM_INIT),
             op0=ALU.mult, op1=ALU.mult, reads=[wda, sg_], writes=[wda])
        TB = 256
        hTb = [T("hTb%d" % i, [128, 8, TB], BF16) for i in range(2)]
        hyb = [T("hyb%d" % i, [128, 8, TB], BF16) for i in range(2)]
        atb = [T("atb%d" % i, [128, 8, TB], BF16) for i in range(2)]
        mT = T("mT", [128, 8, TB], BF16)
        th = [T("th%d" % i, [128, 2 * TB], F32) for i in range(2)]
        mm = [T("mm%d" % i, [128, 2 * TB], F32) for i in range(2)]
        xres = [T("xres%d" % i, [128, D], F32) for i in range(2)]
        ytmp = [T("yt%d" % i, [128, D], F32) for i in range(2)]
        ss2 = [T("ss2_%d" % i, [128, 2], F32) for i in range(2)]
        rs2 = [T("rs2_%d" % i, [128, 1], F32) for i in range(2)]
        psY = [P("psY%d" % i) for i in range(2)]
        psG = [P("psG%d" % i) for i in range(2)]
        pso = [S.reg_psum(st.enter_context(nc.psum_tensor("mopso%d" % i, [128, D], F32))) for i in range(2)]
        ci = 0
        for b in range(L // TB):
            cs = slice(b * TB, (b + 1) * TB)
            hb, ab, hT = hyb[b % 2], atb[b % 2], hTb[b % 2]
            S.dma("sp", hT[:], d.hT_d[:, :, cs], writes=[hT])
            S.dma("sp", hb[:], d.hyT_d[:, :, cs], writes=[hb])
            S.dma("sp", ab[:], d.attnT_d[:, :, cs], writes=[ab])
            for dc in range(8):
                py, pg = psY[ci % 2], psG[ci % 2]
                th_, mm_ = th[ci % 2], mm[ci % 2]
                ci += 1
                dcs = slice(dc * 128, (dc + 1) * 128)
                for c in range(8):
                    S.op("pe", "matmul", py[:, 0:TB], lhsT=why[:, c, dcs], rhs=hb[:, c, :], start=(c == 0), stop=(c == 7),
                         reads=[why, hb], writes=[py])
                for c in range(8):
                    S.op("pe", "matmul", py[:, TB:2 * TB], lhsT=wda[:, c, dcs], rhs=ab[:, c, :], start=(c == 0), stop=(c == 7),
                         reads=[wda, ab], writes=[py])
                for half in range(2):
                    for kc in range(8):
                        S.op("pe", "matmul", pg[:, half * TB:(half + 1) * TB],
                             lhsT=wg[:, kc, half * 1024 + dc * 128:half * 1024 + (dc + 1) * 128], rhs=hT[:, kc, :],
                             start=(kc == 0), stop=(kc == 7), reads=[wg, hT], writes=[pg])
                S.op("act", "activation", out=th_[:], in_=pg[:], func=AF.Tanh, scale=0.5, reads=[pg], writes=[th_])
                S.op("dve", "scalar_tensor_tensor", out=mm_[:], in0=th_[:], scalar=1.0, in1=py[:], op0=ALU.add, op1=ALU.mult,
                     reads=[th_, py], writes=[mm_])
                S.op("pool", "tensor_tensor", out=mT[:, dc, :], in0=mm_[:, 0:TB], in1=mm_[:, TB:2 * TB], op=ALU.add,
                     reads=[mm_], writes=[mT])
            for ti in range(TB // 128):
                r0 = b * TB + ti * 128
                po = pso[ti % 2]
                for half in range(2):
                    for dc in range(8):
                        S.op("pe", "matmul", po[:, half * 512:(half + 1) * 512], lhsT=mT[:, dc, ti * 128:(ti + 1) * 128],
                             rhs=wo_[:, dc, half * 512:(half + 1) * 512], start=(dc == 0), stop=(dc == 7),
                             reads=[mT, wo_], writes=[po])
                xr = xres[ti % 2]
                S.dma("sp", xr[:], d.x1_d[r0:r0 + 128, :], writes=[xr])
                post_norm_residual(S, g, "mo", po, g.Gx[1], xr, ytmp[ti % 2], ss2[ti % 2], rs2[ti % 2],
                                   d.x2_d[r0:r0 + 128, :], ("x2_d", r0))
        S.phase_end()


def build(debug=False):
    nc = bass.Bass("TRN2", target_bir_lowering=False)

    def din(name, shape):
        return nc.dram_tensor(name, shape, F32, kind="ExternalInput").ap()

    x_d = din("x", [L, D])
    c_d = din("c", [1, D])
    ctx_d = din("ctx", [LC, D])
    cctx_d = din("c_ctx", [1, D])
    w_ada_d = din("w_ada", [D, 9 * D])
    b_ada_d = din("b_ada", [1, 9 * D])
    norm_g_d = din("norm_g", [6, D])
    w_ff_in_d = din("w_ff_in", [2, D, 2 * DFF])
    w_ff_out_d = din("w_ff_out", [2, DFF, D])
    w_in_d = din("w_in", [D, NPROJ])
    ident_d = din("ident", [128, 128])
    d = Ctx()
    d.w_in_d = w_in_d
    d.lq1_d = din("lambda_q1", [1, 64]); d.lk1_d = din("lambda_k1", [1, 64])
    d.lq2_d = din("lambda_q2", [1, 64]); d.lk2_d = din("lambda_k2", [1, 64])
    d.subln_g_d = din("subln_g", [1, 128])
    d.w_hy_out_d = din("w_hy_out", [D, D]); d.w_da_out_d = din("w_da_out", [D, D]); d.w_o_d = din("w_o", [D, D])
    d.hy_conv_w_d = din("hy_conv_w", [3, 3 * D]); d.hy_conv_b_d = din("hy_conv_b", [1, 3 * D])
    d.filt_w1_d = din("filt_w1", [33, 64]); d.filt_b1_d = din("filt_b1", [1, 64])
    d.filt_w2_d = din("filt_w2", [64, 64]); d.filt_b2_d = din("filt_b2", [1, 64])
    d.filt_w3_d = din("filt_w3", [64, 64]); d.filt_b3_d = din("filt_b3", [1, 64])
    d.filt_w4_d = din("filt_w4", [64, 2 * D]); d.filt_freq_d = din("filt_freq", [1, 64])
    d.hy_bias_d = din("hy_bias", [1, D])
    d.tabA_d = din("tabA", [128, 130]); d.tabB_d = din("tabB", [64, 65, 256])
    d.tabBp_d = din("tabBp", [128, 65, 128]); d.tabAp_d = din("tabAp", [65, 128])
    d.zemb_d = din("zemb", [33, 2 * L]); d.deltas_d = din("deltas", [128, 8]); d.tlin1_d = din("tlin1", [2, 8]); d.tlin2_d = din("tlin2", [2, 512])
    d.ropec_d = din("ropec", [128, L]); d.ropes_d = din("ropes", [128, L]); d.rotm_d = din("rotm", [128, 128])
    out_d = nc.dram_tensor("out", [L, D], F32, kind="ExternalOutput").ap()
    sk = "ExternalOutput" if debug else "Internal"
    x1_d = nc.dram_tensor("x1_s", [L, D], F32, kind=sk).ap()
    c1_d = nc.dram_tensor("c1_s", [LC, D], F32, kind=sk).ap()
    hT_d = nc.dram_tensor("hT_s", [128, 8, L], BF16, kind=sk).ap()
    hcT_d = nc.dram_tensor("hcT_s", [128, 8, LC], BF16, kind=sk).ap()
    d.hT_d, d.hcT_d, d.x1_d = hT_d, hcT_d, x1_d
    d.attnT_d = nc.dram_tensor("attnT_s", [128, 8, L], BF16, kind=sk).ap()
    d.hyT_d = nc.dram_tensor("hyT_s", [128, 8, L], BF16, kind=sk).ap()
    d.x2_d = nc.dram_tensor("x2_s", [L, D], F32, kind=sk).ap()
    d.u_d = nc.dram_tensor("u_s", [128, 8, L], BF16, kind=sk).ap()
    d.x0_d = nc.dram_tensor("x0_s", [128, 8, L], BF16, kind=sk).ap()

    with nc.cleanup_on_exit(), contextlib.ExitStack() as gst:
        S = Sched(nc)
        g = Ctx()

        def GT(name, shape, dt):
            return gst.enter_context(nc.sbuf_tensor(name, shape, dt))

        g.identb = GT("identb", [128, 128], BF16)
        g.eps_t = GT("eps_t", [128, 1], F32)
        g.onesb = GT("onesb", [1, 128], BF16)
        g.onesbb = GT("onesbb", [128, 128], BF16)
        g.junk = GT("junk", [128, D], BF16)
        g.Ax = [GT("Ax%d" % i, [128, 8], F32) for i in range(3)]
        g.Bx = [GT("Bx%d" % i, [128, 8], F32) for i in range(3)]
        g.Ac = [GT("Ac%d" % i, [128, 8], F32) for i in range(2)]
        g.Bc = [GT("Bc%d" % i, [128, 8], F32) for i in range(2)]
        g.Gx = [GT("Gx%d" % i, [128, D], F32) for i in range(3)]
        g.Gc = GT("Gc0", [128, D], F32)

        st_f1w = contextlib.ExitStack()
        f1w = ffn_alloc_wi(S, nc, st_f1w, "f1", w_ff_in_d[0], bg=True)

        with contextlib.ExitStack() as st:
            def T(name, shape, dt):
                return st.enter_context(nc.sbuf_tensor("p0" + name, shape, dt))

            identf = T("identf", [128, 128], F32)
            S.dma("sp", identf[:], ident_d, writes=[identf])
            S.op("dve", "tensor_copy", out=g.identb[:], in_=identf[:], reads=[identf], writes=[g.identb])
            S.op("dve", "memset", g.eps_t[:], EPS, writes=[g.eps_t])
            S.op("dve", "memset", g.onesb[:], 1.0, writes=[g.onesb])
            S.op("dve", "memset", g.onesbb[:], 1.0, writes=[g.onesbb])
            craw = T("craw", [128, 2, 8], F32)
            S.dma("sp", craw[:, 0, :], c_d.rearrange("o (kc p) -> p (o kc)", p=128), writes=[craw],
                  allow_slow_non_contiguous=True)
            S.dma("sp", craw[:, 1, :], cctx_d.rearrange("o (kc p) -> p (o kc)", p=128), writes=[craw],
                  allow_slow_non_contiguous=True)
            csil = T("csil", [128, 2, 8], F32)
            S.op("act", "activation", out=csil[:], in_=craw[:], func=AF.Silu, reads=[craw], writes=[csil])
            sT = T("sT", [128, 8, 2], BF16)
            for j in range(2):
                S.op("dve", "tensor_copy", out=sT[:, :, j], in_=csil[:, j, :], reads=[csil], writes=[sT])
            sbc = T("sbc", [128, 2, 8, 128], BF16)
            for j in range(2):
                for kc in range(8):
                    S.op("dve", "tensor_copy", out=sbc[:, j, kc, :], in_=csil[:, j, kc:kc + 1].broadcast_to([128, 128]),
                         reads=[csil], writes=[sbc])
            badaf = [T("badaf%d" % i, [1, D], F32) for i in range(2)]
            badab = [T("badab%d" % i, [1, D], BF16) for i in range(2)]
            gpre = T("gpre", [128, 3, 8], F32)
            for i in range(3):
                S.dma("sp", gpre[:, i, :], norm_g_d[2 * i:2 * i + 1, :].rearrange("o (kc p) -> p (o kc)", p=128),
                      writes=[gpre], allow_slow_non_contiguous=True)
            gpost = [T("gpost%d" % i, [128, D], F32) for i in range(3)]
            for i in range(3):
                S.dma("sp", gpost[i][:], norm_g_d[2 * i + 1:2 * i + 2, :].broadcast_to([128, D]), writes=[gpost[i]])
            wa = [T("wa%d" % i, [128, 8, D], BF16) for i in range(2)]
            waf = [T("waf%d" % i, [128, 2, D], F32) for i in range(3)]
            wfi = [0]
            modp = S.reg_psum(st.enter_context(nc.psum_tensor("p0modp", [128, 8, 2], F32)))
            psG = [S.reg_psum(st.enter_context(nc.psum_tensor("p0psG%d" % i, [128, 512], F32))) for i in range(2)]
            wav = w_ada_d.rearrange("(kc p) n -> p kc n", p=128)
            gi = 0
            for m in range(9):
                w = wa[m % 2]
                bada = badab[m % 2]
                S.dma("sp", badaf[m % 2][:], b_ada_d[0:1, m * D:(m + 1) * D], writes=[badaf[m % 2]])
                S.op("dve", "tensor_copy", out=bada[:], in_=badaf[m % 2][:], reads=[badaf[m % 2]], writes=[bada])
                for kc in range(0, 8, 2):
                    wf = waf[wfi[0] % 3]
                    S.dma("sp", wf[:], wav[:, kc:kc + 2, m * D:(m + 1) * D], writes=[wf])
                    if wfi[0] % 2 == 0:
                        S.op("dve", "tensor_copy", out=w[:, kc:kc + 2, :], in_=wf[:], reads=[wf], writes=[(w.name, kc)])
                    else:
                        S.op("act", "copy", out=w[:, kc:kc + 2, :], in_=wf[:], reads=[wf], writes=[(w.name, kc)])
                    wfi[0] += 1
                grp, which = m // 3, m % 3
                if which < 2:
                    for j in range(8):
                        for kc in range(8):
                            S.op("pe", "matmul", modp[:, j, :], lhsT=w[:, kc, j * 128:(j + 1) * 128], rhs=sT[:, kc, :],
                                 start=(kc == 0), stop=False, reads=[(w.name, (kc // 2) * 2), sT], writes=[modp])
                        S.op("pe", "matmul", modp[:, j, :], lhsT=bada[0:1, j * 128:(j + 1) * 128],
                             rhs=g.onesb[0:1, 0:2], start=False, stop=True, reads=[bada, g.onesb], writes=[modp])
                    if which == 0:
                        S.op("dve", "tensor_copy", out=g.Bx[grp][:], in_=modp[:, :, 0], reads=[modp], writes=[g.Bx[grp]])
                        if grp < 2:
                            S.op("dve", "tensor_copy", out=g.Bc[grp][:], in_=modp[:, :, 1], reads=[modp],
                                 writes=[g.Bc[grp]])
                    else:
                        S.op("dve", "scalar_tensor_tensor", out=g.Ax[grp][:], in0=modp[:, :, 0], scalar=1.0,
                             in1=gpre[:, grp, :], op0=ALU.add, op1=ALU.mult, reads=[modp, gpre], writes=[g.Ax[grp]])
                        if grp < 2:
                            S.op("dve", "scalar_tensor_tensor", out=g.Ac[grp][:], in0=modp[:, :, 1], scalar=1.0,
                                 in1=gpre[:, grp, :], op0=ALU.add, op1=ALU.mult, reads=[modp, gpre],
                                 writes=[g.Ac[grp]])
                else:
                    fac = 1.0 if grp == 1 else 0.5
                    targets = [(0, g.Gx[grp])] + ([(1, g.Gc)] if grp == 0 else [])
                    for (j, Gdst) in targets:
                        for half in range(2):
                            pg = psG[gi % 2]
                            gi += 1
                            for kc in range(8):
                                S.op("pe", "matmul", pg[:], lhsT=sbc[:, j, kc, :], rhs=w[:, kc, half * 512:(half + 1) * 512],
                                     start=(kc == 0), stop=False, reads=[(w.name, (kc // 2) * 2), sbc], writes=[pg])
                            S.op("pe", "matmul", pg[:], lhsT=g.onesb[0:1, :],
                                 rhs=bada[0:1, half * 512:(half + 1) * 512],
                                 start=False, stop=True, reads=[bada, g.onesb], writes=[pg])
                            S.op("dve", "scalar_tensor_tensor", out=Gdst[:, half * 512:(half + 1) * 512], in0=pg[:],
                                 scalar=fac, in1=gpost[grp][:, half * 512:(half + 1) * 512], op0=ALU.mult, op1=ALU.mult,
                                 reads=[pg, gpost[grp]], writes=[Gdst])
            S.phase_end(skip_bg=True)

        if debug:
            dbg_d = nc.dram_tensor("dbg", [128, 10, 8], F32, kind="ExternalOutput").ap()
            dbgG_d = nc.dram_tensor("dbgG", [4, 128, D], F32, kind="ExternalOutput").ap()
            for i, t in enumerate(g.Ax + g.Bx + g.Ac + g.Bc):
                S.dma("sp", dbg_d[:, i, :], t[:], reads=[t], writes=[("dbg", i)])
            for i, t in enumerate(g.Gx + [g.Gc]):
                S.dma("sp", dbgG_d[i], t[:], reads=[t], writes=[("dbgG", i)])
        if STAGE >= 1:
          ffn_phase(S, nc, g, "f1", w_ff_in_d[0], w_ff_out_d[0], preloaded=f1w, streams=[
            dict(T=LC, src=ctx_d, dst=c1_d, A=g.Ac[0], B=g.Bc[0], G=g.Gc,
                 nxt=dict(A=g.Ac[1], B=g.Bc[1], dst=hcT_d)),
        ] + ([dict(T=L, src=x_d, dst=x1_d, A=g.Ax[0], B=g.Bx[0], G=g.Gx[0],
                 nxt=dict(A=g.Ax[1], B=g.Bx[1], dst=hT_d))] if STAGE >= 2 else []))

        st_f1w.close()

        if STAGE >= 3:
            if not HYENA:
                with contextlib.ExitStack() as st:
                    z = st.enter_context(nc.sbuf_tensor("zt", [128, 8, 512], BF16))
                    S.op("dve", "memset", z[:], 0.0, writes=[z])
                    for i in range(L // 512):
                        S.dma("sp", d.hyT_d[:, :, i * 512:(i + 1) * 512], z[:], reads=[z], writes=[("hyz", i)])
                    S.phase_end()
            attention_phase(S, nc, g, d)
            if HYENA:
                hyena_fft_phase(S, nc, g, d)
        if STAGE >= 4:
            mixer_out_phase(S, nc, g, d)
        if STAGE >= 5:
            ffn_phase(S, nc, g, "f2", w_ff_in_d[1], w_ff_out_d[1], [
                dict(T=L, src=d.x2_d, dst=out_d, A=g.Ax[2], B=g.Bx[2], G=g.Gx[2], nxt=None)])
        nc.all_engine_barrier()
        print("instructions", S.n_ins, "waits", S.n_wait)
    return nc


_NC_CACHE = {}


def _host_consts():
    p = np.arange(128)
    dd = p % 64
    axis, half, fr = dd // 32, (dd % 32) // 16, dd % 16
    rotm = np.zeros((128, 128), np.float32)
    for po in range(128):
        if half[po] == 0:
            rotm[po + 16, po] = -1.0
        else:
            rotm[po - 16, po] = 1.0
    t = np.arange(L)
    pos = np.stack([(t // 64).astype(np.float32), (t % 64).astype(np.float32)], 0)
    inv = (np.float32(10000.0) ** (-np.arange(0, 32, 2, dtype=np.float32) / np.float32(32))).astype(np.float32)
    ang = pos[axis, :] * inv[fr][:, None]
    out = {"ident": np.eye(128, dtype=np.float32), "rotm": rotm,
           "ropec": np.cos(ang).astype(np.float32), "ropes": np.sin(ang).astype(np.float32)}
    N = 2 * L
    bp = np.arange(128)[:, None]; f2 = np.arange(65)[None, :]
    phi = 2 * np.pi * ((bp * f2) % 128) / 128
    out["tabA"] = np.concatenate([np.cos(phi), -np.sin(phi)], 1).astype(np.float32)
    ap = np.arange(64)[:, None, None]; f2_ = np.arange(65)[None, :, None]; f1 = np.arange(64)[None, None, :]
    th = 2 * np.pi * ((ap * (128 * f1 + f2_)) % N) / N
    out["tabB"] = np.concatenate([np.cos(th), -np.sin(th), np.sin(th), np.cos(th)], 2).astype(np.float32)
    thT = np.transpose(th, (2, 1, 0))
    top = np.concatenate([np.cos(thT), np.sin(thT)], 2)
    bot = np.concatenate([-np.sin(thT), np.cos(thT)], 2)
    out["tabBp"] = np.concatenate([top, bot], 0).astype(np.float32)
    w = np.full(65, 2.0); w[0] = 1; w[64] = 1
    phiT = 2 * np.pi * np.arange(65)[:, None] * np.arange(64)[None, :] / 128
    out["tabAp"] = (np.concatenate([w[:, None] * np.cos(phiT), -w[:, None] * np.sin(phiT)], 1) / N).astype(np.float32)
    tt = np.linspace(0.0, 1.0, L, dtype=np.float32)[None, :]
    ww = (2.0 * np.pi * np.arange(L, dtype=np.float32) / L)[None, :]
    ff = np.linspace(1e-4, 15, 16, dtype=np.float32)[:, None]
    zemb = np.concatenate([tt, np.cos(ff * ww), -np.sin(ff * ww)], 0).astype(np.float32)
    ridx = (L - np.arange(L)) % L
    out["zemb"] = np.ascontiguousarray(np.concatenate([zemb, zemb[:, ridx]], 1))
    deltas = np.abs(np.linspace(math.log(1e-2) / 1.5, math.log(1e-2) / 0.3, D, dtype=np.float32))
    out["deltas"] = np.ascontiguousarray(-deltas.reshape(8, 128).T).astype(np.float32)
    a64 = np.arange(64, dtype=np.float64)
    b8 = np.arange(8, dtype=np.float64); r512 = np.arange(512, dtype=np.float64)
    out["tlin1"] = np.stack([512 * b8 / (L - 1), (L - 512 * b8) / (L - 1)], 0).astype(np.float32)
    out["tlin2"] = np.stack([r512 / (L - 1), -r512 / (L - 1)], 0).astype(np.float32)
    return out


def make_in_maps(inputs, cores):
    consts = _host_consts()
    maps = []
    f = lambda a: np.ascontiguousarray(np.asarray(a, dtype=np.float32))
    for b in cores:
        m = {
            "x": f(inputs["x"][b]), "c": f(inputs["c"][b:b + 1]), "ctx": f(inputs["ctx"][b]),
            "c_ctx": f(np.asarray(inputs["c_ctx"]).reshape(1, D)),
            "w_ada": f(inputs["w_ada"][0]), "b_ada": f(inputs["b_ada"][0:1]), "norm_g": f(inputs["norm_g"][0]),
            "w_ff_in": f(inputs["w_ff_in"][0]), "w_ff_out": f(inputs["w_ff_out"][0]), "w_in": f(inputs["w_in"][0]),
            "lambda_q1": f(inputs["lambda_q1"][0:1]), "lambda_k1": f(inputs["lambda_k1"][0:1]),
            "lambda_q2": f(inputs["lambda_q2"][0:1]), "lambda_k2": f(inputs["lambda_k2"][0:1]),
            "subln_g": f(inputs["subln_g"][0:1]), "w_hy_out": f(inputs["w_hy_out"][0]),
            "hy_conv_w": f(inputs["hy_conv_w"][0]), "hy_conv_b": f(inputs["hy_conv_b"][0:1]),
            "filt_w1": f(inputs["filt_w1"][0]), "filt_b1": f(inputs["filt_b1"][0:1]),
            "filt_w2": f(inputs["filt_w2"][0]), "filt_b2": f(inputs["filt_b2"][0:1]),
            "filt_w3": f(inputs["filt_w3"][0]), "filt_b3": f(inputs["filt_b3"][0:1]),
            "filt_w4": f(inputs["filt_w4"][0]), "filt_freq": f(inputs["filt_freq"][0:1]),
            "hy_bias": f(inputs["hy_bias"][0:1]),
            "w_da_out": f(inputs["w_da_out"][0]), "w_o": f(inputs["w_o"][0]),
        }
        m.update(consts)
        maps.append(m)
    return maps


def kernel(**inputs):
    if "nc" not in _NC_CACHE:
        _NC_CACHE["nc"] = build()
    nc = _NC_CACHE["nc"]
    in_maps = make_in_maps(inputs, list(range(8)))
    res = run_bass_kernel_spmd(nc, in_maps, core_ids=list(range(8)))
    return np.stack([np.asarray(r["out"], dtype=np.float32) for r in res.results], axis=0)
```
